# Optimizing a Trainium2 kernel written in Bass

```python
import math
import jax, jax.numpy as jnp
from jax import lax
import numpy as np

D_MODEL = 1024
BATCH = 4
SEQ = 8192
DEPTH = 2

N_EVEN = (DEPTH + 1) // 2
N_ODD = DEPTH // 2

POOL_WIDTH = D_MODEL // 2
POOL_WINDOWS = (2, 4, 8, 16)
N_POOL_GROUPS = len(POOL_WINDOWS)
POOL_GROUP = POOL_WIDTH // N_POOL_GROUPS

ATTN_WIDTH = D_MODEL - POOL_WIDTH
DA_HEAD_V = 128
DA_HEADS = ATTN_WIDTH // DA_HEAD_V
DA_HEAD_QK = DA_HEAD_V // 2
ROT_DIM = DA_HEAD_QK // 4
ROPE_THETA = 500000.0
Q_BLOCK = 128

IN_WIDTH = POOL_WIDTH + 3 * ATTN_WIDTH

CONV_WIDTH = D_MODEL
CONV_KERNEL = 31

D_FF = 4 * D_MODEL

RMS_EPS = 1e-6
LN_EPS = 1e-5
SUBLN_EPS = 1e-5

kernel_name = "hybrid_pool_diffattn_conformer_trunk"


def rmsnorm(x, g, eps=RMS_EPS):
    xf = x.astype(jnp.float32)
    y = xf * lax.rsqrt(jnp.mean(xf * xf, axis=-1, keepdims=True) + eps)
    return (y * g.astype(jnp.float32)).astype(x.dtype)


def layernorm(x, g, b, eps=LN_EPS):
    xf = x.astype(jnp.float32)
    mu = jnp.mean(xf, axis=-1, keepdims=True)
    var = jnp.mean(jnp.square(xf - mu), axis=-1, keepdims=True)
    y = (xf - mu) * lax.rsqrt(var + eps)
    return (y * g.astype(jnp.float32) + b.astype(jnp.float32)).astype(x.dtype)


def rope_tables(S):
    pos = jnp.arange(S, dtype=jnp.float32)
    inv_freq = ROPE_THETA ** (-jnp.arange(0, ROT_DIM, 2, dtype=jnp.float32) / ROT_DIM)
    ang = pos[:, None] * inv_freq[None, :]
    return jnp.cos(ang), jnp.sin(ang)


def rope_partial(x, cos, sin):
    half = ROT_DIM // 2
    xr = x[..., :ROT_DIM].astype(jnp.float32)
    x1, x2 = xr[..., :half], xr[..., half:]
    c = cos[None, :, None, None, :]
    s = sin[None, :, None, None, :]
    rot = jnp.concatenate([x1 * c - x2 * s, x2 * c + x1 * s], axis=-1).astype(x.dtype)
    return jnp.concatenate([rot, x[..., ROT_DIM:]], axis=-1)


def multiscale_pool(u):
    B, S, _ = u.shape
    uf = u.astype(jnp.float32).reshape(B, S, N_POOL_GROUPS, POOL_GROUP)
    cs = jnp.cumsum(uf, axis=1)
    t = jnp.arange(S)
    outs = []
    for g, w in enumerate(POOL_WINDOWS):
        c = cs[:, :, g]
        prev = jnp.pad(c, ((0, 0), (w, 0), (0, 0)))[:, :S]
        cnt = jnp.minimum(t + 1, w).astype(jnp.float32)[None, :, None]
        outs.append((c - prev) / cnt - uf[:, :, g])
    return jnp.stack(outs, axis=2).astype(u.dtype)


def diff_attention(q, k, v, lam):
    B, S = q.shape[:2]
    nb = S // Q_BLOCK
    scale = DA_HEAD_QK ** -0.5
    qb = q.reshape(B, nb, Q_BLOCK, DA_HEADS, 2, DA_HEAD_QK).transpose(1, 0, 2, 3, 4, 5)
    k_pos = jnp.arange(S)

    def one_block(args):
        i, qi = args
        s = jnp.einsum('bqhcd,bkhcd->bhcqk', qi, k,
                       preferred_element_type=jnp.float32) * scale
        q_pos = i * Q_BLOCK + jnp.arange(Q_BLOCK)
        mask = k_pos[None, :] <= q_pos[:, None]
        s = jnp.where(mask, s, -jnp.inf)
        p = jax.nn.softmax(s, axis=-1)
        w = p[:, :, 0] - lam * p[:, :, 1]
        return jnp.einsum('bhqk,bkhe->bqhe', w.astype(v.dtype), v)

    out = lax.map(one_block, (jnp.arange(nb), qb))
    return out.transpose(1, 0, 2, 3, 4).reshape(B, S, DA_HEADS, DA_HEAD_V)


def pool_diff_mixer(h, layer_idx, cos, sin, w_in, pool_w, pool_scale,
                    lam_q1, lam_k1, lam_q2, lam_k2, subln, w_out):
    B, S, _ = h.shape
    z = h @ w_in
    u = z[..., :POOL_WIDTH]
    q = z[..., POOL_WIDTH:POOL_WIDTH + ATTN_WIDTH].reshape(B, S, DA_HEADS, 2, DA_HEAD_QK)
    k = z[..., POOL_WIDTH + ATTN_WIDTH:POOL_WIDTH + 2 * ATTN_WIDTH].reshape(
        B, S, DA_HEADS, 2, DA_HEAD_QK)
    v = z[..., POOL_WIDTH + 2 * ATTN_WIDTH:].reshape(B, S, DA_HEADS, DA_HEAD_V)

    pooled = multiscale_pool(u)
    a_out = jnp.einsum('bsgc,gcd->bsgd', pooled, pool_w).reshape(B, S, POOL_WIDTH)
    a_out = a_out * pool_scale

    q = rope_partial(q, cos, sin)
    k = rope_partial(k, cos, sin)
    lambda_init = 0.8 - 0.6 * math.exp(-0.3 * layer_idx)
    lam = (jnp.exp(jnp.sum(lam_q1.astype(jnp.float32) * lam_k1.astype(jnp.float32)))
           - jnp.exp(jnp.sum(lam_q2.astype(jnp.float32) * lam_k2.astype(jnp.float32)))
           + lambda_init)
    o = diff_attention(q, k, v, lam)
    o = rmsnorm(o, subln, SUBLN_EPS) * (1.0 - lambda_init)
    b_out = o.reshape(B, S, ATTN_WIDTH)

    return jnp.concatenate([a_out, b_out.astype(a_out.dtype)], axis=-1) @ w_out


def conformer_conv(h, pw1_w, pw1_b, dw_w, dw_b, ln_g, ln_b, pw2_w, pw2_b):
    a = h @ pw1_w + pw1_b
    g = a[..., :CONV_WIDTH] * jax.nn.sigmoid(a[..., CONV_WIDTH:])
    y = lax.conv_general_dilated(
        g, dw_w[:, None, :].astype(g.dtype), window_strides=(1,),
        padding=((CONV_KERNEL - 1, 0),),
        dimension_numbers=('NWC', 'WIO', 'NWC'),
        feature_group_count=CONV_WIDTH) + dw_b
    y = jax.nn.silu(layernorm(y, ln_g, ln_b))
    return y @ pw2_w + pw2_b


def sq_relu_mlp(h, w_up, w_down):
    return jnp.square(jax.nn.relu(h @ w_up)) @ w_down


def setup_inputs(seed: int = 0) -> dict:
    key = jax.random.key(seed)
    ks = iter(jax.random.split(key, 32))

    def nrm(shape, scale):
        return jax.random.normal(next(ks), shape, jnp.float32) * scale

    def gain(shape):
        return 1.0 + nrm(shape, 0.05)

    return {
        "x": nrm((BATCH, SEQ, D_MODEL), 1.0),
        "mix_norm": gain((DEPTH, D_MODEL)),
        "mlp_norm": gain((DEPTH, D_MODEL)),
        "w_up": nrm((DEPTH, D_MODEL, D_FF), D_MODEL ** -0.5),
        "w_down": nrm((DEPTH, D_FF, D_MODEL), D_FF ** -0.5),
        "final_norm": gain((D_MODEL,)),
        "w_in": nrm((N_EVEN, D_MODEL, IN_WIDTH), D_MODEL ** -0.5),
        "pool_w": nrm((N_EVEN, N_POOL_GROUPS, POOL_GROUP, POOL_GROUP), POOL_GROUP ** -0.5),
        "pool_scale": gain((N_EVEN, POOL_WIDTH)),
        "lam_q1": nrm((N_EVEN, DA_HEAD_QK), 0.1),
        "lam_k1": nrm((N_EVEN, DA_HEAD_QK), 0.1),
        "lam_q2": nrm((N_EVEN, DA_HEAD_QK), 0.1),
        "lam_k2": nrm((N_EVEN, DA_HEAD_QK), 0.1),
        "subln": gain((N_EVEN, DA_HEAD_V)),
        "w_out": nrm((N_EVEN, D_MODEL, D_MODEL), D_MODEL ** -0.5),
        "conv_pw1_w": nrm((N_ODD, D_MODEL, 2 * CONV_WIDTH), D_MODEL ** -0.5),
        "conv_pw1_b": nrm((N_ODD, 2 * CONV_WIDTH), 0.02),
        "conv_dw_w": nrm((N_ODD, CONV_KERNEL, CONV_WIDTH), CONV_KERNEL ** -0.5),
        "conv_dw_b": nrm((N_ODD, CONV_WIDTH), 0.02),
        "conv_ln_g": gain((N_ODD, CONV_WIDTH)),
        "conv_ln_b": nrm((N_ODD, CONV_WIDTH), 0.02),
        "conv_pw2_w": nrm((N_ODD, CONV_WIDTH, D_MODEL), CONV_WIDTH ** -0.5),
        "conv_pw2_b": nrm((N_ODD, D_MODEL), 0.02),
    }


def reference(x, mix_norm, mlp_norm, w_up, w_down, final_norm,
              w_in, pool_w, pool_scale, lam_q1, lam_k1, lam_q2, lam_k2, subln, w_out,
              conv_pw1_w, conv_pw1_b, conv_dw_w, conv_dw_b, conv_ln_g, conv_ln_b,
              conv_pw2_w, conv_pw2_b):
    S = x.shape[1]
    cos, sin = rope_tables(S)
    h = x
    for l in range(DEPTH):
        j = l // 2
        hn = rmsnorm(h, mix_norm[l])
        if l % 2 == 0:
            mix = pool_diff_mixer(hn, l, cos, sin, w_in[j], pool_w[j], pool_scale[j],
                                  lam_q1[j], lam_k1[j], lam_q2[j], lam_k2[j],
                                  subln[j], w_out[j])
        else:
            mix = conformer_conv(hn, conv_pw1_w[j], conv_pw1_b[j], conv_dw_w[j],
                                 conv_dw_b[j], conv_ln_g[j], conv_ln_b[j],
                                 conv_pw2_w[j], conv_pw2_b[j])
        h = h + mix.astype(h.dtype)
        h = h + sq_relu_mlp(rmsnorm(h, mlp_norm[l]), w_up[l], w_down[l]).astype(h.dtype)
    return rmsnorm(h, final_norm)
```

```python
import math
from contextlib import ExitStack

import numpy as np
import concourse.bass as bass
import concourse.mybir as mybir
from concourse.bass_utils import run_bass_kernel_spmd

F32 = mybir.dt.float32
BF16 = mybir.dt.bfloat16
AF = mybir.ActivationFunctionType
ALU = mybir.AluOpType

D = 1024
SEQ = 8192
NCORES = 8
HALF = 4096
HALO = 128
NLOC = HALF + HALO
NPRE = HALF - HALO
NKEY = NPRE + NLOC
NPREB = NPRE // 128
DFF = 4096
RMS_EPS = 1e-6
LN_EPS = 1e-5
SUBLN_EPS = 1e-5
POOL_WINDOWS = (2, 4, 8, 16)
CONV_K = 31
NEG = -30000.0

VC_GAIN = 0
VC_PSCALE = 40
VC_PW1B = 44
VC_DWB = 60
VC_LNG = 68
VC_LNB = 76
VC_PW2B = 84
VC_SUBLN = 92
VC_DWW = 93
VC_N = VC_DWW + CONV_K * 8

LOC_TILES = [(0, HALO)] + [(HALO + 512 * i, 512) for i in range(8)]
PRE_TILES = [(512 * i, 512) for i in range(7)] + [(3584, 384)]


class Res:
    __slots__ = ("name", "last_w", "readers", "dma_sem", "dma_cnt")

    def __init__(self, name):
        self.name = name
        self.last_w = None
        self.readers = []
        self.dma_sem = None
        self.dma_cnt = 0


class Op:
    __slots__ = ("eng", "fn", "deps", "is_dma", "sem", "val", "observed")

    def __init__(self, eng, fn, is_dma):
        self.eng = eng
        self.fn = fn
        self.deps = []
        self.is_dma = is_dma
        self.sem = None
        self.val = 0
        self.observed = False


ENGS = ("pe", "act", "dve", "pool", "sp")


class WV:
    def __init__(self, t, width):
        self.t = t
        self.w = width

    def __getitem__(self, idx):
        _, c, sl = idx
        return self.t[:, c * self.w + sl.start:c * self.w + sl.stop]

    def chunk(self, c):
        return self.t[:, c * self.w:(c + 1) * self.w]


class Prog:
    def __init__(self, nc, stack):
        self.nc = nc
        self.stack = stack
        self.ops = {e: [] for e in ENGS}
        self.all_ops = []
        self.res = {}
        self.eng_sem = {e: stack.enter_context(nc.semaphore("sem_" + e)) for e in ("pe", "act", "dve", "pool")}
        self.n_dma_sems = 0

    def R(self, name):
        r = self.res.get(name)
        if r is None:
            r = self.res[name] = Res(name)
        return r

    def _rl(self, xs):
        out = []
        for x in xs:
            out.append(self.R(x) if isinstance(x, str) else x)
        return out

    def add(self, eng, fn, reads=(), writes=(), dma=None):
        op = Op(eng, fn, dma is not None)
        reads = self._rl(reads)
        writes = self._rl(writes)
        psr = [r for r in reads if r.name.startswith("ps") and r.name[2:].isdigit()]
        if psr:
            reads = [r for r in reads if r not in psr]
            writes = writes + [r for r in psr if r not in writes]
        deps = []
        for r in reads:
            if r.last_w is not None:
                deps.append(r.last_w)
        for w in writes:
            if w.last_w is not None:
                deps.append(w.last_w)
            deps.extend(w.readers)
        seen = set()
        for d in deps:
            if d is op or id(d) in seen:
                continue
            seen.add(id(d))
            if eng == "pe" and d.eng == "pe" and not d.is_dma:
                continue
            op.deps.append(d)
            d.observed = True
        for r in reads:
            r.readers.append(op)
        for w in writes:
            w.last_w = op
            w.readers = []
        if dma is not None:
            sr = self.R(dma)
            if sr.dma_sem is None:
                sr.dma_sem = self.stack.enter_context(self.nc.semaphore("dsem%d" % self.n_dma_sems))
                self.n_dma_sems += 1
            sr.dma_cnt += 16
            op.sem = sr.dma_sem
            op.val = sr.dma_cnt
        self.ops[eng].append(op)
        self.all_ops.append(op)
        return op

    def barrier(self):
        lasts = []
        for e in ("pe", "act", "dve", "pool"):
            if self.ops[e]:
                lasts.append(self.ops[e][-1])
        dmas = {}
        for op in self.all_ops:
            if op.is_dma:
                dmas[id(op.sem)] = op
        lasts.extend(dmas.values())
        bops = []
        for e in ENGS:
            op = Op(e, None, False)
            for d in lasts:
                op.deps.append(d)
                d.observed = True
            self.ops[e].append(op)
            self.all_ops.append(op)
            bops.append(op)
        for r in self.res.values():
            r.last_w = None
            r.readers = []
        return bops

    def emit(self):
        nc = self.nc
        for e in ("pe", "act", "dve", "pool"):
            cnt = 0
            for op in self.ops[e]:
                if op.is_dma or op.fn is None:
                    continue
                if op.observed:
                    cnt += 1
                    op.sem = self.eng_sem[e]
                    op.val = cnt
        handles = {"pe": "tensor", "act": "scalar", "dve": "vector", "pool": "gpsimd", "sp": "sync"}
        with nc.Block() as block:
            for e in ENGS:
                ops = self.ops[e]

                def body(eng, ops=ops):
                    waited = {}
                    for op in ops:
                        need = {}
                        for d in op.deps:
                            if d.sem is None:
                                continue
                            k = id(d.sem)
                            if k not in need or need[k][1] < d.val:
                                need[k] = (d.sem, d.val)
                        for k, (sem, val) in need.items():
                            if waited.get(k, 0) < val:
                                eng.wait_ge(sem, val)
                                waited[k] = val
                        if op.fn is None:
                            continue
                        ins = op.fn(eng)
                        if op.is_dma:
                            ins.then_inc(op.sem, 16)
                        elif op.observed:
                            ins.then_inc(op.sem, 1)

                getattr(block, handles[e])(body)


ALL_PHASES = ("p1", "p2", "p3", "m0a", "m0b", "p4a", "p4b", "m1a", "m1b")

SCRATCH = {
    "KT": ([4, 128, NKEY], BF16),
    "V4": ([4, 128, NKEY // 128, 128], BF16),
    "QT": ([4, 128, NLOC], BF16),
    "catT": ([D, NLOC], BF16),
    "catB": ([D // 2, NLOC], BF16),
    "hA": ([D, NLOC], F32),
    "hB": ([D, NLOC], F32),
    "hn": ([D, NLOC], BF16),
    "gT": ([D, NLOC], BF16),
}
PHASE_IO = {
    "p1": ((), ("KT", "V4", "QT", "catT")),
    "p2": (("KT", "V4", "QT"), ("catB",)),
    "p3": (("catT", "catB"), ("hA", "hn")),
    "m0a": (("hA", "hn"), ("hB",)),
    "m0b": (("hB", "hn"), ("hA",)),
    "p4a": (("hA",), ("gT",)),
    "p4b": (("hA", "gT"), ("hB", "hn")),
    "m1a": (("hB", "hn"), ("hA",)),
    "m1b": (("hA", "hn"), ()),
}


def build_program(phases=ALL_PHASES, dump=(), dbg=None):
    dbg = dbg or {}
    nc = bass.Bass("TRN2", target_bir_lowering=False)
    dr = {}

    def din(name, shape, dt=F32):
        dr[name] = nc.dram_tensor(name, list(shape), dt, kind="ExternalInput").ap()

    din("xT_loc", [D, NLOC])
    din("xT_pre", [D, NPRE])
    din("ropeC", [128, NKEY])
    din("ropeS", [128, NKEY])
    din("kbias", [128, NKEY // 128])
    din("flags", [128, 4])
    din("pfix", [128, 4 * 16])
    din("vec", [128, VC_N])
    din("lamv", [128, 4 * 64])
    din("cmat", [128, 5 * 128])
    din("selm", [64, 256])
    if "p1" in phases:
        din("w_in", [D, 2048])
        din("pool_w", [4, 128, 128])
    if "p3" in phases:
        din("w_out", [D, D])
    if any(p in phases for p in ("m0a", "m0b", "m1a", "m1b")):
        din("w_up", [2, D, DFF])
        din("w_down", [2, DFF, D])
    if "p4a" in phases:
        din("pw1", [D, 2048])
    if "p4b" in phases:
        din("pw2", [D, D])
        din("wst", [128, 32 * 8])
        din("i4", [128, 32])

    produced = set()
    consumed_ext = set()
    for ph in ALL_PHASES:
        if ph not in phases:
            continue
        ins_, outs_ = PHASE_IO[ph]
        for t in ins_:
            if t not in produced:
                consumed_ext.add(t)
        produced.update(outs_)
    for name, (shape, dt) in SCRATCH.items():
        if name in consumed_ext:
            kind = "ExternalInput"
        elif name in dump and name in produced:
            kind = "ExternalOutput"
        else:
            kind = "Internal"
        dr[name] = nc.dram_tensor(name, list(shape), dt, kind=kind).ap()
    if "m1b" in phases:
        dr["yT"] = nc.dram_tensor("yT", [D, HALF], F32, kind="ExternalOutput").ap()

    with ExitStack() as stack:
        P = Prog(nc, stack)

        uniq = [0]

        def sb(st, name, shape, dt):
            uniq[0] += 1
            return st.enter_context(nc.sbuf_tensor("%s_u%d" % (name, uniq[0]), list(shape), dt))

        psum = stack.enter_context(nc.psum_tensor("psum", [128, 8, 512], F32))
        wg = [None, None]

        def wl_rows(name, b, width, row_sel):
            def f():
                t = wg[b]
                for c in range(8):
                    P.add("pool", lambda g, c=c, t=t: g.dma_start(out=t[:, c * width:(c + 1) * width], in_=row_sel(c)),
                          writes=["wg%d" % b], dma="wg%d" % b)
            return f

        wplan = []
        if "p1" in phases:
            wplan.append(("p1", 2048, lambda c: dr["w_in"][c * 128:(c + 1) * 128, :]))
        if "p3" in phases:
            wplan.append(("p3", 1024, lambda c: dr["w_out"][c * 128:(c + 1) * 128, :]))
        for nm, l, hf in (("m0a", 0, 0), ("m0b", 0, 1)):
            if nm in phases:
                wplan.append((nm, 2048, lambda c, l=l, hf=hf: dr["w_up"][l, c * 128:(c + 1) * 128, hf * 2048:(hf + 1) * 2048]))
        if "p4a" in phases:
            wplan.append(("p4a", 2048, lambda c: dr["pw1"][c * 128:(c + 1) * 128, :]))
        for nm, l, hf in (("m1a", 1, 0), ("m1b", 1, 1)):
            if nm in phases:
                wplan.append((nm, 2048, lambda c, l=l, hf=hf: dr["w_up"][l, c * 128:(c + 1) * 128, hf * 2048:(hf + 1) * 2048]))
        wslot = {nm: i % 2 for i, (nm, _, _) in enumerate(wplan)}
        wloaders = {nm: wl_rows(nm, i % 2, width, sel) for i, (nm, width, sel) in enumerate(wplan)}
        GROUP2 = ("m1a", "m1b")
        wnext = {wplan[i][0]: wplan[i + 1][0] for i in range(len(wplan) - 1)
                 if (wplan[i][0] in GROUP2) == (wplan[i + 1][0] in GROUP2)}
        wfirst = set()
        for grp in (False, True):
            names = [w[0] for w in wplan if (w[0] in GROUP2) == grp]
            if names:
                wfirst.add(names[0])

        def wstart(nm):
            if nm in wfirst:
                wloaders[nm]()

        def wprefetch(nm):
            if nm in wnext:
                wloaders[wnext[nm]]()
        cst = sb(stack, "cst", [128, 5, 128], BF16)
        selm = sb(stack, "selm_sb", [64, 2, 128], F32)
        cavg = sb(stack, "cavg", [128, 2, 128], BF16)
        vec = sb(stack, "vecs", [128, VC_N], F32)
        flags = sb(stack, "flags_sb", [128, 4], F32)
        kbias = sb(stack, "kbias_sb", [128, NKEY // 128], F32)
        lam_sb = sb(stack, "lam_sb", [128, 8], F32)
        lamv = sb(stack, "lamv_sb", [128, 4, 64], F32)
        lamt = sb(stack, "lamt_sb", [128, 2, 64], F32)
        subl = sb(stack, "subl_sb", [128, 1], F32)
        epsb = sb(stack, "epsb", [128, 4], F32)

        P.add("pool", lambda g: g.dma_start(out=cst[:], in_=dr["cmat"].rearrange("p (a b) -> p a b", a=5)),
              writes=["cst"], dma="cst")
        P.add("sp", lambda e: e.dma_start(out=vec[:], in_=dr["vec"]), writes=["vec"], dma="vec")
        P.add("sp", lambda e: e.dma_start(out=selm[:], in_=dr["selm"].rearrange("p (a b) -> p a b", a=2)), writes=["selm"], dma="selm")
        P.add("sp", lambda e: e.dma_start(out=flags[:], in_=dr["flags"]), writes=["flags"], dma="flags")
        P.add("sp", lambda e: e.dma_start(out=kbias[:], in_=dr["kbias"]), writes=["kbias"], dma="kbias")
        P.add("sp", lambda e: e.dma_start(out=lamv[:], in_=dr["lamv"].rearrange("p (a b) -> p a b", a=4)),
              writes=["lamv"], dma="lamv")
        P.add("dve", lambda e: e.memset(cavg[:, 0, :], 1.0 / D), writes=["cavg"])
        P.add("dve", lambda e: e.memset(cavg[:, 1, :], 1.0 / 128), writes=["cavg"])
        P.add("dve", lambda e: e.memset(epsb[:, 0:1], RMS_EPS), writes=["epsb"])
        P.add("dve", lambda e: e.memset(epsb[:, 1:2], LN_EPS), writes=["epsb"])
        P.add("dve", lambda e: e.memset(epsb[:, 2:3], SUBLN_EPS), writes=["epsb"])
        lambda_init = 0.8 - 0.6 * math.exp(-0.3 * 0)
        P.add("dve", lambda e: e.tensor_tensor(out=lamt[:, 0, :], in0=lamv[:, 0, :], in1=lamv[:, 1, :], op=ALU.mult),
              reads=["lamv"], writes=["lamt"])
        P.add("dve", lambda e: e.tensor_tensor(out=lamt[:, 1, :], in0=lamv[:, 2, :], in1=lamv[:, 3, :], op=ALU.mult),
              reads=["lamv"], writes=["lamt"])
        P.add("dve", lambda e: e.reduce_sum(out=lam_sb[:, 2:3], in_=lamt[:, 0, :], axis=mybir.AxisListType.X),
              reads=["lamt"], writes=["lam_a"])
        P.add("dve", lambda e: e.reduce_sum(out=lam_sb[:, 3:4], in_=lamt[:, 1, :], axis=mybir.AxisListType.X),
              reads=["lamt"], writes=["lam_b"])
        P.add("act", lambda e: e.activation(out=lam_sb[:, 4:6], in_=lam_sb[:, 2:4], func=AF.Exp),
              reads=["lam_a", "lam_b"], writes=["lam_c"])
        P.add("dve", lambda e: e.scalar_tensor_tensor(out=lam_sb[:, 0:1], in0=lam_sb[:, 5:6], scalar=-lambda_init,
                                                      in1=lam_sb[:, 4:5], op0=ALU.add, op1=ALU.subtract),
              reads=["lam_c"], writes=["lam"])
        P.add("dve", lambda e: e.tensor_scalar(out=subl[:], in0=vec[:, VC_SUBLN:VC_SUBLN + 1], scalar1=1.0 - lambda_init,
                                               scalar2=None, op0=ALU.mult),
              reads=["vec"], writes=["subl"])
        IDENT, PERM, TRI, ONES = 0, 1, 2, 4

        def gcol(norm_idx, c):
            j = VC_GAIN + norm_idx * 8 + c
            return vec[:, j:j + 1]

        def vcol(base, c):
            return vec[:, base + c:base + c + 1]

        def emit_rstd(tag, sq_ap_fn, nch, n, ps_bank, avg_idx, eps_col, rstd_ap, tmp_ap, sq_res, out_res):
            if sq_res is None:
                sq_res = ["sq%d" % c for c in range(nch)]

            def mm(e):
                ins = None
                for c in range(nch):
                    ins = e.matmul(psum[:, ps_bank, :n], cavg[:, avg_idx, :], sq_ap_fn(c), start=(c == 0), stop=(c == nch - 1))
                return ins
            P.add("pe", mm, reads=list(sq_res) + ["cavg"], writes=["ps%d" % ps_bank])
            P.add("act", lambda e: e.activation(out=tmp_ap, in_=psum[:, ps_bank, :n], func=AF.Ln,
                                                bias=epsb[:, eps_col:eps_col + 1], scale=1.0),
                  reads=["ps%d" % ps_bank, "epsb"], writes=[out_res + "_t"])
            P.add("act", lambda e: e.activation(out=rstd_ap, in_=tmp_ap, func=AF.Exp, scale=-0.5),
                  reads=[out_res + "_t"], writes=[out_res])

        def phase1():
            with ExitStack() as st:
                win = WV(wg[wslot["p1"]], 2048)
                wres = "wg%d" % wslot["p1"]
                poolw = sb(st, "poolw", [128, 4, 128], BF16)
                pfix = sb(st, "pfix_sb", [128, 4, 16], F32)
                xt = [sb(st, "xt%d" % i, [128, 8, 512], F32) for i in range(2)]
                cs = [sb(st, "cs%d" % i, [128, 2, 512], F32) for i in range(2)]
                sq = sb(st, "sq", [128, 8, 512], BF16)
                rstd = sb(st, "rstd", [128, 512], F32)
                rtmp = sb(st, "rtmp", [128, 512], F32)
                hnd = [sb(st, "hn_sb%d" % i, [128, 8, 512], BF16) for i in range(2)]
                ubuf = [sb(st, "ubuf%d" % i, [128, 4, 16 + 512], F32) for i in range(2)]
                wk = [sb(st, "wk%d" % i, [128, 16 + 512], F32) for i in range(2)]
                pooled = sb(st, "pooled", [128, 4, 512], BF16)
                ptmp = sb(st, "ptmp", [128, 2, 16], F32)
                ra = [sb(st, "ra%d" % i, [128, 512], BF16) for i in range(2)]
                rb = [sb(st, "rb%d" % i, [128, 512], BF16) for i in range(2)]
                qrot = [sb(st, "qrot%d" % i, [128, 4, 512], BF16) for i in range(2)]
                krot = [sb(st, "krot%d" % i, [128, 4, 512], BF16) for i in range(2)]
                vbuf = sb(st, "vbuf", [128, 4, 512], BF16)
                cata = sb(st, "cata", [128, 4, 512], BF16)

                wstart("p1")
                P.add("pool", lambda g: g.dma_start(out=poolw[:], in_=dr["pool_w"].rearrange("g c d -> c g d")),
                      writes=["poolw"], dma="poolw")
                wprefetch("p1")
                P.add("sp", lambda e: e.dma_start(out=pfix[:], in_=dr["pfix"].rearrange("p (a b) -> p a b", a=4)),
                      writes=["pfix"], dma="pfix")
                P.add("dve", lambda e: e.memset(ubuf[0][:, :, 0:16], 0.0), writes=["ubuf0"])

                tiles = [("pre", o, n) for (o, n) in PRE_TILES] + [("loc", o, n) for (o, n) in LOC_TILES]
                if "p1_tiles" in dbg:
                    tiles = [tiles[j] for j in dbg["p1_tiles"]]
                loc_index = {}
                for i, (kind, off, n) in enumerate(tiles):
                    if kind == "loc":
                        loc_index[i] = len(loc_index)

                def load_x(i):
                    kind, off, n = tiles[i]
                    s = i % 2
                    src = dr["xT_pre"] if kind == "pre" else dr["xT_loc"]
                    P.add("sp", lambda e: e.dma_start(out=xt[s][:, :, :n], in_=src.rearrange("(c p) t -> p c t", p=128)[:, :, off:off + n]),
                          writes=["xt%d" % s], dma="xt%d" % s)

                def load_cs(i):
                    kind, off, n = tiles[i]
                    s = i % 2
                    koff = off if kind == "pre" else NPRE + off
                    P.add("sp", lambda e: e.dma_start(out=cs[s][:, 0, :n], in_=dr["ropeC"][:, koff:koff + n]),
                          writes=["cs%d" % s], dma="cs%d" % s)
                    P.add("sp", lambda e: e.dma_start(out=cs[s][:, 1, :n], in_=dr["ropeS"][:, koff:koff + n]),
                          writes=["cs%d" % s], dma="cs%d" % s)

                rr = [0]

                def stage_a(i):
                    kind, off, n = tiles[i]
                    s = i % 2
                    xs = xt[s]
                    xres = "xt%d" % s
                    hn = hnd[s]
                    for c in range(8):
                        P.add("act", lambda e, c=c: e.activation(out=sq[:, c, :n], in_=xs[:, c, :n], func=AF.Square),
                              reads=[xres], writes=["sq%d" % c])
                    emit_rstd("n", lambda c: sq[:, c, :n], 8, n, 0, 0, 0, rstd[:, :n], rtmp[:, :n], None, "rstd")
                    for c in range(8):
                        P.add("dve", lambda e, c=c: e.scalar_tensor_tensor(out=hn[:, c, :n], in0=xs[:, c, :n], scalar=gcol(0, c),
                                                                           in1=rstd[:, :n], op0=ALU.mult, op1=ALU.mult),
                              reads=[xres, "rstd", "vec"], writes=["hn%d_%d" % (s, c)])

                def stage_b(i):
                    kind, off, n = tiles[i]
                    s = i % 2
                    hn = hnd[s]
                    koff = off if kind == "pre" else NPRE + off
                    hn_res = ["hn%d_%d" % (s, c) for c in range(8)]
                    nb = n // 128
                    is_loc = kind == "loc"

                    def proj_fm(ocol, bank):
                        def mm(e):
                            ins = None
                            for c in range(8):
                                ins = e.matmul(psum[:, bank, :n], win[:, c, ocol:ocol + 128], hn[:, c, :n], start=(c == 0), stop=(c == 7))
                            return ins
                        P.add("pe", mm, reads=hn_res + [wres], writes=["ps%d" % bank])

                    def rope_chunk(ocol, dst_ap, dst_res, between=None):
                        k = rr[0]
                        rr[0] += 1
                        bank = 1 + (k % 2)
                        b2 = 3 + (k % 2)
                        u = k % 2
                        proj_fm(ocol, bank)
                        P.add("dve", lambda e: e.tensor_tensor(out=ra[u][:, :n], in0=psum[:, bank, :n], in1=cs[s][:, 0, :n], op=ALU.mult),
                              reads=["ps%d" % bank, "cs%d" % s], writes=["ra%d" % u])
                        P.add("dve", lambda e: e.tensor_tensor(out=rb[u][:, :n], in0=psum[:, bank, :n], in1=cs[s][:, 1, :n], op=ALU.mult),
                              reads=["ps%d" % bank, "cs%d" % s], writes=["rb%d" % u])
                        if between is not None:
                            between()

                        def mm(e):
                            e.matmul(psum[:, b2, :n], cst[:, IDENT, :], ra[u][:, :n], start=True, stop=False)
                            return e.matmul(psum[:, b2, :n], cst[:, PERM, :], rb[u][:, :n], start=False, stop=True)
                        P.add("pe", mm, reads=["ra%d" % u, "rb%d" % u, "cst"], writes=["ps%d" % b2])
                        P.add("act", lambda e: e.activation(out=dst_ap, in_=psum[:, b2, :n], func=AF.Identity),
                              reads=["ps%d" % b2], writes=[dst_res])

                    def v_block(tb):
                        bank = 5 + (tb % 2)

                        def mmv(e):
                            ins = None
                            for c in range(8):
                                ins = e.matmul(psum[:, bank, :], hn[:, c, tb * 128:(tb + 1) * 128], win[:, c, 1536:2048], start=(c == 0), stop=(c == 7))
                            return ins
                        P.add("pe", mmv, reads=hn_res + [wres], writes=["ps%d" % bank])
                        P.add("act", lambda e: e.activation(out=vbuf[:, tb, :], in_=psum[:, bank, :], func=AF.Identity),
                              reads=["ps%d" % bank], writes=["vbuf"])

                    if is_loc:
                        lt = loc_index[i]
                        us = lt % 2
                        ub = ubuf[us]
                        ures = "ubuf%d" % us

                    def u_chunk(g):
                        bank = 7
                        proj_fm(g * 128, bank)
                        P.add("act", lambda e: e.activation(out=ub[:, g, 16:16 + n], in_=psum[:, bank, :n], func=AF.Identity),
                              reads=["ps%d" % bank], writes=[ures + "_%d" % g])

                    for h in range(4):
                        rope_chunk(1024 + h * 128, krot[s][:, h, :n], "krot%d" % s,
                                   between=(lambda h=h: v_block(h)) if h < nb else None)
                        if is_loc:
                            rope_chunk(512 + h * 128, qrot[s][:, h, :n], "qrot%d" % s, between=lambda h=h: u_chunk(h))
                    for h in range(4):
                        P.add("sp", lambda e, h=h: e.dma_start(out=dr["KT"][h, :, koff:koff + n], in_=krot[s][:, h, :n]),
                              reads=["krot%d" % s], dma="krot%d" % s)
                    for h in range(4):
                        P.add("sp", lambda e, h=h: e.dma_start(out=dr["V4"][h, :, koff // 128:koff // 128 + nb, :],
                                                              in_=vbuf[:, 0:nb, h * 128:(h + 1) * 128]),
                              reads=["vbuf"], dma="vbuf")
                    if not is_loc:
                        return
                    for h in range(4):
                        P.add("sp", lambda e, h=h: e.dma_start(out=dr["QT"][h, :, off:off + n], in_=qrot[s][:, h, :n]),
                              reads=["qrot%d" % s], dma="qrot%d" % s)
                    for g, w in enumerate(POOL_WINDOWS):
                        ug = ures + "_%d" % g
                        lvl = 1
                        k = 0
                        src = None
                        srcres = None
                        while lvl < w:
                            lo = -(w - 2 * lvl)
                            dst = wk[k % 2]
                            dres = "wk%d" % (k % 2)
                            if src is None:
                                a0 = ub[:, g, 16 + lo:16 + n]
                                a1 = ub[:, g, 16 + lo - lvl:16 + n - lvl]
                                sres = [ug, ures]
                            else:
                                a0 = src[:, 16 + lo:16 + n]
                                a1 = src[:, 16 + lo - lvl:16 + n - lvl]
                                sres = [srcres]
                            P.add("pool", lambda e, a0=a0, a1=a1, dst=dst, lo=lo: e.tensor_tensor(out=dst[:, 16 + lo:16 + n], in0=a0, in1=a1, op=ALU.add),
                                  reads=sres, writes=[dres])
                            src, srcres = dst, dres
                            lvl *= 2
                            k += 1
                        P.add("dve", lambda e, g=g, w=w, src=src: e.scalar_tensor_tensor(out=pooled[:, g, :n], in0=src[:, 16:16 + n], scalar=1.0 / w,
                                                                                        in1=ub[:, g, 16:16 + n], op0=ALU.mult, op1=ALU.subtract),
                              reads=[srcres, ug], writes=["pooled%d" % g])
                        if lt == 1:
                            P.add("dve", lambda e, g=g, src=src: e.tensor_tensor(out=ptmp[:, 0, :], in0=src[:, 16:32], in1=pfix[:, g, :], op=ALU.mult),
                                  reads=[srcres, "pfix"], writes=["ptmp0"])
                            P.add("dve", lambda e, g=g, w=w, src=src: e.scalar_tensor_tensor(out=ptmp[:, 1, :], in0=src[:, 16:32], scalar=1.0 / w,
                                                                                            in1=ub[:, g, 16:32], op0=ALU.mult, op1=ALU.subtract),
                                  reads=[srcres, ug], writes=["ptmp1"])
                            P.add("dve", lambda e, g=g: e.tensor_tensor(out=pooled[:, g, 0:16], in0=ptmp[:, 0, :], in1=ptmp[:, 1, :], op=ALU.add),
                                  reads=["ptmp0", "ptmp1"], writes=["pooled%d" % g])
                        P.add("pe", lambda e, g=g: e.matmul(psum[:, 0, :n], poolw[:, g, :], pooled[:, g, :n], start=True, stop=True),
                              reads=["pooled%d" % g, "poolw"], writes=["ps0"])
                        P.add("act", lambda e, g=g: e.activation(out=cata[:, g, :n], in_=psum[:, 0, :n], func=AF.Identity, scale=vcol(VC_PSCALE, g)),
                              reads=["ps0", "vec"], writes=["cata"])
                    nub = ubuf[1 - us]
                    P.add("dve", lambda e: e.tensor_copy(out=nub[:, :, 0:16], in_=ub[:, :, n:n + 16]),
                          reads=[ures] + [ures + "_%d" % g for g in range(4)], writes=["ubuf%d" % (1 - us)])
                    P.add("sp", lambda e: e.dma_start(out=dr["catT"].rearrange("(g p) t -> p g t", p=128)[:, 0:4, off:off + n], in_=cata[:, :, :n]),
                          reads=["cata"], dma="cata")

                nt = len(tiles)
                load_x(0)
                load_cs(0)
                if nt > 1:
                    load_x(1)
                stage_a(0)
                for i in range(nt):
                    if i + 1 < nt:
                        stage_a(i + 1)
                        load_cs(i + 1)
                    if i + 2 < nt:
                        load_x(i + 2)
                    stage_b(i)
                P.barrier()

        def phase2():
            with ExitStack() as st:
                kt = [sb(st, "kt%d" % i, [128, NKEY], BF16) for i in range(2)]
                v4 = [sb(st, "v4_%d" % i, [128, NKEY // 128, 128], BF16) for i in range(2)]
                qt = [sb(st, "qt%d" % i, [128, 512], BF16) for i in range(2)]
                pT = [sb(st, "pT%d" % i, [128, 2, 512], BF16) for i in range(2)]
                lsb = sb(st, "lsb", [64, 512], F32)
                oo = [sb(st, "oo%d" % m, [128, 512], F32) for m in range(2)]
                osq = sb(st, "osq", [128, 512], BF16)
                orstd = sb(st, "orstd", [128, 512], F32)
                otmp = sb(st, "otmp", [128, 512], F32)
                on = [sb(st, "on%d" % i, [128, 512], BF16) for i in range(2)]
                nlam = sb(st, "nlam", [64, 1], F32)
                scale = 64 ** -0.5
                P.add("dve", lambda e: e.memset(nlam[0:32, :], 1.0), writes=["nlam"])
                P.add("dve", lambda e: e.tensor_copy(out=nlam[32:64, :], in_=lam_sb[32:64, 0:1]), reads=["lam"], writes=["nlam"])

                def load_head(h):
                    hs = h % 2
                    P.add("sp", lambda e: e.dma_start(out=kt[hs][:], in_=dr["KT"][h]), writes=["kt%d" % hs], dma="kt%d" % hs)
                    P.add("sp", lambda e: e.dma_start(out=v4[hs][:], in_=dr["V4"][h]), writes=["v4_%d" % hs], dma="v4_%d" % hs)

                jobs = [(h, t) for h in range(dbg.get("p2_heads", 4)) for t in range(len(LOC_TILES))]
                if "p2_jobs" in dbg:
                    jobs = [tuple(j) for j in dbg["p2_jobs"]]

                def load_q(j):
                    h, t = jobs[j]
                    off, n = LOC_TILES[t]
                    qs = j % 2
                    P.add("sp", lambda e: e.dma_start(out=qt[qs][:, :n], in_=dr["QT"][h, :, off:off + n]),
                          writes=["qt%d" % qs], dma="qt%d" % qs)

                def job(j, pending):
                    h, t = jobs[j]
                    off, n = LOC_TILES[t]
                    hs = h % 2
                    qs = j % 2
                    qb0 = off // 128
                    nb = n // 128
                    nkb = NPREB + qb0 + nb
                    ktr, v4r, qtr = "kt%d" % hs, "v4_%d" % hs, "qt%d" % qs

                    def c0_of(kb):
                        jj = kb - (NPREB + qb0)
                        return 0 if jj < 0 else jj * 128

                    def qk(kb):
                        c0 = c0_of(kb)
                        for m in range(2):
                            bank = 2 * (kb % 2) + m
                            P.add("pe", lambda e, m=m, bank=bank: e.matmul(psum[:, bank, c0:n], kt[hs][m * 64:(m + 1) * 64, kb * 128:(kb + 1) * 128],
                                                                           qt[qs][m * 64:(m + 1) * 64, c0:n], start=True, stop=True),
                                  reads=[ktr, qtr], writes=["ps%d" % bank])

                    def ex(kb):
                        c0 = c0_of(kb)
                        diag = kb >= NPREB + qb0
                        b0 = 2 * (kb % 2)
                        pt = pT[kb % 2]
                        pr = "pT%d" % (kb % 2)
                        bias = kbias[:, kb:kb + 1] if kb <= NPREB else 0.0
                        P.add("act", lambda e: e.activation(out=pt[:, :, c0:n], in_=psum[:, b0:b0 + 2, c0:n], func=AF.Exp, bias=bias, scale=scale),
                              reads=["ps%d" % b0, "ps%d" % (b0 + 1), "kbias"], writes=[pr])
                        if diag:
                            P.add("dve", lambda e: e.tensor_tensor(out=pt[:, :, c0:c0 + 128], in0=pt[:, :, c0:c0 + 128], in1=cst[:, TRI:TRI + 2, :], op=ALU.mult),
                                  reads=["cst"], writes=[pr])

                    def pv(kb):
                        c0 = c0_of(kb)
                        first = kb == 0
                        last = kb == nkb - 1
                        pt = pT[kb % 2]
                        pr = "pT%d" % (kb % 2)

                        def mm(e):
                            for m in range(2):
                                e.matmul(psum[:, 4 + m, c0:n], v4[hs][:, kb, :], pt[:, m, c0:n], start=first, stop=last, skip_group_check=True)
                            ins = None
                            for m in range(2):
                                ins = e.matmul(psum[32 * m:32 * m + 32, 6, c0:n], cst[:, ONES, 0:32], pt[:, m, c0:n], start=first, stop=last,
                                               skip_group_check=True, tile_position=(0, 32 * m))
                            return ins
                        P.add("pe", mm, reads=[pr, v4r, "cst"], writes=["ps4", "ps5", "ps6"])

                    qk(0)
                    qk(1)
                    for kb in range(nkb):
                        ex(kb)
                        if kb + 2 < nkb:
                            qk(kb + 2)
                        pv(kb)
                        if pending and kb >= 3 and kb % 3 == 0:
                            pending.pop(0)()
                    while pending:
                        pending.pop(0)()
                    P.add("dve", lambda e: e.tensor_scalar(out=lsb[:, :n], in0=psum[0:64, 6, :n], scalar1=1e-30, scalar2=None, op0=ALU.max),
                          reads=["ps6"], writes=["lsb"])
                    for m in range(2):
                        P.add("dve", lambda e, m=m: e.tensor_copy(out=oo[m][:, :n], in_=psum[:, 4 + m, :n]),
                              reads=["ps%d" % (4 + m)], writes=["oo%d" % m])
                    os_ = j % 2

                    def st0():
                        P.add("dve", lambda e: e.reciprocal(out=lsb[:, :n], in_=lsb[:, :n]), writes=["lsb"])
                        P.add("dve", lambda e: e.tensor_scalar(out=lsb[:, :n], in0=lsb[:, :n], scalar1=nlam[:, 0:1], scalar2=None, op0=ALU.mult),
                              reads=["nlam"], writes=["lsb"])
                        P.add("pe", lambda e: e.matmul(psum[:, 7, :n], selm[:, 0, :], lsb[:, :n], start=True, stop=True),
                              reads=["lsb", "selm"], writes=["ps7"])

                    def st1():
                        P.add("dve", lambda e: e.tensor_tensor(out=oo[0][:, :n], in0=oo[0][:, :n], in1=psum[:, 7, :n], op=ALU.mult),
                              reads=["ps7"], writes=["oo0"])
                        P.add("pe", lambda e: e.matmul(psum[:, 7, :n], selm[:, 1, :], lsb[:, :n], start=True, stop=True),
                              reads=["lsb", "selm"], writes=["ps7"])

                    def st2():
                        P.add("dve", lambda e: e.tensor_tensor(out=oo[1][:, :n], in0=oo[1][:, :n], in1=psum[:, 7, :n], op=ALU.mult),
                              reads=["ps7"], writes=["oo1"])
                        P.add("dve", lambda e: e.tensor_tensor(out=oo[0][:, :n], in0=oo[0][:, :n], in1=oo[1][:, :n], op=ALU.add),
                              reads=["oo1"], writes=["oo0"])
                        P.add("dve", lambda e: e.tensor_tensor(out=osq[:, :n], in0=oo[0][:, :n], in1=oo[0][:, :n], op=ALU.mult), reads=["oo0"], writes=["osq"])

                        def mm(e):
                            return e.matmul(psum[:, 7, :n], cavg[:, 1, :], osq[:, :n], start=True, stop=True)
                        P.add("pe", mm, reads=["osq", "cavg"], writes=["ps7"])

                    def st3():
                        P.add("act", lambda e: e.activation(out=otmp[:, :n], in_=psum[:, 7, :n], func=AF.Ln, bias=epsb[:, 2:3], scale=1.0),
                              reads=["ps7", "epsb"], writes=["otmp"])
                        P.add("act", lambda e: e.activation(out=orstd[:, :n], in_=otmp[:, :n], func=AF.Exp, scale=-0.5),
                              reads=["otmp"], writes=["orstd"])

                    def st4():
                        P.add("dve", lambda e: e.scalar_tensor_tensor(out=on[os_][:, :n], in0=oo[0][:, :n], scalar=subl[:, 0:1], in1=orstd[:, :n],
                                                                      op0=ALU.mult, op1=ALU.mult),
                              reads=["oo0", "orstd", "subl"], writes=["on%d" % os_])
                        P.add("sp", lambda e: e.dma_start(out=dr["catB"][h * 128:(h + 1) * 128, off:off + n], in_=on[os_][:, :n]),
                              reads=["on%d" % os_], dma="on%d" % os_)
                    return [st0, st1, st2, st3, st4]

                loaded = set()
                pend = []
                load_q(0)
                for j in range(len(jobs)):
                    h = jobs[j][0]
                    if h not in loaded:
                        load_head(h)
                        loaded.add(h)
                    if j + 1 < len(jobs):
                        load_q(j + 1)
                    if j == 0 or jobs[j - 1][0] != h:
                        nxt = [jj[0] for jj in jobs[j:] if jj[0] != h]
                        if nxt and nxt[0] not in loaded:
                            load_head(nxt[0])
                            loaded.add(nxt[0])
                    pend = job(j, pend)
                while pend:
                    pend.pop(0)()
                P.barrier()

        def phase3():
            with ExitStack() as st:
                wout = WV(wg[wslot["p3"]], 1024)
                NS = 3
                cat = [sb(st, "cat%d" % i, [128, 8, 512], BF16) for i in range(NS)]
                xr = [sb(st, "xr%d" % i, [128, 8, 512], F32) for i in range(NS)]
                sqd = [sb(st, "sq3_%d" % i, [128, 8, 512], BF16) for i in range(2)]
                rstd = sb(st, "rstd3", [128, 512], F32)
                rtmp = sb(st, "rtmp3", [128, 512], F32)
                hno = [sb(st, "hno%d" % i, [128, 8, 512], BF16) for i in range(2)]
                wstart("p3")
                wprefetch("p3")
                tiles = LOC_TILES

                def load(i):
                    off, n = tiles[i]
                    s_ = i % NS
                    P.add("sp", lambda e: e.dma_start(out=cat[s_][:, 0:4, :n], in_=dr["catT"].rearrange("(c p) t -> p c t", p=128)[:, 0:4, off:off + n]),
                          writes=["cat%d" % s_], dma="cat%d" % s_)
                    P.add("sp", lambda e: e.dma_start(out=cat[s_][:, 4:8, :n], in_=dr["catB"].rearrange("(c p) t -> p c t", p=128)[:, :, off:off + n]),
                          writes=["cat%d" % s_], dma="cat%d" % s_)
                    P.add("sp", lambda e: e.dma_start(out=xr[s_][:, :, :n], in_=dr["xT_loc"].rearrange("(c p) t -> p c t", p=128)[:, :, off:off + n]),
                          writes=["xr%d" % s_] + ["xr%d_%d" % (s_, c) for c in range(8)], dma="xr%d" % s_)

                def s1(i):
                    off, n = tiles[i]
                    s_ = i % NS
                    sq = sqd[i % 2]
                    for oc in range(8):
                        bank = oc % 2

                        def mm(e, oc=oc, bank=bank):
                            ins = None
                            for c in range(8):
                                ins = e.matmul(psum[:, bank, :n], wout[:, c, oc * 128:(oc + 1) * 128], cat[s_][:, c, :n], start=(c == 0), stop=(c == 7))
                            return ins
                        P.add("pe", mm, reads=["cat%d" % s_, "wg%d" % wslot["p3"]], writes=["ps%d" % bank])
                        P.add("dve", lambda e, oc=oc, bank=bank: e.tensor_tensor(out=xr[s_][:, oc, :n], in0=psum[:, bank, :n], in1=xr[s_][:, oc, :n], op=ALU.add),
                              reads=["ps%d" % bank], writes=["xr%d_%d" % (s_, oc), "xr%d" % s_])
                        P.add("act", lambda e, oc=oc: e.activation(out=sq[:, oc, :n], in_=xr[s_][:, oc, :n], func=AF.Square),
                              reads=["xr%d_%d" % (s_, oc)], writes=["sq3_%d_%d" % (i % 2, oc)])
                    P.add("sp", lambda e: e.dma_start(out=dr["hA"].rearrange("(c p) t -> p c t", p=128)[:, :, off:off + n], in_=xr[s_][:, :, :n]),
                          reads=["xr%d" % s_] + ["xr%d_%d" % (s_, c) for c in range(8)], dma="xr%d" % s_)

                def s2(i):
                    off, n = tiles[i]
                    s_ = i % NS
                    h_ = i % 2
                    sq = sqd[h_]
                    emit_rstd("3", lambda c: sq[:, c, :n], 8, n, 2, 0, 0, rstd[:, :n], rtmp[:, :n], ["sq3_%d_%d" % (h_, c) for c in range(8)], "rstd3")
                    for c in range(8):
                        P.add("dve", lambda e, c=c: e.scalar_tensor_tensor(out=hno[h_][:, c, :n], in0=xr[s_][:, c, :n], scalar=gcol(1, c), in1=rstd[:, :n],
                                                                           op0=ALU.mult, op1=ALU.mult),
                              reads=["xr%d_%d" % (s_, c), "rstd3", "vec"], writes=["hno%d" % h_])
                    P.add("sp", lambda e: e.dma_start(out=dr["hn"].rearrange("(c p) t -> p c t", p=128)[:, :, off:off + n], in_=hno[h_][:, :, :n]),
                          reads=["hno%d" % h_], dma="hno%d" % h_)

                nt = len(tiles)
                for i0 in range(min(NS, nt)):
                    load(i0)
                s1(0)
                for i in range(nt):
                    if i + 1 < nt:
                        s1(i + 1)
                    s2(i)
                    if i + NS < nt:
                        load(i + NS)
                P.barrier()

        def mlp_half(nm, l, hf, src, dst, final):
            with ExitStack() as st:
                wup = WV(wg[wslot[nm]], 2048)
                wdn = sb(st, "wdn", [128, 16, 1024], BF16)
                hnb = [sb(st, "hnb%d" % i, [128, 8, 512], BF16) for i in range(2)]
                hr = [sb(st, "hr%d" % i, [128, 8, 512], F32) for i in range(2)]
                act = [sb(st, "act%d" % i, [128, 16, 512], BF16) for i in range(2)]
                rl = [sb(st, "rl%d" % i, [128, 512], F32) for i in range(2)]
                if final:
                    sq = sb(st, "sqf", [128, 8, 512], BF16)
                    rstd = sb(st, "rstdf", [128, 512], F32)
                    rtmp = sb(st, "rtmpf", [128, 512], F32)
                wstart(nm)
                for j4 in range(4):
                    P.add("pool", lambda g, j4=j4: g.dma_start(
                        out=wdn[:, j4 * 4:(j4 + 1) * 4, :],
                        in_=dr["w_down"][l, hf * 2048 + j4 * 512:hf * 2048 + (j4 + 1) * 512, :].rearrange("(j p) o -> p j o", p=128)),
                        writes=["wdn"], dma="wdn")
                wprefetch(nm)
                tiles = (LOC_TILES[1:] + LOC_TILES[:1]) if l == 0 else LOC_TILES[1:]
                if "mlp_tiles" in dbg:
                    tiles = [tiles[j] for j in dbg["mlp_tiles"]]

                def load(i):
                    off, n = tiles[i]
                    s_ = i % 2
                    P.add("sp", lambda e: e.dma_start(out=hnb[s_][:, :, :n], in_=dr["hn"].rearrange("(c p) t -> p c t", p=128)[:, :, off:off + n]),
                          writes=["hnb%d" % s_], dma="hnb%d" % s_)
                    P.add("sp", lambda e: e.dma_start(out=hr[s_][:, :, :n], in_=dr[src].rearrange("(c p) t -> p c t", p=128)[:, :, off:off + n]),
                          writes=["hr%d" % s_] + ["hr%d_%d" % (s_, c) for c in range(8)], dma="hr%d" % s_)

                def body(i):
                    off, n = tiles[i]
                    s_ = i % 2
                    a_ = act[s_]
                    for j in range(16):
                        bank = j % 4
                        u = j % 2

                        def mm(e, j=j, bank=bank):
                            ins = None
                            for c in range(8):
                                ins = e.matmul(psum[:, bank, :n], wup[:, c, j * 128:(j + 1) * 128], hnb[s_][:, c, :n], start=(c == 0), stop=(c == 7))
                            return ins
                        P.add("pe", mm, reads=["hnb%d" % s_, "wg%d" % wslot[nm]], writes=["ps%d" % bank])
                        P.add("act", lambda e, bank=bank, u=u: e.activation(out=rl[u][:, :n], in_=psum[:, bank, :n], func=AF.Relu),
                              reads=["ps%d" % bank], writes=["rl%d" % u])
                        P.add("dve", lambda e, j=j, u=u: e.tensor_tensor(out=a_[:, j, :n], in0=rl[u][:, :n], in1=rl[u][:, :n], op=ALU.mult),
                              reads=["rl%d" % u], writes=["act%d_%d" % (s_, j)])
                    ares = ["act%d_%d" % (s_, j) for j in range(16)]
                    for oc in range(8):
                        bank = 4 + (oc % 2)

                        def mm2(e, oc=oc, bank=bank):
                            ins = None
                            for j in range(16):
                                ins = e.matmul(psum[:, bank, :n], wdn[:, j, oc * 128:(oc + 1) * 128], a_[:, j, :n], start=(j == 0), stop=(j == 15))
                            return ins
                        P.add("pe", mm2, reads=ares + ["wdn"], writes=["ps%d" % bank])
                        P.add("dve", lambda e, oc=oc, bank=bank: e.tensor_tensor(out=hr[s_][:, oc, :n], in0=psum[:, bank, :n], in1=hr[s_][:, oc, :n], op=ALU.add),
                              reads=["ps%d" % bank], writes=["hr%d_%d" % (s_, oc), "hr%d" % s_])
                        if final:
                            P.add("act", lambda e, oc=oc: e.activation(out=sq[:, oc, :n], in_=hr[s_][:, oc, :n], func=AF.Square),
                                  reads=["hr%d_%d" % (s_, oc)], writes=["sqf_%d" % oc])
                    hres = ["hr%d" % s_] + ["hr%d_%d" % (s_, c) for c in range(8)]
                    if not final:
                        P.add("sp", lambda e: e.dma_start(out=dr[dst].rearrange("(c p) t -> p c t", p=128)[:, :, off:off + n], in_=hr[s_][:, :, :n]),
                              reads=hres, dma="hr%d" % s_)
                    else:
                        emit_rstd("f", lambda c: sq[:, c, :n], 8, n, 6, 0, 0, rstd[:, :n], rtmp[:, :n], ["sqf_%d" % c for c in range(8)], "rstdf")
                        for c in range(8):
                            P.add("dve", lambda e, c=c: e.scalar_tensor_tensor(out=hr[s_][:, c, :n], in0=hr[s_][:, c, :n], scalar=gcol(4, c), in1=rstd[:, :n],
                                                                               op0=ALU.mult, op1=ALU.mult),
                                  reads=["rstdf", "vec"], writes=["hr%d_%d" % (s_, c), "hr%d" % s_])
                        P.add("sp", lambda e: e.dma_start(out=dr["yT"].rearrange("(c p) t -> p c t", p=128)[:, :, off - HALO:off - HALO + n], in_=hr[s_][:, :, :n]),
                              reads=hres, dma="hr%d" % s_)

                load(0)
                for i in range(len(tiles)):
                    if i + 1 < len(tiles):
                        load(i + 1)
                    body(i)
                P.barrier()

        def phase4a():
            with ExitStack() as st:
                pw1 = WV(wg[wslot["p4a"]], 2048)
                xr = [sb(st, "x4_%d" % i, [128, 8, 512], F32) for i in range(2)]
                sq = sb(st, "sq4", [128, 8, 512], BF16)
                rstd = sb(st, "rstd4", [128, 512], F32)
                rtmp = sb(st, "rtmp4", [128, 512], F32)
                hnd = [sb(st, "hn4_%d" % i, [128, 8, 512], BF16) for i in range(2)]
                sg = [sb(st, "sg%d" % i, [128, 512], F32) for i in range(2)]
                gb = [sb(st, "gb%d" % i, [128, 8, 512], BF16) for i in range(2)]
                wstart("p4a")
                wprefetch("p4a")
                tiles = LOC_TILES

                def load(i):
                    off, n = tiles[i]
                    s_ = i % 2
                    P.add("sp", lambda e: e.dma_start(out=xr[s_][:, :, :n], in_=dr["hA"].rearrange("(c p) t -> p c t", p=128)[:, :, off:off + n]),
                          writes=["x4_%d" % s_], dma="x4_%d" % s_)

                def sa(i):
                    off, n = tiles[i]
                    s_ = i % 2
                    hnb = hnd[s_]
                    for c in range(8):
                        P.add("act", lambda e, c=c: e.activation(out=sq[:, c, :n], in_=xr[s_][:, c, :n], func=AF.Square),
                              reads=["x4_%d" % s_], writes=["sq4_%d" % c])
                    emit_rstd("4", lambda c: sq[:, c, :n], 8, n, 6, 0, 0, rstd[:, :n], rtmp[:, :n], ["sq4_%d" % c for c in range(8)], "rstd4")
                    for c in range(8):
                        P.add("dve", lambda e, c=c: e.scalar_tensor_tensor(out=hnb[:, c, :n], in0=xr[s_][:, c, :n], scalar=gcol(2, c), in1=rstd[:, :n],
                                                                           op0=ALU.mult, op1=ALU.mult),
                              reads=["x4_%d" % s_, "rstd4", "vec"], writes=["hn4_%d_%d" % (s_, c)])

                def sb_(i):
                    off, n = tiles[i]
                    s_ = i % 2
                    hnb = hnd[s_]
                    hres = ["hn4_%d_%d" % (s_, c) for c in range(8)]
                    for c in range(8):
                        ba = 2 * (c % 2)
                        bb = ba + 1
                        u = c % 2

                        def mm(e, col, bank):
                            ins = None
                            for k in range(8):
                                ins = e.matmul(psum[:, bank, :n], pw1[:, k, col:col + 128], hnb[:, k, :n], start=(k == 0), stop=(k == 7))
                            return ins
                        P.add("pe", lambda e, c=c, ba=ba: mm(e, c * 128, ba), reads=hres + ["wg%d" % wslot["p4a"]], writes=["ps%d" % ba])
                        P.add("pe", lambda e, c=c, bb=bb: mm(e, 1024 + c * 128, bb), reads=hres + ["wg%d" % wslot["p4a"]], writes=["ps%d" % bb])
                        P.add("act", lambda e, c=c, bb=bb, u=u: e.activation(out=sg[u][:, :n], in_=psum[:, bb, :n], func=AF.Sigmoid,
                                                                             bias=vcol(VC_PW1B, 8 + c), scale=1.0),
                              reads=["ps%d" % bb, "vec"], writes=["sg%d" % u])
                        P.add("dve", lambda e, c=c, ba=ba, u=u: e.scalar_tensor_tensor(out=gb[s_][:, c, :n], in0=psum[:, ba, :n], scalar=vcol(VC_PW1B, c),
                                                                                      in1=sg[u][:, :n], op0=ALU.add, op1=ALU.mult),
                              reads=["ps%d" % ba, "sg%d" % u, "vec"], writes=["gb%d" % s_])
                    if i == 0:
                        P.add("dve", lambda e: e.tensor_scalar(out=gb[s_][:, :, :n], in0=gb[s_][:, :, :n], scalar1=flags[:, 0:1], scalar2=None, op0=ALU.mult),
                              reads=["flags"], writes=["gb%d" % s_])
                    P.add("sp", lambda e: e.dma_start(out=dr["gT"].rearrange("(c p) t -> p c t", p=128)[:, :, off:off + n], in_=gb[s_][:, :, :n]),
                          reads=["gb%d" % s_], dma="gb%d" % s_)

                nt = len(tiles)
                load(0)
                if nt > 1:
                    load(1)
                sa(0)
                for i in range(nt):
                    if i + 1 < nt:
                        sa(i + 1)
                    if i + 2 < nt:
                        load(i + 2)
                    sb_(i)
                P.barrier()

        def phase4b():
            with ExitStack() as st:
                pw2t = sb(st, "pw2l", [128, 8192], BF16)
                pw2 = WV(pw2t, 1024)
                dgs = sb(st, "dgs", [128, 32, 8, 32], BF16)
                wst = sb(st, "wst_sb", [128, 32, 8], F32)
                i4 = sb(st, "i4_sb", [128, 32], F32)
                SW = 32 + 512
                gst = [sb(st, "gst%d" % i, [128, 32, SW], BF16) for i in range(2)]
                hr = [sb(st, "h4_%d" % i, [128, 8, 512], F32) for i in range(2)]
                b8d = [sb(st, "b8_%d" % i, [128, 8, 512], BF16) for i in range(2)]
                sqd = [sb(st, "sq5_%d" % i, [128, 8, 512], BF16) for i in range(2)]
                mean = sb(st, "mean", [128, 512], F32)
                m2 = sb(st, "m2", [128, 512], F32)
                var = sb(st, "var", [128, 512], F32)
                lrstd = sb(st, "lrstd", [128, 512], F32)
                ltmp = sb(st, "ltmp", [128, 512], F32)
                tt = [sb(st, "tt%d" % i, [128, 512], F32) for i in range(2)]
                rstd = sb(st, "rstd5", [128, 512], F32)
                rtmp = sb(st, "rtmp5", [128, 512], F32)
                hno1 = sb(st, "hno5", [128, 8, 512], BF16)
                hno = [hno1, hno1]
                for c in range(8):
                    P.add("pool", lambda g, c=c: g.dma_start(out=pw2.chunk(c), in_=dr["pw2"][c * 128:(c + 1) * 128, :]),
                          writes=["pw2l"], dma="pw2l")
                for i_ in range(2):
                    P.add("dve", lambda e, i_=i_: e.memset(gst[i_][96:128, :, 28 + 511:28 + 512], 0.0), writes=["gbuf%d" % i_])
                P.add("sp", lambda e: e.dma_start(out=wst[:], in_=dr["wst"].rearrange("p (q t) -> p q t", q=32)), writes=["wst"], dma="wst")
                P.add("sp", lambda e: e.dma_start(out=i4[:], in_=dr["i4"]), writes=["i4"], dma="i4")
                for q in range(32):
                    for tg in range(8):
                        P.add("dve", lambda e, q=q, tg=tg: e.tensor_scalar(out=dgs[:, q, tg, :], in0=i4[:], scalar1=wst[:, q, tg:tg + 1],
                                                                           scalar2=None, op0=ALU.mult),
                              reads=["i4", "wst"], writes=["dgs_%d" % q])
                tiles = LOC_TILES[1:]
                if "p4b_tiles" in dbg:
                    tiles = [tiles[j] for j in dbg["p4b_tiles"]]

                def load(i):
                    off, n = tiles[i]
                    s_ = i % 2
                    for jj in range(4):
                        w_ = 28 + n - (1 if jj == 3 else 0)
                        P.add("sp", lambda e, jj=jj, w_=w_: e.dma_start(
                            out=gst[s_][32 * jj:32 * jj + 32, :, 0:w_],
                            in_=dr["gT"].rearrange("(q ch) t -> ch q t", ch=32)[:, :, off - 30 + jj:off - 30 + jj + w_]),
                            writes=["gbuf%d" % s_], dma="gbuf%d" % s_)

                def load_h(i):
                    off, n = tiles[i]
                    s_ = i % 2
                    P.add("sp", lambda e: e.dma_start(out=hr[s_][:, :, :n], in_=dr["hA"].rearrange("(c p) t -> p c t", p=128)[:, :, off:off + n]),
                          writes=["h4_%d" % s_] + ["h4_%d_%d" % (s_, c) for c in range(8)], dma="h4_%d" % s_)

                def conv_chunk(i, c):
                    off, n = tiles[i]
                    s_ = i % 2
                    gres = "gbuf%d" % s_
                    bank = c % 2
                    b8, sq = b8d[s_], sqd[s_]

                    def mm(e):
                        ins = None
                        for tg in range(8):
                            for qq in range(4):
                                q = 4 * c + qq
                                ins = e.matmul(psum[32 * qq:32 * qq + 32, bank, :n], dgs[:, q, tg, :], gst[s_][:, q, 4 * tg:4 * tg + n],
                                               start=(tg == 0), stop=(tg == 7), skip_group_check=True, tile_position=(0, 32 * qq))
                        return ins
                    P.add("pe", mm, reads=[gres] + ["dgs_%d" % (4 * c + qq) for qq in range(4)], writes=["ps%d" % bank])
                    P.add("act", lambda e: e.activation(out=b8[:, c, :n], in_=psum[:, bank, :n], func=AF.Identity, bias=vcol(VC_DWB, c), scale=1.0),
                          reads=["ps%d" % bank, "vec"], writes=["b8_%d_%d" % (s_, c)])
                    P.add("act", lambda e: e.activation(out=sq[:, c, :n], in_=psum[:, bank, :n], func=AF.Square, bias=vcol(VC_DWB, c), scale=1.0),
                          reads=["ps%d" % bank, "vec"], writes=["sq5_%d_%d" % (s_, c)])

                def stats(i):
                    off, n = tiles[i]
                    s_ = i % 2
                    b8, sq = b8d[s_], sqd[s_]

                    def mm_mean(e):
                        ins = None
                        for c in range(8):
                            ins = e.matmul(psum[:, 2, :n], cavg[:, 0, :], b8[:, c, :n], start=(c == 0), stop=(c == 7))
                        return ins
                    P.add("pe", mm_mean, reads=["b8_%d_%d" % (s_, c) for c in range(8)] + ["cavg"], writes=["ps2"])

                    def mm_msq(e):
                        ins = None
                        for c in range(8):
                            ins = e.matmul(psum[:, 3, :n], cavg[:, 0, :], sq[:, c, :n], start=(c == 0), stop=(c == 7))
                        return ins
                    P.add("pe", mm_msq, reads=["sq5_%d_%d" % (s_, c) for c in range(8)] + ["cavg"], writes=["ps3"])
                    P.add("act", lambda e: e.activation(out=mean[:, :n], in_=psum[:, 2, :n], func=AF.Identity), reads=["ps2"], writes=["mean"])
                    P.add("dve", lambda e: e.tensor_tensor(out=m2[:, :n], in0=mean[:, :n], in1=mean[:, :n], op=ALU.mult), reads=["mean"], writes=["m2"])
                    P.add("dve", lambda e: e.tensor_tensor(out=var[:, :n], in0=psum[:, 3, :n], in1=m2[:, :n], op=ALU.subtract),
                          reads=["ps3", "m2"], writes=["var"])
                    P.add("dve", lambda e: e.tensor_scalar(out=var[:, :n], in0=var[:, :n], scalar1=0.0, scalar2=None, op0=ALU.max), writes=["var"])
                    P.add("act", lambda e: e.activation(out=ltmp[:, :n], in_=var[:, :n], func=AF.Ln, bias=epsb[:, 1:2], scale=1.0),
                          reads=["var", "epsb"], writes=["ltmp"])
                    P.add("act", lambda e: e.activation(out=lrstd[:, :n], in_=ltmp[:, :n], func=AF.Exp, scale=-0.5), reads=["ltmp"], writes=["lrstd"])

                def ln_chunk(i, c):
                    off, n = tiles[i]
                    s_ = i % 2
                    b8 = b8d[s_]
                    u = c % 2
                    P.add("dve", lambda e: e.tensor_tensor(out=tt[u][:, :n], in0=b8[:, c, :n], in1=mean[:, :n], op=ALU.subtract),
                          reads=["b8_%d_%d" % (s_, c), "mean"], writes=["tt%d" % u])
                    P.add("dve", lambda e: e.tensor_tensor(out=tt[u][:, :n], in0=tt[u][:, :n], in1=lrstd[:, :n], op=ALU.mult),
                          reads=["lrstd"], writes=["tt%d" % u])
                    P.add("act", lambda e: e.activation(out=b8[:, c, :n], in_=tt[u][:, :n], func=AF.Silu, bias=vcol(VC_LNB, c), scale=vcol(VC_LNG, c)),
                          reads=["tt%d" % u, "vec"], writes=["b8_%d_%d" % (s_, c)])

                def tail(i):
                    off, n = tiles[i]
                    s_ = i % 2
                    b8, sq = b8d[s_], sqd[s_]
                    for oc in range(8):
                        bank = 4 + (oc % 2)

                        def mm2(e, oc=oc, bank=bank):
                            ins = None
                            for c in range(8):
                                ins = e.matmul(psum[:, bank, :n], pw2[:, c, oc * 128:(oc + 1) * 128], b8[:, c, :n], start=(c == 0), stop=(c == 7))
                            return ins
                        P.add("pe", mm2, reads=["b8_%d_%d" % (s_, c) for c in range(8)] + ["pw2l"], writes=["ps%d" % bank])
                        P.add("dve", lambda e, oc=oc, bank=bank: e.scalar_tensor_tensor(out=hr[s_][:, oc, :n], in0=psum[:, bank, :n], scalar=vcol(VC_PW2B, oc),
                                                                                       in1=hr[s_][:, oc, :n], op0=ALU.add, op1=ALU.add),
                              reads=["ps%d" % bank, "vec"], writes=["h4_%d_%d" % (s_, oc), "h4_%d" % s_])
                        P.add("act", lambda e, oc=oc: e.activation(out=sq[:, oc, :n], in_=hr[s_][:, oc, :n], func=AF.Square),
                              reads=["h4_%d_%d" % (s_, oc)], writes=["sq5_%d_%d" % (s_, oc)])
                    hres = ["h4_%d" % s_] + ["h4_%d_%d" % (s_, c) for c in range(8)]
                    P.add("sp", lambda e: e.dma_start(out=dr["hB"].rearrange("(c p) t -> p c t", p=128)[:, :, off:off + n], in_=hr[s_][:, :, :n]),
                          reads=hres, dma="h4_%d" % s_)
                    emit_rstd("5", lambda c: sq[:, c, :n], 8, n, 6, 0, 0, rstd[:, :n], rtmp[:, :n], ["sq5_%d_%d" % (s_, c) for c in range(8)], "rstd5")
                    for c in range(8):
                        P.add("dve", lambda e, c=c: e.scalar_tensor_tensor(out=hno[s_][:, c, :n], in0=hr[s_][:, c, :n], scalar=gcol(3, c), in1=rstd[:, :n],
                                                                           op0=ALU.mult, op1=ALU.mult),
                              reads=["h4_%d_%d" % (s_, c), "rstd5", "vec"], writes=["hno5"])
                    P.add("sp", lambda e: e.dma_start(out=dr["hn"].rearrange("(c p) t -> p c t", p=128)[:, :, off:off + n], in_=hno[s_][:, :, :n]),
                          reads=["hno5"], dma="hno5")

                nt = len(tiles)
                load(0)
                load_h(0)
                if nt > 1:
                    load(1)
                    load_h(1)
                for c in range(8):
                    conv_chunk(0, c)
                for i in range(nt):
                    if i + 2 < nt:
                        load(i + 2)
                    stats(i)
                    for c in range(8):
                        if i + 1 < nt:
                            conv_chunk(i + 1, c)
                        if c >= 2:
                            ln_chunk(i, c - 2)
                    ln_chunk(i, 6)
                    ln_chunk(i, 7)
                    tail(i)
                    if i + 2 < nt:
                        load_h(i + 2)
                P.barrier()

        g1 = [p for p in ("p1", "p2", "p3", "m0a", "m0b", "p4a") if p in phases]
        if g1:
            with ExitStack() as wst:
                wg[0] = sb(wst, "wgA0", [128, 16384], BF16)
                wg[1] = sb(wst, "wgA1", [128, 16384], BF16)
                if "p1" in phases:
                    phase1()
                if "p2" in phases:
                    phase2()
                if "p3" in phases:
                    phase3()
                if "m0a" in phases:
                    mlp_half("m0a", 0, 0, "hA", "hB", False)
                if "m0b" in phases:
                    mlp_half("m0b", 0, 1, "hB", "hA", False)
                if "p4a" in phases:
                    phase4a()
        if "p4b" in phases:
            phase4b()
        if "m1a" in phases or "m1b" in phases:
            with ExitStack() as wst:
                wg[0] = sb(wst, "wgB0", [128, 16384], BF16)
                wg[1] = sb(wst, "wgB1", [128, 16384], BF16)
                if "m1a" in phases:
                    mlp_half("m1a", 1, 0, "hB", "hA", False)
                if "m1b" in phases:
                    mlp_half("m1b", 1, 1, "hA", None, True)

        P.barrier()
        P.emit()
    return nc


def _rope_tables(pos):
    inv_freq = (np.float32(500000.0) ** (-np.arange(0, 16, 2, dtype=np.float32) / np.float32(16))).astype(np.float32)
    ang = pos[:, None].astype(np.float32) * inv_freq[None, :]
    cos = np.cos(ang).astype(np.float32).T
    sin = np.sin(ang).astype(np.float32).T
    C = np.ones((128, pos.shape[0]), np.float32)
    S = np.zeros((128, pos.shape[0]), np.float32)
    for m in range(2):
        b = m * 64
        C[b:b + 8] = cos
        C[b + 8:b + 16] = cos
        S[b:b + 8] = -sin
        S[b + 8:b + 16] = sin
    S2 = S.copy()
    for m in range(2):
        b = m * 64
        S2[b:b + 8] = S[b + 8:b + 16]
        S2[b + 8:b + 16] = S[b:b + 8]
    return C, S2


def _const_mats():
    ident = np.eye(128, dtype=np.float32)
    perm = np.eye(128, dtype=np.float32)
    for m in range(2):
        b = m * 64
        for d in range(8):
            perm[b + d, b + d] = 0
            perm[b + d + 8, b + d + 8] = 0
            perm[b + d, b + d + 8] = 1
            perm[b + d + 8, b + d] = 1
    tri = (np.arange(128)[None, :] >= np.arange(128)[:, None]).astype(np.float32)
    ones = np.ones((128, 128), np.float32)
    return np.concatenate([ident, perm, tri, tri, ones], axis=1)


def prepare_inputs(inp):
    x = np.asarray(inp["x"], np.float32)
    f = lambda k: np.asarray(inp[k], np.float32)

    def chunked(v):
        return np.ascontiguousarray(v.reshape(-1, 128).T)

    vec = np.zeros((128, VC_N), np.float32)
    gains = [f("mix_norm")[0], f("mlp_norm")[0], f("mix_norm")[1], f("mlp_norm")[1], f("final_norm")]
    for i, g in enumerate(gains):
        vec[:, VC_GAIN + 8 * i:VC_GAIN + 8 * i + 8] = chunked(g)
    vec[:, VC_PSCALE:VC_PSCALE + 4] = chunked(f("pool_scale")[0])
    vec[:, VC_PW1B:VC_PW1B + 16] = chunked(f("conv_pw1_b")[0])
    vec[:, VC_DWB:VC_DWB + 8] = chunked(f("conv_dw_b")[0])
    vec[:, VC_LNG:VC_LNG + 8] = chunked(f("conv_ln_g")[0])
    vec[:, VC_LNB:VC_LNB + 8] = chunked(f("conv_ln_b")[0])
    vec[:, VC_PW2B:VC_PW2B + 8] = chunked(f("conv_pw2_b")[0])
    vec[:, VC_SUBLN] = f("subln")[0]
    dww = f("conv_dw_w")[0]
    for t in range(CONV_K):
        vec[:, VC_DWW + t * 8:VC_DWW + t * 8 + 8] = chunked(dww[t])
    lamv = np.concatenate([f("lam_q1")[0], f("lam_k1")[0], f("lam_q2")[0], f("lam_k2")[0]])[None, :]
    lamv = np.ascontiguousarray(np.broadcast_to(lamv, (128, 256)))
    cmat = _const_mats()
    wst = np.zeros((128, 32, 8), np.float32)
    for jj in range(4):
        for tg in range(8):
            k = 4 * tg + jj
            if k < CONV_K:
                wst[32 * jj:32 * jj + 32, :, tg] = dww[k].reshape(32, 32).T
    i4 = np.concatenate([np.eye(32, dtype=np.float32)] * 4, axis=0)
    selm = np.zeros((64, 2, 128), np.float32)
    selm[0:32, 0, :] = 1.0 / 32
    selm[32:64, 1, :] = 1.0 / 32
    shared = {
        "vec": vec, "lamv": lamv, "cmat": cmat, "selm": np.ascontiguousarray(selm.reshape(64, 256)),
        "wst": np.ascontiguousarray(wst.reshape(128, 256)), "i4": i4,
        "w_in": np.ascontiguousarray(f("w_in")[0]), "pool_w": np.ascontiguousarray(f("pool_w")[0]),
        "w_out": np.ascontiguousarray(f("w_out")[0]), "w_up": f("w_up"), "w_down": f("w_down"),
        "pw1": np.ascontiguousarray(f("conv_pw1_w")[0]), "pw2": np.ascontiguousarray(f("conv_pw2_w")[0]),
    }
    in_maps = []
    for c in range(NCORES):
        b, half = divmod(c, 2)
        xb = x[b]
        if half == 0:
            xl = np.zeros((NLOC, D), np.float32)
            xl[HALO:] = xb[:HALF]
            xp = np.zeros((NPRE, D), np.float32)
            pos = np.arange(NKEY, dtype=np.float32) - np.float32(HALF)
            kb = np.zeros((128, NKEY // 128), np.float32)
            kb[:, :NPREB + 1] = NEG
            hv = 0.0
            pfix = np.zeros((128, 4, 16), np.float32)
            for g, w in enumerate(POOL_WINDOWS):
                t = np.arange(16)
                pfix[:, g, :] = (1.0 / np.minimum(t + 1, w) - 1.0 / w).astype(np.float32)[None, :]
        else:
            xl = xb[HALF - HALO:]
            xp = xb[:NPRE]
            pos = np.arange(NKEY, dtype=np.float32)
            kb = np.zeros((128, NKEY // 128), np.float32)
            hv = 1.0
            pfix = np.zeros((128, 4, 16), np.float32)
        C, S = _rope_tables(pos)
        flags = np.zeros((128, 4), np.float32)
        flags[:, 0] = hv
        m = dict(shared)
        m.update({
            "xT_loc": np.ascontiguousarray(xl.T), "xT_pre": np.ascontiguousarray(xp.T),
            "ropeC": C, "ropeS": S, "kbias": kb, "flags": flags,
            "pfix": np.ascontiguousarray(pfix.reshape(128, 64)),
        })
        in_maps.append(m)
    return in_maps


def kernel(**inputs):
    in_maps = prepare_inputs(inputs)
    nc = build_program()
    res = run_bass_kernel_spmd(nc, in_maps, core_ids=list(range(NCORES)))
    out = np.empty((4, SEQ, D), np.float32)
    for c in range(NCORES):
        b, half = divmod(c, 2)
        out[b, half * HALF:(half + 1) * HALF, :] = res.results[c]["yT"].T
    return out
```

```python
import math
from contextlib import ExitStack

import numpy as np
import concourse.bass as bass
import concourse.mybir as mybir
from concourse.bass_utils import run_bass_kernel_spmd

F32 = mybir.dt.float32
BF16 = mybir.dt.bfloat16
AF = mybir.ActivationFunctionType
ALU = mybir.AluOpType

D = 1024
SEQ = 8192
NCORES = 8
HALF = 4096
HALO = 128
NLOC = HALF + HALO
NPRE = HALF - HALO
NKEY = NPRE + NLOC
NPREB = NPRE // 128
DFF = 4096
RMS_EPS = 1e-6
LN_EPS = 1e-5
SUBLN_EPS = 1e-5
POOL_WINDOWS = (2, 4, 8, 16)
CONV_K = 31
NEG = -30000.0

VC_GAIN = 0
VC_PSCALE = 40
VC_PW1B = 44
VC_DWB = 60
VC_LNG = 68
VC_LNB = 76
VC_PW2B = 84
VC_SUBLN = 92
VC_DWW = 93
VC_N = VC_DWW + CONV_K * 8

LOC_TILES = [(0, HALO)] + [(HALO + 512 * i, 512) for i in range(8)]
PRE_TILES = [(512 * i, 512) for i in range(7)] + [(3584, 384)]


class Res:
    __slots__ = ("name", "last_w", "readers", "dma_sem", "dma_cnt")

    def __init__(self, name):
        self.name = name
        self.last_w = None
        self.readers = []
        self.dma_sem = None
        self.dma_cnt = 0


class Op:
    __slots__ = ("eng", "fn", "deps", "is_dma", "sem", "val", "observed")

    def __init__(self, eng, fn, is_dma):
        self.eng = eng
        self.fn = fn
        self.deps = []
        self.is_dma = is_dma
        self.sem = None
        self.val = 0
        self.observed = False


ENGS = ("pe", "act", "dve", "pool", "sp")


class WV:
    def __init__(self, t, width):
        self.t = t
        self.w = width

    def __getitem__(self, idx):
        _, c, sl = idx
        return self.t[:, c * self.w + sl.start:c * self.w + sl.stop]

    def chunk(self, c):
        return self.t[:, c * self.w:(c + 1) * self.w]


class Prog:
    def __init__(self, nc, stack):
        self.nc = nc
        self.stack = stack
        self.ops = {e: [] for e in ENGS}
        self.all_ops = []
        self.res = {}
        self.eng_sem = {e: stack.enter_context(nc.semaphore("sem_" + e)) for e in ("pe", "act", "dve", "pool")}
        self.n_dma_sems = 0

    def R(self, name):
        r = self.res.get(name)
        if r is None:
            r = self.res[name] = Res(name)
        return r

    def _rl(self, xs):
        out = []
        for x in xs:
            out.append(self.R(x) if isinstance(x, str) else x)
        return out

    def add(self, eng, fn, reads=(), writes=(), dma=None):
        op = Op(eng, fn, dma is not None)
        reads = self._rl(reads)
        writes = self._rl(writes)
        psr = [r for r in reads if r.name.startswith("ps") and r.name[2:].isdigit()]
        if psr:
            reads = [r for r in reads if r not in psr]
            writes = writes + [r for r in psr if r not in writes]
        deps = []
        for r in reads:
            if r.last_w is not None:
                deps.append(r.last_w)
        for w in writes:
            if w.last_w is not None:
                deps.append(w.last_w)
            deps.extend(w.readers)
        seen = set()
        for d in deps:
            if d is op or id(d) in seen:
                continue
            seen.add(id(d))
            if eng == "pe" and d.eng == "pe" and not d.is_dma:
                continue
            op.deps.append(d)
            d.observed = True
        for r in reads:
            r.readers.append(op)
        for w in writes:
            w.last_w = op
            w.readers = []
        if dma is not None:
            sr = self.R(dma)
            if sr.dma_sem is None:
                sr.dma_sem = self.stack.enter_context(self.nc.semaphore("dsem%d" % self.n_dma_sems))
                self.n_dma_sems += 1
            sr.dma_cnt += 16
            op.sem = sr.dma_sem
            op.val = sr.dma_cnt
        self.ops[eng].append(op)
        self.all_ops.append(op)
        return op

    def barrier(self):
        lasts = []
        for e in ("pe", "act", "dve", "pool"):
            if self.ops[e]:
                lasts.append(self.ops[e][-1])
        dmas = {}
        for op in self.all_ops:
            if op.is_dma:
                dmas[id(op.sem)] = op
        lasts.extend(dmas.values())
        bops = []
        for e in ENGS:
            op = Op(e, None, False)
            for d in lasts:
                op.deps.append(d)
                d.observed = True
            self.ops[e].append(op)
            self.all_ops.append(op)
            bops.append(op)
        for r in self.res.values():
            r.last_w = None
            r.readers = []
        return bops

    def emit(self):
        nc = self.nc
        for e in ("pe", "act", "dve", "pool"):
            cnt = 0
            for op in self.ops[e]:
                if op.is_dma or op.fn is None:
                    continue
                if op.observed:
                    cnt += 1
                    op.sem = self.eng_sem[e]
                    op.val = cnt
        handles = {"pe": "tensor", "act": "scalar", "dve": "vector", "pool": "gpsimd", "sp": "sync"}
        with nc.Block() as block:
            for e in ENGS:
                ops = self.ops[e]

                def body(eng, ops=ops):
                    waited = {}
                    for op in ops:
                        need = {}
                        for d in op.deps:
                            if d.sem is None:
                                continue
                            k = id(d.sem)
                            if k not in need or need[k][1] < d.val:
                                need[k] = (d.sem, d.val)
                        for k, (sem, val) in need.items():
                            if waited.get(k, 0) < val:
                                eng.wait_ge(sem, val)
                                waited[k] = val
                        if op.fn is None:
                            continue
                        ins = op.fn(eng)
                        if op.is_dma:
                            ins.then_inc(op.sem, 16)
                        elif op.observed:
                            ins.then_inc(op.sem, 1)

                getattr(block, handles[e])(body)


ALL_PHASES = ("p1", "p2", "p3", "m0a", "m0b", "p4a", "p4b", "m1a", "m1b")

SCRATCH = {
    "KT": ([4, 128, NKEY], BF16),
    "V4": ([4, 128, NKEY // 128, 128], BF16),
    "QT": ([4, 128, NLOC], BF16),
    "catT": ([D, NLOC], BF16),
    "catB": ([D // 2, NLOC], BF16),
    "hA": ([D, NLOC], F32),
    "hB": ([D, NLOC], F32),
    "hn": ([D, NLOC], BF16),
    "gT": ([D, NLOC], BF16),
}
PHASE_IO = {
    "p1": ((), ("KT", "V4", "QT", "catT")),
    "p2": (("KT", "V4", "QT"), ("catB",)),
    "p3": (("catT", "catB"), ("hA", "hn")),
    "m0a": (("hA", "hn"), ("hB",)),
    "m0b": (("hB", "hn"), ("hA",)),
    "p4a": (("hA",), ("gT",)),
    "p4b": (("hA", "gT"), ("hB", "hn")),
    "m1a": (("hB", "hn"), ("hA",)),
    "m1b": (("hA", "hn"), ()),
}


def build_program(phases=ALL_PHASES, dump=(), dbg=None):
    dbg = dbg or {}
    nc = bass.Bass("TRN2", target_bir_lowering=False)
    dr = {}

    def din(name, shape, dt=F32):
        dr[name] = nc.dram_tensor(name, list(shape), dt, kind="ExternalInput").ap()

    din("xT_loc", [D, NLOC])
    din("xT_pre", [D, NPRE])
    din("ropeC", [128, NKEY])
    din("ropeS", [128, NKEY])
    din("kbias", [128, NKEY // 128])
    din("flags", [128, 4])
    din("pfix", [128, 4 * 16])
    din("vec", [128, VC_N])
    din("lamv", [128, 4 * 64])
    din("cmat", [128, 5 * 128])
    din("selm", [64, 256])
    if "p1" in phases:
        din("w_in", [D, 2048])
        din("pool_w", [4, 128, 128])
    if "p3" in phases:
        din("w_out", [D, D])
    if any(p in phases for p in ("m0a", "m0b", "m1a", "m1b")):
        din("w_up", [2, D, DFF])
        din("w_down", [2, DFF, D])
    if "p4a" in phases:
        din("pw1", [D, 2048])
    if "p4b" in phases:
        din("pw2", [D, D])
        din("wst", [128, 32 * 8])
        din("i4", [128, 32])

    produced = set()
    consumed_ext = set()
    for ph in ALL_PHASES:
        if ph not in phases:
            continue
        ins_, outs_ = PHASE_IO[ph]
        for t in ins_:
            if t not in produced:
                consumed_ext.add(t)
        produced.update(outs_)
    for name, (shape, dt) in SCRATCH.items():
        if name in consumed_ext:
            kind = "ExternalInput"
        elif name in dump and name in produced:
            kind = "ExternalOutput"
        else:
            kind = "Internal"
        dr[name] = nc.dram_tensor(name, list(shape), dt, kind=kind).ap()
    if "m1b" in phases:
        dr["yT"] = nc.dram_tensor("yT", [D, HALF], F32, kind="ExternalOutput").ap()

    with ExitStack() as stack:
        P = Prog(nc, stack)

        uniq = [0]

        def sb(st, name, shape, dt):
            uniq[0] += 1
            return st.enter_context(nc.sbuf_tensor("%s_u%d" % (name, uniq[0]), list(shape), dt))

        psum = stack.enter_context(nc.psum_tensor("psum", [128, 8, 512], F32))
        wg = [None, None]

        def wl_rows(name, b, width, row_sel):
            def f():
                t = wg[b]
                nblk = width // 512
                order = WBLK_ORDER.get(name, list(range(nblk)))
                for blk in order:
                    for c in range(8):
                        P.add("pool", lambda g, c=c, t=t, blk=blk: g.dma_start(
                            out=t[:, c * width + blk * 512:c * width + (blk + 1) * 512], in_=row_sel(c)[:, blk * 512:(blk + 1) * 512]),
                            writes=["wg%d_b%d" % (b, blk)], dma="wg%d_b%d" % (b, blk))
            return f

        WBLK_ORDER = {"p1": [2, 3, 1, 0]}

        wplan = []
        if "p1" in phases:
            wplan.append(("p1", 2048, lambda c: dr["w_in"][c * 128:(c + 1) * 128, :]))
        if "p3" in phases:
            wplan.append(("p3", 1024, lambda c: dr["w_out"][c * 128:(c + 1) * 128, :]))
        for nm, l, hf in (("m0a", 0, 0), ("m0b", 0, 1)):
            if nm in phases:
                wplan.append((nm, 2048, lambda c, l=l, hf=hf: dr["w_up"][l, c * 128:(c + 1) * 128, hf * 2048:(hf + 1) * 2048]))
        if "p4a" in phases:
            wplan.append(("p4a", 2048, lambda c: dr["pw1"][c * 128:(c + 1) * 128, :]))
        for nm, l, hf in (("m1a", 1, 0), ("m1b", 1, 1)):
            if nm in phases:
                wplan.append((nm, 2048, lambda c, l=l, hf=hf: dr["w_up"][l, c * 128:(c + 1) * 128, hf * 2048:(hf + 1) * 2048]))
        wslot = {nm: i % 2 for i, (nm, _, _) in enumerate(wplan)}
        wloaders = {nm: wl_rows(nm, i % 2, width, sel) for i, (nm, width, sel) in enumerate(wplan)}
        GROUP2 = ("m1a", "m1b")
        wnext = {wplan[i][0]: wplan[i + 1][0] for i in range(len(wplan) - 1)
                 if (wplan[i][0] in GROUP2) == (wplan[i + 1][0] in GROUP2)}
        wfirst = set()
        for grp in (False, True):
            names = [w[0] for w in wplan if (w[0] in GROUP2) == grp]
            if names:
                wfirst.add(names[0])

        def wstart(nm):
            if nm in wfirst:
                wloaders[nm]()

        def wprefetch(nm):
            if nm in wnext:
                wloaders[wnext[nm]]()
        cst = sb(stack, "cst", [128, 5, 128], BF16)
        selm = sb(stack, "selm_sb", [64, 2, 128], F32)
        cavg = sb(stack, "cavg", [128, 2, 128], BF16)
        vec = sb(stack, "vecs", [128, VC_N], F32)
        flags = sb(stack, "flags_sb", [128, 4], F32)
        kbias = sb(stack, "kbias_sb", [128, NKEY // 128], F32)
        lam_sb = sb(stack, "lam_sb", [128, 8], F32)
        lamv = sb(stack, "lamv_sb", [128, 4, 64], F32)
        lamt = sb(stack, "lamt_sb", [128, 2, 64], F32)
        subl = sb(stack, "subl_sb", [128, 1], F32)
        epsb = sb(stack, "epsb", [128, 4], F32)

        P.add("pool", lambda g: g.dma_start(out=cst[:], in_=dr["cmat"].rearrange("p (a b) -> p a b", a=5)),
              writes=["cst"], dma="cst")
        P.add("sp", lambda e: e.dma_start(out=vec[:], in_=dr["vec"]), writes=["vec"], dma="vec")
        P.add("sp", lambda e: e.dma_start(out=selm[:], in_=dr["selm"].rearrange("p (a b) -> p a b", a=2)), writes=["selm"], dma="selm")
        P.add("sp", lambda e: e.dma_start(out=flags[:], in_=dr["flags"]), writes=["flags"], dma="flags")
        P.add("sp", lambda e: e.dma_start(out=kbias[:], in_=dr["kbias"]), writes=["kbias"], dma="kbias")
        P.add("sp", lambda e: e.dma_start(out=lamv[:], in_=dr["lamv"].rearrange("p (a b) -> p a b", a=4)),
              writes=["lamv"], dma="lamv")
        P.add("dve", lambda e: e.memset(cavg[:, 0, :], 1.0 / D), writes=["cavg"])
        P.add("dve", lambda e: e.memset(cavg[:, 1, :], 1.0 / 128), writes=["cavg"])
        P.add("dve", lambda e: e.memset(epsb[:, 0:1], RMS_EPS), writes=["epsb"])
        P.add("dve", lambda e: e.memset(epsb[:, 1:2], LN_EPS), writes=["epsb"])
        P.add("dve", lambda e: e.memset(epsb[:, 2:3], SUBLN_EPS), writes=["epsb"])
        lambda_init = 0.8 - 0.6 * math.exp(-0.3 * 0)
        P.add("dve", lambda e: e.tensor_tensor(out=lamt[:, 0, :], in0=lamv[:, 0, :], in1=lamv[:, 1, :], op=ALU.mult),
              reads=["lamv"], writes=["lamt"])
        P.add("dve", lambda e: e.tensor_tensor(out=lamt[:, 1, :], in0=lamv[:, 2, :], in1=lamv[:, 3, :], op=ALU.mult),
              reads=["lamv"], writes=["lamt"])
        P.add("dve", lambda e: e.reduce_sum(out=lam_sb[:, 2:3], in_=lamt[:, 0, :], axis=mybir.AxisListType.X),
              reads=["lamt"], writes=["lam_a"])
        P.add("dve", lambda e: e.reduce_sum(out=lam_sb[:, 3:4], in_=lamt[:, 1, :], axis=mybir.AxisListType.X),
              reads=["lamt"], writes=["lam_b"])
        P.add("act", lambda e: e.activation(out=lam_sb[:, 4:6], in_=lam_sb[:, 2:4], func=AF.Exp),
              reads=["lam_a", "lam_b"], writes=["lam_c"])
        P.add("dve", lambda e: e.scalar_tensor_tensor(out=lam_sb[:, 0:1], in0=lam_sb[:, 5:6], scalar=-lambda_init,
                                                      in1=lam_sb[:, 4:5], op0=ALU.add, op1=ALU.subtract),
              reads=["lam_c"], writes=["lam"])
        P.add("dve", lambda e: e.tensor_scalar(out=subl[:], in0=vec[:, VC_SUBLN:VC_SUBLN + 1], scalar1=1.0 - lambda_init,
                                               scalar2=None, op0=ALU.mult),
              reads=["vec"], writes=["subl"])
        IDENT, PERM, TRI, ONES = 0, 1, 2, 4

        def gcol(norm_idx, c):
            j = VC_GAIN + norm_idx * 8 + c
            return vec[:, j:j + 1]

        def vcol(base, c):
            return vec[:, base + c:base + c + 1]

        def emit_rstd(tag, sq_ap_fn, nch, n, ps_bank, avg_idx, eps_col, rstd_ap, tmp_ap, sq_res, out_res):
            if sq_res is None:
                sq_res = ["sq%d" % c for c in range(nch)]

            def mm(e):
                ins = None
                for c in range(nch):
                    ins = e.matmul(psum[:, ps_bank, :n], cavg[:, avg_idx, :], sq_ap_fn(c), start=(c == 0), stop=(c == nch - 1))
                return ins
            P.add("pe", mm, reads=list(sq_res) + ["cavg"], writes=["ps%d" % ps_bank])
            P.add("act", lambda e: e.activation(out=tmp_ap, in_=psum[:, ps_bank, :n], func=AF.Ln,
                                                bias=epsb[:, eps_col:eps_col + 1], scale=1.0),
                  reads=["ps%d" % ps_bank, "epsb"], writes=[out_res + "_t"])
            P.add("act", lambda e: e.activation(out=rstd_ap, in_=tmp_ap, func=AF.Exp, scale=-0.5),
                  reads=[out_res + "_t"], writes=[out_res])

        def phase1():
            with ExitStack() as st:
                win = WV(wg[wslot["p1"]], 2048)
                wres_ = "wg%d" % wslot["p1"]
                poolw = sb(st, "poolw", [128, 4, 128], BF16)
                pfix = sb(st, "pfix_sb", [128, 4, 16], F32)
                xt = [sb(st, "xt%d" % i, [128, 8, 512], F32) for i in range(2)]
                cs = [sb(st, "cs%d" % i, [128, 2, 512], F32) for i in range(2)]
                sq = sb(st, "sq", [128, 8, 512], BF16)
                rstd = sb(st, "rstd", [128, 512], F32)
                rtmp = sb(st, "rtmp", [128, 512], F32)
                hnd = [sb(st, "hn_sb%d" % i, [128, 8, 512], BF16) for i in range(2)]
                ubuf = [sb(st, "ubuf%d" % i, [128, 4, 16 + 512], F32) for i in range(2)]
                wk = [sb(st, "wk%d" % i, [128, 16 + 512], F32) for i in range(2)]
                pooled = sb(st, "pooled", [128, 4, 512], BF16)
                ptmp = sb(st, "ptmp", [128, 2, 16], F32)
                qkb = [sb(st, "qkb%d" % i, [128, 512], BF16) for i in range(2)]
                t1 = [sb(st, "t1_%d" % i, [128, 512], F32) for i in range(2)]
                t2 = [sb(st, "t2_%d" % i, [128, 512], F32) for i in range(2)]
                qrot = [sb(st, "qrot%d" % i, [128, 4, 512], BF16) for i in range(2)]
                krot = [sb(st, "krot%d" % i, [128, 4, 512], BF16) for i in range(2)]
                vbuf = sb(st, "vbuf", [128, 4, 512], BF16)
                cata = sb(st, "cata", [128, 4, 512], BF16)

                wstart("p1")
                P.add("pool", lambda g: g.dma_start(out=poolw[:], in_=dr["pool_w"].rearrange("g c d -> c g d")),
                      writes=["poolw"], dma="poolw")
                wprefetch("p1")
                P.add("sp", lambda e: e.dma_start(out=pfix[:], in_=dr["pfix"].rearrange("p (a b) -> p a b", a=4)),
                      writes=["pfix"], dma="pfix")
                P.add("dve", lambda e: e.memset(ubuf[0][:, :, 0:16], 0.0), writes=["ubuf0"])

                tiles = [("pre", o, n) for (o, n) in PRE_TILES] + [("loc", o, n) for (o, n) in LOC_TILES]
                if "p1_tiles" in dbg:
                    tiles = [tiles[j] for j in dbg["p1_tiles"]]
                loc_index = {}
                for i, (kind, off, n) in enumerate(tiles):
                    if kind == "loc":
                        loc_index[i] = len(loc_index)

                def load_x(i):
                    kind, off, n = tiles[i]
                    s = i % 2
                    src = dr["xT_pre"] if kind == "pre" else dr["xT_loc"]
                    P.add("sp", lambda e: e.dma_start(out=xt[s][:, :, :n], in_=src.rearrange("(c p) t -> p c t", p=128)[:, :, off:off + n]),
                          writes=["xt%d" % s], dma="xt%d" % s)

                def load_cs(i):
                    kind, off, n = tiles[i]
                    s = i % 2
                    koff = off if kind == "pre" else NPRE + off
                    P.add("sp", lambda e: e.dma_start(out=cs[s][:, 0, :n], in_=dr["ropeC"][:, koff:koff + n]),
                          writes=["cs%d" % s], dma="cs%d" % s)
                    P.add("sp", lambda e: e.dma_start(out=cs[s][:, 1, :n], in_=dr["ropeS"][:, koff:koff + n]),
                          writes=["cs%d" % s], dma="cs%d" % s)

                rr = [0]

                def stage_a(i):
                    kind, off, n = tiles[i]
                    s = i % 2
                    xs = xt[s]
                    xres = "xt%d" % s
                    hn = hnd[s]
                    for c in range(8):
                        P.add("act", lambda e, c=c: e.activation(out=sq[:, c, :n], in_=xs[:, c, :n], func=AF.Square),
                              reads=[xres], writes=["sq%d" % c])
                    emit_rstd("n", lambda c: sq[:, c, :n], 8, n, 0, 0, 0, rstd[:, :n], rtmp[:, :n], None, "rstd")
                    for c in range(8):
                        P.add("dve", lambda e, c=c: e.scalar_tensor_tensor(out=hn[:, c, :n], in0=xs[:, c, :n], scalar=gcol(0, c),
                                                                           in1=rstd[:, :n], op0=ALU.mult, op1=ALU.mult),
                              reads=[xres, "rstd", "vec"], writes=["hn%d_%d" % (s, c)])

                def stage_b(i):
                    kind, off, n = tiles[i]
                    s = i % 2
                    hn = hnd[s]
                    koff = off if kind == "pre" else NPRE + off
                    hn_res = ["hn%d_%d" % (s, c) for c in range(8)]
                    nb = n // 128
                    is_loc = kind == "loc"

                    def proj_fm(ocol, bank):
                        def mm(e):
                            ins = None
                            for c in range(8):
                                ins = e.matmul(psum[:, bank, :n], win[:, c, ocol:ocol + 128], hn[:, c, :n], start=(c == 0), stop=(c == 7))
                            return ins
                        P.add("pe", mm, reads=hn_res + ["%s_b%d" % (wres_, ocol // 512)], writes=["ps%d" % bank])

                    def rope_chunk(ocol, dst_ap, dst_res, between=None):
                        k = rr[0]
                        rr[0] += 1
                        bank = 1 + (k % 2)
                        b2 = 3 + (k % 2)
                        u = k % 2
                        proj_fm(ocol, bank)
                        P.add("act", lambda e: e.activation(out=qkb[u][:, :n], in_=psum[:, bank, :n], func=AF.Identity),
                              reads=["ps%d" % bank], writes=["qkb%d" % u])
                        if between is not None:
                            between()
                        P.add("pe", lambda e: e.matmul(psum[:, b2, :n], cst[:, PERM, :], qkb[u][:, :n], start=True, stop=True),
                              reads=["qkb%d" % u, "cst"], writes=["ps%d" % b2])
                        P.add("dve", lambda e: e.tensor_tensor(out=t1[u][:, :n], in0=psum[:, bank, :n], in1=cs[s][:, 0, :n], op=ALU.mult),
                              reads=["ps%d" % bank, "cs%d" % s], writes=["t1_%d" % u])
                        P.add("dve", lambda e: e.tensor_tensor(out=t2[u][:, :n], in0=psum[:, b2, :n], in1=cs[s][:, 1, :n], op=ALU.mult),
                              reads=["ps%d" % b2, "cs%d" % s], writes=["t2_%d" % u])
                        P.add("pool", lambda e: e.tensor_tensor(out=dst_ap, in0=t1[u][:, :n], in1=t2[u][:, :n], op=ALU.add),
                              reads=["t1_%d" % u, "t2_%d" % u], writes=[dst_res])

                    def v_block(tb):
                        bank = 5 + (tb % 2)

                        def mmv(e):
                            ins = None
                            for c in range(8):
                                ins = e.matmul(psum[:, bank, :], hn[:, c, tb * 128:(tb + 1) * 128], win[:, c, 1536:2048], start=(c == 0), stop=(c == 7))
                            return ins
                        P.add("pe", mmv, reads=hn_res + ["%s_b3" % wres_], writes=["ps%d" % bank])
                        P.add("act", lambda e: e.activation(out=vbuf[:, tb, :], in_=psum[:, bank, :], func=AF.Identity),
                              reads=["ps%d" % bank], writes=["vbuf"])

                    if is_loc:
                        lt = loc_index[i]
                        us = lt % 2
                        ub = ubuf[us]
                        ures = "ubuf%d" % us

                    def u_chunk(g):
                        bank = 7
                        proj_fm(g * 128, bank)
                        P.add("act", lambda e: e.activation(out=ub[:, g, 16:16 + n], in_=psum[:, bank, :n], func=AF.Identity),
                              reads=["ps%d" % bank], writes=[ures + "_%d" % g])

                    for h in range(4):
                        rope_chunk(1024 + h * 128, krot[s][:, h, :n], "krot%d" % s,
                                   between=(lambda h=h: v_block(h)) if h < nb else None)
                        if is_loc:
                            rope_chunk(512 + h * 128, qrot[s][:, h, :n], "qrot%d" % s, between=lambda h=h: u_chunk(h))
                    for h in range(4):
                        P.add("sp", lambda e, h=h: e.dma_start(out=dr["KT"][h, :, koff:koff + n], in_=krot[s][:, h, :n]),
                              reads=["krot%d" % s], dma="krot%d" % s)
                    for h in range(4):
                        P.add("sp", lambda e, h=h: e.dma_start(out=dr["V4"][h, :, koff // 128:koff // 128 + nb, :],
                                                              in_=vbuf[:, 0:nb, h * 128:(h + 1) * 128]),
                              reads=["vbuf"], dma="vbuf")
                    if not is_loc:
                        return
                    for h in range(4):
                        P.add("sp", lambda e, h=h: e.dma_start(out=dr["QT"][h, :, off:off + n], in_=qrot[s][:, h, :n]),
                              reads=["qrot%d" % s], dma="qrot%d" % s)
                    for g, w in enumerate(POOL_WINDOWS):
                        ug = ures + "_%d" % g
                        lvl = 1
                        k = 0
                        src = None
                        srcres = None
                        while lvl < w:
                            lo = -(w - 2 * lvl)
                            dst = wk[k % 2]
                            dres = "wk%d" % (k % 2)
                            if src is None:
                                a0 = ub[:, g, 16 + lo:16 + n]
                                a1 = ub[:, g, 16 + lo - lvl:16 + n - lvl]
                                sres = [ug, ures]
                            else:
                                a0 = src[:, 16 + lo:16 + n]
                                a1 = src[:, 16 + lo - lvl:16 + n - lvl]
                                sres = [srcres]
                            P.add("pool", lambda e, a0=a0, a1=a1, dst=dst, lo=lo: e.tensor_tensor(out=dst[:, 16 + lo:16 + n], in0=a0, in1=a1, op=ALU.add),
                                  reads=sres, writes=[dres])
                            src, srcres = dst, dres
                            lvl *= 2
                            k += 1
                        P.add("dve", lambda e, g=g, w=w, src=src: e.scalar_tensor_tensor(out=pooled[:, g, :n], in0=src[:, 16:16 + n], scalar=1.0 / w,
                                                                                        in1=ub[:, g, 16:16 + n], op0=ALU.mult, op1=ALU.subtract),
                              reads=[srcres, ug], writes=["pooled%d" % g])
                        if lt == 1:
                            P.add("dve", lambda e, g=g, src=src: e.tensor_tensor(out=ptmp[:, 0, :], in0=src[:, 16:32], in1=pfix[:, g, :], op=ALU.mult),
                                  reads=[srcres, "pfix"], writes=["ptmp0"])
                            P.add("dve", lambda e, g=g, w=w, src=src: e.scalar_tensor_tensor(out=ptmp[:, 1, :], in0=src[:, 16:32], scalar=1.0 / w,
                                                                                            in1=ub[:, g, 16:32], op0=ALU.mult, op1=ALU.subtract),
                                  reads=[srcres, ug], writes=["ptmp1"])
                            P.add("dve", lambda e, g=g: e.tensor_tensor(out=pooled[:, g, 0:16], in0=ptmp[:, 0, :], in1=ptmp[:, 1, :], op=ALU.add),
                                  reads=["ptmp0", "ptmp1"], writes=["pooled%d" % g])
                        P.add("pe", lambda e, g=g: e.matmul(psum[:, 0, :n], poolw[:, g, :], pooled[:, g, :n], start=True, stop=True),
                              reads=["pooled%d" % g, "poolw"], writes=["ps0"])
                        P.add("act", lambda e, g=g: e.activation(out=cata[:, g, :n], in_=psum[:, 0, :n], func=AF.Identity, scale=vcol(VC_PSCALE, g)),
                              reads=["ps0", "vec"], writes=["cata"])
                    nub = ubuf[1 - us]
                    P.add("dve", lambda e: e.tensor_copy(out=nub[:, :, 0:16], in_=ub[:, :, n:n + 16]),
                          reads=[ures] + [ures + "_%d" % g for g in range(4)], writes=["ubuf%d" % (1 - us)])
                    P.add("sp", lambda e: e.dma_start(out=dr["catT"].rearrange("(g p) t -> p g t", p=128)[:, 0:4, off:off + n], in_=cata[:, :, :n]),
                          reads=["cata"], dma="cata")

                nt = len(tiles)
                load_x(0)
                load_cs(0)
                if nt > 1:
                    load_x(1)
                stage_a(0)
                for i in range(nt):
                    if i + 1 < nt:
                        stage_a(i + 1)
                        load_cs(i + 1)
                    if i + 2 < nt:
                        load_x(i + 2)
                    stage_b(i)
                P.barrier()

        def phase2():
            with ExitStack() as st:
                kt = [sb(st, "kt%d" % i, [128, NKEY], BF16) for i in range(2)]
                v4 = [sb(st, "v4_%d" % i, [128, NKEY // 128, 128], BF16) for i in range(2)]
                qt = [sb(st, "qt%d" % i, [128, 512], BF16) for i in range(2)]
                pT = [sb(st, "pT%d" % i, [128, 2, 512], BF16) for i in range(2)]
                lsb = sb(st, "lsb", [64, 512], F32)
                oo = [sb(st, "oo%d" % m, [128, 512], F32) for m in range(2)]
                osq = sb(st, "osq", [128, 512], BF16)
                orstd = sb(st, "orstd", [128, 512], F32)
                otmp = sb(st, "otmp", [128, 512], F32)
                on = [sb(st, "on%d" % i, [128, 512], BF16) for i in range(2)]
                nlam = sb(st, "nlam", [64, 1], F32)
                scale = 64 ** -0.5
                P.add("dve", lambda e: e.memset(nlam[0:32, :], 1.0), writes=["nlam"])
                P.add("dve", lambda e: e.tensor_copy(out=nlam[32:64, :], in_=lam_sb[32:64, 0:1]), reads=["lam"], writes=["nlam"])

                def load_head(h):
                    hs = h % 2
                    P.add("sp", lambda e: e.dma_start(out=kt[hs][:], in_=dr["KT"][h]), writes=["kt%d" % hs], dma="kt%d" % hs)
                    P.add("sp", lambda e: e.dma_start(out=v4[hs][:], in_=dr["V4"][h]), writes=["v4_%d" % hs], dma="v4_%d" % hs)

                jobs = [(h, t) for h in range(dbg.get("p2_heads", 4)) for t in range(len(LOC_TILES))]
                if "p2_jobs" in dbg:
                    jobs = [tuple(j) for j in dbg["p2_jobs"]]

                def load_q(j):
                    h, t = jobs[j]
                    off, n = LOC_TILES[t]
                    qs = j % 2
                    P.add("sp", lambda e: e.dma_start(out=qt[qs][:, :n], in_=dr["QT"][h, :, off:off + n]),
                          writes=["qt%d" % qs], dma="qt%d" % qs)

                def job(j, pending):
                    h, t = jobs[j]
                    off, n = LOC_TILES[t]
                    hs = h % 2
                    qs = j % 2
                    qb0 = off // 128
                    nb = n // 128
                    nkb = NPREB + qb0 + nb
                    ktr, v4r, qtr = "kt%d" % hs, "v4_%d" % hs, "qt%d" % qs

                    def c0_of(kb):
                        jj = kb - (NPREB + qb0)
                        return 0 if jj < 0 else jj * 128

                    def qk(kb):
                        c0 = c0_of(kb)
                        for m in range(2):
                            bank = 2 * (kb % 2) + m
                            P.add("pe", lambda e, m=m, bank=bank: e.matmul(psum[:, bank, c0:n], kt[hs][m * 64:(m + 1) * 64, kb * 128:(kb + 1) * 128],
                                                                           qt[qs][m * 64:(m + 1) * 64, c0:n], start=True, stop=True),
                                  reads=[ktr, qtr], writes=["ps%d" % bank])

                    def ex(kb):
                        c0 = c0_of(kb)
                        diag = kb >= NPREB + qb0
                        b0 = 2 * (kb % 2)
                        pt = pT[kb % 2]
                        pr = "pT%d" % (kb % 2)
                        bias = kbias[:, kb:kb + 1] if kb <= NPREB else 0.0
                        P.add("act", lambda e: e.activation(out=pt[:, :, c0:n], in_=psum[:, b0:b0 + 2, c0:n], func=AF.Exp, bias=bias, scale=scale),
                              reads=["ps%d" % b0, "ps%d" % (b0 + 1), "kbias"], writes=[pr])
                        if diag:
                            P.add("dve", lambda e: e.tensor_tensor(out=pt[:, :, c0:c0 + 128], in0=pt[:, :, c0:c0 + 128], in1=cst[:, TRI:TRI + 2, :], op=ALU.mult),
                                  reads=["cst"], writes=[pr])

                    def pv(kb):
                        c0 = c0_of(kb)
                        first = kb == 0
                        last = kb == nkb - 1
                        pt = pT[kb % 2]
                        pr = "pT%d" % (kb % 2)

                        def mm(e):
                            for m in range(2):
                                e.matmul(psum[:, 4 + m, c0:n], v4[hs][:, kb, :], pt[:, m, c0:n], start=first, stop=last, skip_group_check=True)
                            ins = None
                            for m in range(2):
                                ins = e.matmul(psum[32 * m:32 * m + 32, 6, c0:n], cst[:, ONES, 0:32], pt[:, m, c0:n], start=first, stop=last,
                                               skip_group_check=True, tile_position=(0, 32 * m))
                            return ins
                        P.add("pe", mm, reads=[pr, v4r, "cst"], writes=["ps4", "ps5", "ps6"])

                    qk(0)
                    qk(1)
                    for kb in range(nkb):
                        ex(kb)
                        if kb + 2 < nkb:
                            qk(kb + 2)
                        pv(kb)
                        if pending and kb >= 3 and kb % 3 == 0:
                            pending.pop(0)()
                    while pending:
                        pending.pop(0)()
                    P.add("dve", lambda e: e.tensor_scalar(out=lsb[:, :n], in0=psum[0:64, 6, :n], scalar1=1e-30, scalar2=None, op0=ALU.max),
                          reads=["ps6"], writes=["lsb"])
                    for m in range(2):
                        P.add("dve", lambda e, m=m: e.tensor_copy(out=oo[m][:, :n], in_=psum[:, 4 + m, :n]),
                              reads=["ps%d" % (4 + m)], writes=["oo%d" % m])
                    os_ = j % 2

                    def st0():
                        P.add("dve", lambda e: e.reciprocal(out=lsb[:, :n], in_=lsb[:, :n]), writes=["lsb"])
                        P.add("dve", lambda e: e.tensor_scalar(out=lsb[:, :n], in0=lsb[:, :n], scalar1=nlam[:, 0:1], scalar2=None, op0=ALU.mult),
                              reads=["nlam"], writes=["lsb"])
                        P.add("pe", lambda e: e.matmul(psum[:, 7, :n], selm[:, 0, :], lsb[:, :n], start=True, stop=True),
                              reads=["lsb", "selm"], writes=["ps7"])

                    def st1():
                        P.add("dve", lambda e: e.tensor_tensor(out=oo[0][:, :n], in0=oo[0][:, :n], in1=psum[:, 7, :n], op=ALU.mult),
                              reads=["ps7"], writes=["oo0"])
                        P.add("pe", lambda e: e.matmul(psum[:, 7, :n], selm[:, 1, :], lsb[:, :n], start=True, stop=True),
                              reads=["lsb", "selm"], writes=["ps7"])

                    def st2():
                        P.add("dve", lambda e: e.tensor_tensor(out=oo[1][:, :n], in0=oo[1][:, :n], in1=psum[:, 7, :n], op=ALU.mult),
                              reads=["ps7"], writes=["oo1"])
                        P.add("dve", lambda e: e.tensor_tensor(out=oo[0][:, :n], in0=oo[0][:, :n], in1=oo[1][:, :n], op=ALU.add),
                              reads=["oo1"], writes=["oo0"])
                        P.add("dve", lambda e: e.tensor_tensor(out=osq[:, :n], in0=oo[0][:, :n], in1=oo[0][:, :n], op=ALU.mult), reads=["oo0"], writes=["osq"])

                        def mm(e):
                            return e.matmul(psum[:, 7, :n], cavg[:, 1, :], osq[:, :n], start=True, stop=True)
                        P.add("pe", mm, reads=["osq", "cavg"], writes=["ps7"])

                    def st3():
                        P.add("act", lambda e: e.activation(out=otmp[:, :n], in_=psum[:, 7, :n], func=AF.Ln, bias=epsb[:, 2:3], scale=1.0),
                              reads=["ps7", "epsb"], writes=["otmp"])
                        P.add("act", lambda e: e.activation(out=orstd[:, :n], in_=otmp[:, :n], func=AF.Exp, scale=-0.5),
                              reads=["otmp"], writes=["orstd"])

                    def st4():
                        P.add("dve", lambda e: e.scalar_tensor_tensor(out=on[os_][:, :n], in0=oo[0][:, :n], scalar=subl[:, 0:1], in1=orstd[:, :n],
                                                                      op0=ALU.mult, op1=ALU.mult),
                              reads=["oo0", "orstd", "subl"], writes=["on%d" % os_])
                        P.add("sp", lambda e: e.dma_start(out=dr["catB"][h * 128:(h + 1) * 128, off:off + n], in_=on[os_][:, :n]),
                              reads=["on%d" % os_], dma="on%d" % os_)
                    return [st0, st1, st2, st3, st4]

                loaded = set()
                pend = []
                load_q(0)
                for j in range(len(jobs)):
                    h = jobs[j][0]
                    if h not in loaded:
                        load_head(h)
                        loaded.add(h)
                    if j + 1 < len(jobs):
                        load_q(j + 1)
                    if j == 0 or jobs[j - 1][0] != h:
                        nxt = [jj[0] for jj in jobs[j:] if jj[0] != h]
                        if nxt and nxt[0] not in loaded:
                            load_head(nxt[0])
                            loaded.add(nxt[0])
                    pend = job(j, pend)
                while pend:
                    pend.pop(0)()
                P.barrier()

        def phase3():
            with ExitStack() as st:
                wout = WV(wg[wslot["p3"]], 1024)
                NS = 3
                cat = [sb(st, "cat%d" % i, [128, 8, 512], BF16) for i in range(NS)]
                xr = [sb(st, "xr%d" % i, [128, 8, 512], F32) for i in range(NS)]
                sqd = [sb(st, "sq3_%d" % i, [128, 8, 512], BF16) for i in range(2)]
                rstd = sb(st, "rstd3", [128, 512], F32)
                rtmp = sb(st, "rtmp3", [128, 512], F32)
                hno = [sb(st, "hno%d" % i, [128, 8, 512], BF16) for i in range(2)]
                wstart("p3")
                wprefetch("p3")
                tiles = LOC_TILES

                def load(i):
                    off, n = tiles[i]
                    s_ = i % NS
                    P.add("sp", lambda e: e.dma_start(out=cat[s_][:, 0:4, :n], in_=dr["catT"].rearrange("(c p) t -> p c t", p=128)[:, 0:4, off:off + n]),
                          writes=["cat%d" % s_], dma="cat%d" % s_)
                    P.add("sp", lambda e: e.dma_start(out=cat[s_][:, 4:8, :n], in_=dr["catB"].rearrange("(c p) t -> p c t", p=128)[:, :, off:off + n]),
                          writes=["cat%d" % s_], dma="cat%d" % s_)
                    P.add("sp", lambda e: e.dma_start(out=xr[s_][:, :, :n], in_=dr["xT_loc"].rearrange("(c p) t -> p c t", p=128)[:, :, off:off + n]),
                          writes=["xr%d" % s_] + ["xr%d_%d" % (s_, c) for c in range(8)], dma="xr%d" % s_)

                def s1(i):
                    off, n = tiles[i]
                    s_ = i % NS
                    sq = sqd[i % 2]
                    for oc in range(8):
                        bank = oc % 2

                        def mm(e, oc=oc, bank=bank):
                            ins = None
                            for c in range(8):
                                ins = e.matmul(psum[:, bank, :n], wout[:, c, oc * 128:(oc + 1) * 128], cat[s_][:, c, :n], start=(c == 0), stop=(c == 7))
                            return ins
                        P.add("pe", mm, reads=["cat%d" % s_, "wg%d_b%d" % (wslot["p3"], oc // 4)], writes=["ps%d" % bank])
                        P.add("dve", lambda e, oc=oc, bank=bank: e.tensor_tensor(out=xr[s_][:, oc, :n], in0=psum[:, bank, :n], in1=xr[s_][:, oc, :n], op=ALU.add),
                              reads=["ps%d" % bank], writes=["xr%d_%d" % (s_, oc), "xr%d" % s_])
                        P.add("act", lambda e, oc=oc: e.activation(out=sq[:, oc, :n], in_=xr[s_][:, oc, :n], func=AF.Square),
                              reads=["xr%d_%d" % (s_, oc)], writes=["sq3_%d_%d" % (i % 2, oc)])
                    P.add("sp", lambda e: e.dma_start(out=dr["hA"].rearrange("(c p) t -> p c t", p=128)[:, :, off:off + n], in_=xr[s_][:, :, :n]),
                          reads=["xr%d" % s_] + ["xr%d_%d" % (s_, c) for c in range(8)], dma="xr%d" % s_)

                def s2(i):
                    off, n = tiles[i]
                    s_ = i % NS
                    h_ = i % 2
                    sq = sqd[h_]
                    emit_rstd("3", lambda c: sq[:, c, :n], 8, n, 2, 0, 0, rstd[:, :n], rtmp[:, :n], ["sq3_%d_%d" % (h_, c) for c in range(8)], "rstd3")
                    for c in range(8):
                        P.add("dve", lambda e, c=c: e.scalar_tensor_tensor(out=hno[h_][:, c, :n], in0=xr[s_][:, c, :n], scalar=gcol(1, c), in1=rstd[:, :n],
                                                                           op0=ALU.mult, op1=ALU.mult),
                              reads=["xr%d_%d" % (s_, c), "rstd3", "vec"], writes=["hno%d" % h_])
                    P.add("sp", lambda e: e.dma_start(out=dr["hn"].rearrange("(c p) t -> p c t", p=128)[:, :, off:off + n], in_=hno[h_][:, :, :n]),
                          reads=["hno%d" % h_], dma="hno%d" % h_)

                nt = len(tiles)
                for i0 in range(min(NS, nt)):
                    load(i0)
                s1(0)
                for i in range(nt):
                    if i + 1 < nt:
                        s1(i + 1)
                    s2(i)
                    if i + NS < nt:
                        load(i + NS)
                P.barrier()

        def mlp_half(nm, l, hf, src, dst, final):
            with ExitStack() as st:
                wup = WV(wg[wslot[nm]], 2048)
                wdn = sb(st, "wdn", [128, 16, 1024], BF16)
                hnb = [sb(st, "hnb%d" % i, [128, 8, 512], BF16) for i in range(2)]
                hr = [sb(st, "hr%d" % i, [128, 8, 512], F32) for i in range(2)]
                act = [sb(st, "act%d" % i, [128, 16, 512], BF16) for i in range(2)]
                rl = [sb(st, "rl%d" % i, [128, 512], F32) for i in range(2)]
                if final:
                    sq = sb(st, "sqf", [128, 8, 512], BF16)
                    rstd = sb(st, "rstdf", [128, 512], F32)
                    rtmp = sb(st, "rtmpf", [128, 512], F32)
                wstart(nm)
                for j4 in range(4):
                    P.add("pool", lambda g, j4=j4: g.dma_start(
                        out=wdn[:, j4 * 4:(j4 + 1) * 4, :],
                        in_=dr["w_down"][l, hf * 2048 + j4 * 512:hf * 2048 + (j4 + 1) * 512, :].rearrange("(j p) o -> p j o", p=128)),
                        writes=["wdn"], dma="wdn")
                wprefetch(nm)
                tiles = (LOC_TILES[1:] + LOC_TILES[:1]) if l == 0 else LOC_TILES[1:]
                if "mlp_tiles" in dbg:
                    tiles = [tiles[j] for j in dbg["mlp_tiles"]]

                def load(i):
                    off, n = tiles[i]
                    s_ = i % 2
                    P.add("sp", lambda e: e.dma_start(out=hnb[s_][:, :, :n], in_=dr["hn"].rearrange("(c p) t -> p c t", p=128)[:, :, off:off + n]),
                          writes=["hnb%d" % s_], dma="hnb%d" % s_)
                    P.add("sp", lambda e: e.dma_start(out=hr[s_][:, :, :n], in_=dr[src].rearrange("(c p) t -> p c t", p=128)[:, :, off:off + n]),
                          writes=["hr%d" % s_] + ["hr%d_%d" % (s_, c) for c in range(8)], dma="hr%d" % s_)

                def body(i):
                    off, n = tiles[i]
                    s_ = i % 2
                    a_ = act[s_]
                    for j in range(16):
                        bank = j % 4
                        u = j % 2

                        def mm(e, j=j, bank=bank):
                            ins = None
                            for c in range(8):
                                ins = e.matmul(psum[:, bank, :n], wup[:, c, j * 128:(j + 1) * 128], hnb[s_][:, c, :n], start=(c == 0), stop=(c == 7))
                            return ins
                        P.add("pe", mm, reads=["hnb%d" % s_, "wg%d_b%d" % (wslot[nm], j // 4)], writes=["ps%d" % bank])
                        P.add("act", lambda e, bank=bank, u=u: e.activation(out=rl[u][:, :n], in_=psum[:, bank, :n], func=AF.Relu),
                              reads=["ps%d" % bank], writes=["rl%d" % u])
                        P.add("dve", lambda e, j=j, u=u: e.tensor_tensor(out=a_[:, j, :n], in0=rl[u][:, :n], in1=rl[u][:, :n], op=ALU.mult),
                              reads=["rl%d" % u], writes=["act%d_%d" % (s_, j)])
                    ares = ["act%d_%d" % (s_, j) for j in range(16)]
                    for oc in range(8):
                        bank = 4 + (oc % 2)

                        def mm2(e, oc=oc, bank=bank):
                            ins = None
                            for j in range(16):
                                ins = e.matmul(psum[:, bank, :n], wdn[:, j, oc * 128:(oc + 1) * 128], a_[:, j, :n], start=(j == 0), stop=(j == 15))
                            return ins
                        P.add("pe", mm2, reads=ares + ["wdn"], writes=["ps%d" % bank])
                        P.add("dve", lambda e, oc=oc, bank=bank: e.tensor_tensor(out=hr[s_][:, oc, :n], in0=psum[:, bank, :n], in1=hr[s_][:, oc, :n], op=ALU.add),
                              reads=["ps%d" % bank], writes=["hr%d_%d" % (s_, oc), "hr%d" % s_])
                        if final:
                            P.add("act", lambda e, oc=oc: e.activation(out=sq[:, oc, :n], in_=hr[s_][:, oc, :n], func=AF.Square),
                                  reads=["hr%d_%d" % (s_, oc)], writes=["sqf_%d" % oc])
                    hres = ["hr%d" % s_] + ["hr%d_%d" % (s_, c) for c in range(8)]
                    if not final:
                        P.add("sp", lambda e: e.dma_start(out=dr[dst].rearrange("(c p) t -> p c t", p=128)[:, :, off:off + n], in_=hr[s_][:, :, :n]),
                              reads=hres, dma="hr%d" % s_)
                    else:
                        emit_rstd("f", lambda c: sq[:, c, :n], 8, n, 6, 0, 0, rstd[:, :n], rtmp[:, :n], ["sqf_%d" % c for c in range(8)], "rstdf")
                        for c in range(8):
                            P.add("dve", lambda e, c=c: e.scalar_tensor_tensor(out=hr[s_][:, c, :n], in0=hr[s_][:, c, :n], scalar=gcol(4, c), in1=rstd[:, :n],
                                                                               op0=ALU.mult, op1=ALU.mult),
                                  reads=["rstdf", "vec"], writes=["hr%d_%d" % (s_, c), "hr%d" % s_])
                        P.add("sp", lambda e: e.dma_start(out=dr["yT"].rearrange("(c p) t -> p c t", p=128)[:, :, off - HALO:off - HALO + n], in_=hr[s_][:, :, :n]),
                              reads=hres, dma="hr%d" % s_)

                load(0)
                for i in range(len(tiles)):
                    if i + 1 < len(tiles):
                        load(i + 1)
                    body(i)
                P.barrier()

        def phase4a():
            with ExitStack() as st:
                pw1 = WV(wg[wslot["p4a"]], 2048)
                xr = [sb(st, "x4_%d" % i, [128, 8, 512], F32) for i in range(2)]
                sq = sb(st, "sq4", [128, 8, 512], BF16)
                rstd = sb(st, "rstd4", [128, 512], F32)
                rtmp = sb(st, "rtmp4", [128, 512], F32)
                hnd = [sb(st, "hn4_%d" % i, [128, 8, 512], BF16) for i in range(2)]
                sg = [sb(st, "sg%d" % i, [128, 512], F32) for i in range(2)]
                gb = [sb(st, "gb%d" % i, [128, 8, 512], BF16) for i in range(2)]
                wstart("p4a")
                wprefetch("p4a")
                tiles = LOC_TILES

                def load(i):
                    off, n = tiles[i]
                    s_ = i % 2
                    P.add("sp", lambda e: e.dma_start(out=xr[s_][:, :, :n], in_=dr["hA"].rearrange("(c p) t -> p c t", p=128)[:, :, off:off + n]),
                          writes=["x4_%d" % s_], dma="x4_%d" % s_)

                def sa(i):
                    off, n = tiles[i]
                    s_ = i % 2
                    hnb = hnd[s_]
                    for c in range(8):
                        P.add("act", lambda e, c=c: e.activation(out=sq[:, c, :n], in_=xr[s_][:, c, :n], func=AF.Square),
                              reads=["x4_%d" % s_], writes=["sq4_%d" % c])
                    emit_rstd("4", lambda c: sq[:, c, :n], 8, n, 6, 0, 0, rstd[:, :n], rtmp[:, :n], ["sq4_%d" % c for c in range(8)], "rstd4")
                    for c in range(8):
                        P.add("dve", lambda e, c=c: e.scalar_tensor_tensor(out=hnb[:, c, :n], in0=xr[s_][:, c, :n], scalar=gcol(2, c), in1=rstd[:, :n],
                                                                           op0=ALU.mult, op1=ALU.mult),
                              reads=["x4_%d" % s_, "rstd4", "vec"], writes=["hn4_%d_%d" % (s_, c)])

                def sb_(i):
                    off, n = tiles[i]
                    s_ = i % 2
                    hnb = hnd[s_]
                    hres = ["hn4_%d_%d" % (s_, c) for c in range(8)]
                    for c in range(8):
                        ba = 2 * (c % 2)
                        bb = ba + 1
                        u = c % 2

                        def mm(e, col, bank):
                            ins = None
                            for k in range(8):
                                ins = e.matmul(psum[:, bank, :n], pw1[:, k, col:col + 128], hnb[:, k, :n], start=(k == 0), stop=(k == 7))
                            return ins
                        P.add("pe", lambda e, c=c, ba=ba: mm(e, c * 128, ba), reads=hres + ["wg%d_b%d" % (wslot["p4a"], c // 4)], writes=["ps%d" % ba])
                        P.add("pe", lambda e, c=c, bb=bb: mm(e, 1024 + c * 128, bb), reads=hres + ["wg%d_b%d" % (wslot["p4a"], 2 + c // 4)], writes=["ps%d" % bb])
                        P.add("act", lambda e, c=c, bb=bb, u=u: e.activation(out=sg[u][:, :n], in_=psum[:, bb, :n], func=AF.Sigmoid,
                                                                             bias=vcol(VC_PW1B, 8 + c), scale=1.0),
                              reads=["ps%d" % bb, "vec"], writes=["sg%d" % u])
                        P.add("dve", lambda e, c=c, ba=ba, u=u: e.scalar_tensor_tensor(out=gb[s_][:, c, :n], in0=psum[:, ba, :n], scalar=vcol(VC_PW1B, c),
                                                                                      in1=sg[u][:, :n], op0=ALU.add, op1=ALU.mult),
                              reads=["ps%d" % ba, "sg%d" % u, "vec"], writes=["gb%d" % s_])
                    if i == 0:
                        P.add("dve", lambda e: e.tensor_scalar(out=gb[s_][:, :, :n], in0=gb[s_][:, :, :n], scalar1=flags[:, 0:1], scalar2=None, op0=ALU.mult),
                              reads=["flags"], writes=["gb%d" % s_])
                    P.add("sp", lambda e: e.dma_start(out=dr["gT"].rearrange("(c p) t -> p c t", p=128)[:, :, off:off + n], in_=gb[s_][:, :, :n]),
                          reads=["gb%d" % s_], dma="gb%d" % s_)

                nt = len(tiles)
                load(0)
                if nt > 1:
                    load(1)
                sa(0)
                for i in range(nt):
                    if i + 1 < nt:
                        sa(i + 1)
                    if i + 2 < nt:
                        load(i + 2)
                    sb_(i)
                P.barrier()

        def phase4b():
            with ExitStack() as st:
                pw2t = sb(st, "pw2l", [128, 8192], BF16)
                pw2 = WV(pw2t, 1024)
                dgs = sb(st, "dgs", [128, 32, 8, 32], BF16)
                wst = sb(st, "wst_sb", [128, 32, 8], F32)
                i4 = sb(st, "i4_sb", [128, 32], F32)
                SW = 32 + 512
                gst = [sb(st, "gst%d" % i, [128, 32, SW], BF16) for i in range(2)]
                hr = [sb(st, "h4_%d" % i, [128, 8, 512], F32) for i in range(2)]
                b8d = [sb(st, "b8_%d" % i, [128, 8, 512], BF16) for i in range(2)]
                sqd = [sb(st, "sq5_%d" % i, [128, 8, 512], BF16) for i in range(2)]
                mean = sb(st, "mean", [128, 512], F32)
                m2 = sb(st, "m2", [128, 512], F32)
                var = sb(st, "var", [128, 512], F32)
                lrstd = sb(st, "lrstd", [128, 512], F32)
                ltmp = sb(st, "ltmp", [128, 512], F32)
                tt = [sb(st, "tt%d" % i, [128, 512], F32) for i in range(2)]
                rstd = sb(st, "rstd5", [128, 512], F32)
                rtmp = sb(st, "rtmp5", [128, 512], F32)
                hno1 = sb(st, "hno5", [128, 8, 512], BF16)
                hno = [hno1, hno1]
                for c in range(8):
                    P.add("pool", lambda g, c=c: g.dma_start(out=pw2.chunk(c), in_=dr["pw2"][c * 128:(c + 1) * 128, :]),
                          writes=["pw2l"], dma="pw2l")
                for i_ in range(2):
                    P.add("dve", lambda e, i_=i_: e.memset(gst[i_][96:128, :, 28 + 511:28 + 512], 0.0), writes=["gbuf%d" % i_])
                P.add("sp", lambda e: e.dma_start(out=wst[:], in_=dr["wst"].rearrange("p (q t) -> p q t", q=32)), writes=["wst"], dma="wst")
                P.add("sp", lambda e: e.dma_start(out=i4[:], in_=dr["i4"]), writes=["i4"], dma="i4")
                for q in range(32):
                    for tg in range(8):
                        P.add("dve", lambda e, q=q, tg=tg: e.tensor_scalar(out=dgs[:, q, tg, :], in0=i4[:], scalar1=wst[:, q, tg:tg + 1],
                                                                           scalar2=None, op0=ALU.mult),
                              reads=["i4", "wst"], writes=["dgs_%d" % q])
                tiles = LOC_TILES[1:]
                if "p4b_tiles" in dbg:
                    tiles = [tiles[j] for j in dbg["p4b_tiles"]]

                def load(i):
                    off, n = tiles[i]
                    s_ = i % 2
                    for jj in range(4):
                        w_ = 28 + n - (1 if jj == 3 else 0)
                        P.add("sp", lambda e, jj=jj, w_=w_: e.dma_start(
                            out=gst[s_][32 * jj:32 * jj + 32, :, 0:w_],
                            in_=dr["gT"].rearrange("(q ch) t -> ch q t", ch=32)[:, :, off - 30 + jj:off - 30 + jj + w_]),
                            writes=["gbuf%d" % s_], dma="gbuf%d" % s_)

                def load_h(i):
                    off, n = tiles[i]
                    s_ = i % 2
                    P.add("sp", lambda e: e.dma_start(out=hr[s_][:, :, :n], in_=dr["hA"].rearrange("(c p) t -> p c t", p=128)[:, :, off:off + n]),
                          writes=["h4_%d" % s_] + ["h4_%d_%d" % (s_, c) for c in range(8)], dma="h4_%d" % s_)

                def conv_chunk(i, c):
                    off, n = tiles[i]
                    s_ = i % 2
                    gres = "gbuf%d" % s_
                    bank = c % 2
                    b8, sq = b8d[s_], sqd[s_]

                    def mm(e):
                        ins = None
                        for tg in range(8):
                            for qq in range(4):
                                q = 4 * c + qq
                                ins = e.matmul(psum[32 * qq:32 * qq + 32, bank, :n], dgs[:, q, tg, :], gst[s_][:, q, 4 * tg:4 * tg + n],
                                               start=(tg == 0), stop=(tg == 7), skip_group_check=True, tile_position=(0, 32 * qq))
                        return ins
                    P.add("pe", mm, reads=[gres] + ["dgs_%d" % (4 * c + qq) for qq in range(4)], writes=["ps%d" % bank])
                    P.add("act", lambda e: e.activation(out=b8[:, c, :n], in_=psum[:, bank, :n], func=AF.Identity, bias=vcol(VC_DWB, c), scale=1.0),
                          reads=["ps%d" % bank, "vec"], writes=["b8_%d_%d" % (s_, c)])
                    P.add("act", lambda e: e.activation(out=sq[:, c, :n], in_=psum[:, bank, :n], func=AF.Square, bias=vcol(VC_DWB, c), scale=1.0),
                          reads=["ps%d" % bank, "vec"], writes=["sq5_%d_%d" % (s_, c)])

                def stats(i):
                    off, n = tiles[i]
                    s_ = i % 2
                    b8, sq = b8d[s_], sqd[s_]

                    def mm_mean(e):
                        ins = None
                        for c in range(8):
                            ins = e.matmul(psum[:, 2, :n], cavg[:, 0, :], b8[:, c, :n], start=(c == 0), stop=(c == 7))
                        return ins
                    P.add("pe", mm_mean, reads=["b8_%d_%d" % (s_, c) for c in range(8)] + ["cavg"], writes=["ps2"])

                    def mm_msq(e):
                        ins = None
                        for c in range(8):
                            ins = e.matmul(psum[:, 3, :n], cavg[:, 0, :], sq[:, c, :n], start=(c == 0), stop=(c == 7))
                        return ins
                    P.add("pe", mm_msq, reads=["sq5_%d_%d" % (s_, c) for c in range(8)] + ["cavg"], writes=["ps3"])
                    P.add("act", lambda e: e.activation(out=mean[:, :n], in_=psum[:, 2, :n], func=AF.Identity), reads=["ps2"], writes=["mean"])
                    P.add("dve", lambda e: e.tensor_tensor(out=m2[:, :n], in0=mean[:, :n], in1=mean[:, :n], op=ALU.mult), reads=["mean"], writes=["m2"])
                    P.add("dve", lambda e: e.tensor_tensor(out=var[:, :n], in0=psum[:, 3, :n], in1=m2[:, :n], op=ALU.subtract),
                          reads=["ps3", "m2"], writes=["var"])
                    P.add("dve", lambda e: e.tensor_scalar(out=var[:, :n], in0=var[:, :n], scalar1=0.0, scalar2=None, op0=ALU.max), writes=["var"])
                    P.add("act", lambda e: e.activation(out=ltmp[:, :n], in_=var[:, :n], func=AF.Ln, bias=epsb[:, 1:2], scale=1.0),
                          reads=["var", "epsb"], writes=["ltmp"])
                    P.add("act", lambda e: e.activation(out=lrstd[:, :n], in_=ltmp[:, :n], func=AF.Exp, scale=-0.5), reads=["ltmp"], writes=["lrstd"])

                def ln_chunk(i, c):
                    off, n = tiles[i]
                    s_ = i % 2
                    b8 = b8d[s_]
                    u = c % 2
                    P.add("dve", lambda e: e.tensor_tensor(out=tt[u][:, :n], in0=b8[:, c, :n], in1=mean[:, :n], op=ALU.subtract),
                          reads=["b8_%d_%d" % (s_, c), "mean"], writes=["tt%d" % u])
                    P.add("dve", lambda e: e.tensor_tensor(out=tt[u][:, :n], in0=tt[u][:, :n], in1=lrstd[:, :n], op=ALU.mult),
                          reads=["lrstd"], writes=["tt%d" % u])
                    P.add("act", lambda e: e.activation(out=b8[:, c, :n], in_=tt[u][:, :n], func=AF.Silu, bias=vcol(VC_LNB, c), scale=vcol(VC_LNG, c)),
                          reads=["tt%d" % u, "vec"], writes=["b8_%d_%d" % (s_, c)])

                def tail(i):
                    off, n = tiles[i]
                    s_ = i % 2
                    b8, sq = b8d[s_], sqd[s_]
                    for oc in range(8):
                        bank = 4 + (oc % 2)

                        def mm2(e, oc=oc, bank=bank):
                            ins = None
                            for c in range(8):
                                ins = e.matmul(psum[:, bank, :n], pw2[:, c, oc * 128:(oc + 1) * 128], b8[:, c, :n], start=(c == 0), stop=(c == 7))
                            return ins
                        P.add("pe", mm2, reads=["b8_%d_%d" % (s_, c) for c in range(8)] + ["pw2l"], writes=["ps%d" % bank])
                        P.add("dve", lambda e, oc=oc, bank=bank: e.scalar_tensor_tensor(out=hr[s_][:, oc, :n], in0=psum[:, bank, :n], scalar=vcol(VC_PW2B, oc),
                                                                                       in1=hr[s_][:, oc, :n], op0=ALU.add, op1=ALU.add),
                              reads=["ps%d" % bank, "vec"], writes=["h4_%d_%d" % (s_, oc), "h4_%d" % s_])
                        P.add("act", lambda e, oc=oc: e.activation(out=sq[:, oc, :n], in_=hr[s_][:, oc, :n], func=AF.Square),
                              reads=["h4_%d_%d" % (s_, oc)], writes=["sq5_%d_%d" % (s_, oc)])
                    hres = ["h4_%d" % s_] + ["h4_%d_%d" % (s_, c) for c in range(8)]
                    P.add("sp", lambda e: e.dma_start(out=dr["hB"].rearrange("(c p) t -> p c t", p=128)[:, :, off:off + n], in_=hr[s_][:, :, :n]),
                          reads=hres, dma="h4_%d" % s_)
                    emit_rstd("5", lambda c: sq[:, c, :n], 8, n, 6, 0, 0, rstd[:, :n], rtmp[:, :n], ["sq5_%d_%d" % (s_, c) for c in range(8)], "rstd5")
                    for c in range(8):
                        P.add("dve", lambda e, c=c: e.scalar_tensor_tensor(out=hno[s_][:, c, :n], in0=hr[s_][:, c, :n], scalar=gcol(3, c), in1=rstd[:, :n],
                                                                           op0=ALU.mult, op1=ALU.mult),
                              reads=["h4_%d_%d" % (s_, c), "rstd5", "vec"], writes=["hno5"])
                    P.add("sp", lambda e: e.dma_start(out=dr["hn"].rearrange("(c p) t -> p c t", p=128)[:, :, off:off + n], in_=hno[s_][:, :, :n]),
                          reads=["hno5"], dma="hno5")

                nt = len(tiles)
                load(0)
                load_h(0)
                if nt > 1:
                    load(1)
                    load_h(1)
                for c in range(8):
                    conv_chunk(0, c)
                for i in range(nt):
                    if i + 2 < nt:
                        load(i + 2)
                    stats(i)
                    for c in range(8):
                        if i + 1 < nt:
                            conv_chunk(i + 1, c)
                        if c >= 2:
                            ln_chunk(i, c - 2)
                    ln_chunk(i, 6)
                    ln_chunk(i, 7)
                    tail(i)
                    if i + 2 < nt:
                        load_h(i + 2)
                P.barrier()

        g1 = [p for p in ("p1", "p2", "p3", "m0a", "m0b", "p4a") if p in phases]
        if g1:
            with ExitStack() as wst:
                wg[0] = sb(wst, "wgA0", [128, 16384], BF16)
                wg[1] = sb(wst, "wgA1", [128, 16384], BF16)
                if "p1" in phases:
                    phase1()
                if "p2" in phases:
                    phase2()
                if "p3" in phases:
                    phase3()
                if "m0a" in phases:
                    mlp_half("m0a", 0, 0, "hA", "hB", False)
                if "m0b" in phases:
                    mlp_half("m0b", 0, 1, "hB", "hA", False)
                if "p4a" in phases:
                    phase4a()
        if "p4b" in phases:
            phase4b()
        if "m1a" in phases or "m1b" in phases:
            with ExitStack() as wst:
                wg[0] = sb(wst, "wgB0", [128, 16384], BF16)
                wg[1] = sb(wst, "wgB1", [128, 16384], BF16)
                if "m1a" in phases:
                    mlp_half("m1a", 1, 0, "hB", "hA", False)
                if "m1b" in phases:
                    mlp_half("m1b", 1, 1, "hA", None, True)

        P.barrier()
        P.emit()
    return nc


def _rope_tables(pos):
    inv_freq = (np.float32(500000.0) ** (-np.arange(0, 16, 2, dtype=np.float32) / np.float32(16))).astype(np.float32)
    ang = pos[:, None].astype(np.float32) * inv_freq[None, :]
    cos = np.cos(ang).astype(np.float32).T
    sin = np.sin(ang).astype(np.float32).T
    C = np.ones((128, pos.shape[0]), np.float32)
    S = np.zeros((128, pos.shape[0]), np.float32)
    for m in range(2):
        b = m * 64
        C[b:b + 8] = cos
        C[b + 8:b + 16] = cos
        S[b:b + 8] = -sin
        S[b + 8:b + 16] = sin
    return C, S


def _const_mats():
    ident = np.eye(128, dtype=np.float32)
    perm = np.eye(128, dtype=np.float32)
    for m in range(2):
        b = m * 64
        for d in range(8):
            perm[b + d, b + d] = 0
            perm[b + d + 8, b + d + 8] = 0
            perm[b + d, b + d + 8] = 1
            perm[b + d + 8, b + d] = 1
    tri = (np.arange(128)[None, :] >= np.arange(128)[:, None]).astype(np.float32)
    ones = np.ones((128, 128), np.float32)
    return np.concatenate([ident, perm, tri, tri, ones], axis=1)


def prepare_inputs(inp):
    x = np.asarray(inp["x"], np.float32)
    f = lambda k: np.asarray(inp[k], np.float32)

    def chunked(v):
        return np.ascontiguousarray(v.reshape(-1, 128).T)

    vec = np.zeros((128, VC_N), np.float32)
    gains = [f("mix_norm")[0], f("mlp_norm")[0], f("mix_norm")[1], f("mlp_norm")[1], f("final_norm")]
    for i, g in enumerate(gains):
        vec[:, VC_GAIN + 8 * i:VC_GAIN + 8 * i + 8] = chunked(g)
    vec[:, VC_PSCALE:VC_PSCALE + 4] = chunked(f("pool_scale")[0])
    vec[:, VC_PW1B:VC_PW1B + 16] = chunked(f("conv_pw1_b")[0])
    vec[:, VC_DWB:VC_DWB + 8] = chunked(f("conv_dw_b")[0])
    vec[:, VC_LNG:VC_LNG + 8] = chunked(f("conv_ln_g")[0])
    vec[:, VC_LNB:VC_LNB + 8] = chunked(f("conv_ln_b")[0])
    vec[:, VC_PW2B:VC_PW2B + 8] = chunked(f("conv_pw2_b")[0])
    vec[:, VC_SUBLN] = f("subln")[0]
    dww = f("conv_dw_w")[0]
    for t in range(CONV_K):
        vec[:, VC_DWW + t * 8:VC_DWW + t * 8 + 8] = chunked(dww[t])
    lamv = np.concatenate([f("lam_q1")[0], f("lam_k1")[0], f("lam_q2")[0], f("lam_k2")[0]])[None, :]
    lamv = np.ascontiguousarray(np.broadcast_to(lamv, (128, 256)))
    cmat = _const_mats()
    wst = np.zeros((128, 32, 8), np.float32)
    for jj in range(4):
        for tg in range(8):
            k = 4 * tg + jj
            if k < CONV_K:
                wst[32 * jj:32 * jj + 32, :, tg] = dww[k].reshape(32, 32).T
    i4 = np.concatenate([np.eye(32, dtype=np.float32)] * 4, axis=0)
    selm = np.zeros((64, 2, 128), np.float32)
    selm[0:32, 0, :] = 1.0 / 32
    selm[32:64, 1, :] = 1.0 / 32
    shared = {
        "vec": vec, "lamv": lamv, "cmat": cmat, "selm": np.ascontiguousarray(selm.reshape(64, 256)),
        "wst": np.ascontiguousarray(wst.reshape(128, 256)), "i4": i4,
        "w_in": np.ascontiguousarray(f("w_in")[0]), "pool_w": np.ascontiguousarray(f("pool_w")[0]),
        "w_out": np.ascontiguousarray(f("w_out")[0]), "w_up": f("w_up"), "w_down": f("w_down"),
        "pw1": np.ascontiguousarray(f("conv_pw1_w")[0]), "pw2": np.ascontiguousarray(f("conv_pw2_w")[0]),
    }
    in_maps = []
    for c in range(NCORES):
        b, half = divmod(c, 2)
        xb = x[b]
        if half == 0:
            xl = np.zeros((NLOC, D), np.float32)
            xl[HALO:] = xb[:HALF]
            xp = np.zeros((NPRE, D), np.float32)
            pos = np.arange(NKEY, dtype=np.float32) - np.float32(HALF)
            kb = np.zeros((128, NKEY // 128), np.float32)
            kb[:, :NPREB + 1] = NEG
            hv = 0.0
            pfix = np.zeros((128, 4, 16), np.float32)
            for g, w in enumerate(POOL_WINDOWS):
                t = np.arange(16)
                pfix[:, g, :] = (1.0 / np.minimum(t + 1, w) - 1.0 / w).astype(np.float32)[None, :]
        else:
            xl = xb[HALF - HALO:]
            xp = xb[:NPRE]
            pos = np.arange(NKEY, dtype=np.float32)
            kb = np.zeros((128, NKEY // 128), np.float32)
            hv = 1.0
            pfix = np.zeros((128, 4, 16), np.float32)
        C, S = _rope_tables(pos)
        flags = np.zeros((128, 4), np.float32)
        flags[:, 0] = hv
        m = dict(shared)
        m.update({
            "xT_loc": np.ascontiguousarray(xl.T), "xT_pre": np.ascontiguousarray(xp.T),
            "ropeC": C, "ropeS": S, "kbias": kb, "flags": flags,
            "pfix": np.ascontiguousarray(pfix.reshape(128, 64)),
        })
        in_maps.append(m)
    return in_maps


def kernel(**inputs):
    in_maps = prepare_inputs(inputs)
    nc = build_program()
    res = run_bass_kernel_spmd(nc, in_maps, core_ids=list(range(NCORES)))
    out = np.empty((4, SEQ, D), np.float32)
    for c in range(NCORES):
        b, half = divmod(c, 2)
        out[b, half * HALF:(half + 1) * HALF, :] = res.results[c]["yT"].T
    return out
```

```python
import math
from contextlib import ExitStack

import numpy as np
import concourse.bass as bass
import concourse.mybir as mybir
from concourse.bass_utils import run_bass_kernel_spmd

F32 = mybir.dt.float32
BF16 = mybir.dt.bfloat16
AF = mybir.ActivationFunctionType
ALU = mybir.AluOpType

D = 1024
SEQ = 8192
NCORES = 8
HALF = 4096
HALO = 128
NLOC = HALF + HALO
NPRE = HALF - HALO
NKEY = NPRE + NLOC
NPREB = NPRE // 128
DFF = 4096
RMS_EPS = 1e-6
LN_EPS = 1e-5
SUBLN_EPS = 1e-5
POOL_WINDOWS = (2, 4, 8, 16)
CONV_K = 31
NEG = -30000.0

VC_GAIN = 0
VC_PSCALE = 40
VC_PW1B = 44
VC_DWB = 60
VC_LNG = 68
VC_LNB = 76
VC_PW2B = 84
VC_SUBLN = 92
VC_DWW = 93
VC_N = VC_DWW + CONV_K * 8

LOC_TILES = [(0, HALO)] + [(HALO + 512 * i, 512) for i in range(8)]
PRE_TILES = [(512 * i, 512) for i in range(7)] + [(3584, 384)]


class Res:
    __slots__ = ("name", "last_w", "readers", "dma_sem", "dma_cnt")

    def __init__(self, name):
        self.name = name
        self.last_w = None
        self.readers = []
        self.dma_sem = None
        self.dma_cnt = 0


class Op:
    __slots__ = ("eng", "fn", "deps", "is_dma", "sem", "val", "observed")

    def __init__(self, eng, fn, is_dma):
        self.eng = eng
        self.fn = fn
        self.deps = []
        self.is_dma = is_dma
        self.sem = None
        self.val = 0
        self.observed = False


ENGS = ("pe", "act", "dve", "pool", "sp")


class WV:
    def __init__(self, t, width):
        self.t = t
        self.w = width

    def __getitem__(self, idx):
        _, c, sl = idx
        return self.t[:, c * self.w + sl.start:c * self.w + sl.stop]

    def chunk(self, c):
        return self.t[:, c * self.w:(c + 1) * self.w]


class Prog:
    def __init__(self, nc, stack):
        self.nc = nc
        self.stack = stack
        self.ops = {e: [] for e in ENGS}
        self.all_ops = []
        self.res = {}
        self.eng_sem = {e: stack.enter_context(nc.semaphore("sem_" + e)) for e in ("pe", "act", "dve", "pool")}
        self.n_dma_sems = 0

    def R(self, name):
        r = self.res.get(name)
        if r is None:
            r = self.res[name] = Res(name)
        return r

    def _rl(self, xs):
        out = []
        for x in xs:
            out.append(self.R(x) if isinstance(x, str) else x)
        return out

    def add(self, eng, fn, reads=(), writes=(), dma=None):
        op = Op(eng, fn, dma is not None)
        reads = self._rl(reads)
        writes = self._rl(writes)
        psr = [r for r in reads if r.name.startswith("ps") and r.name[2:].isdigit()]
        if psr:
            reads = [r for r in reads if r not in psr]
            writes = writes + [r for r in psr if r not in writes]
        deps = []
        for r in reads:
            if r.last_w is not None:
                deps.append(r.last_w)
        for w in writes:
            if w.last_w is not None:
                deps.append(w.last_w)
            deps.extend(w.readers)
        seen = set()
        for d in deps:
            if d is op or id(d) in seen:
                continue
            seen.add(id(d))
            if eng == "pe" and d.eng == "pe" and not d.is_dma:
                continue
            op.deps.append(d)
            d.observed = True
        for r in reads:
            r.readers.append(op)
        for w in writes:
            w.last_w = op
            w.readers = []
        if dma is not None:
            sr = self.R(dma)
            if sr.dma_sem is None:
                sr.dma_sem = self.stack.enter_context(self.nc.semaphore("dsem%d" % self.n_dma_sems))
                self.n_dma_sems += 1
            sr.dma_cnt += 16
            op.sem = sr.dma_sem
            op.val = sr.dma_cnt
        self.ops[eng].append(op)
        self.all_ops.append(op)
        return op

    def barrier(self):
        lasts = []
        for e in ("pe", "act", "dve", "pool"):
            if self.ops[e]:
                lasts.append(self.ops[e][-1])
        dmas = {}
        for op in self.all_ops:
            if op.is_dma:
                dmas[id(op.sem)] = op
        lasts.extend(dmas.values())
        bops = []
        for e in ENGS:
            op = Op(e, None, False)
            for d in lasts:
                op.deps.append(d)
                d.observed = True
            self.ops[e].append(op)
            self.all_ops.append(op)
            bops.append(op)
        for r in self.res.values():
            r.last_w = None
            r.readers = []
        return bops

    def emit(self):
        nc = self.nc
        for e in ("pe", "act", "dve", "pool"):
            cnt = 0
            for op in self.ops[e]:
                if op.is_dma or op.fn is None:
                    continue
                if op.observed:
                    cnt += 1
                    op.sem = self.eng_sem[e]
                    op.val = cnt
        handles = {"pe": "tensor", "act": "scalar", "dve": "vector", "pool": "gpsimd", "sp": "sync"}
        with nc.Block() as block:
            for e in ENGS:
                ops = self.ops[e]

                def body(eng, ops=ops):
                    waited = {}
                    for op in ops:
                        need = {}
                        for d in op.deps:
                            if d.sem is None:
                                continue
                            k = id(d.sem)
                            if k not in need or need[k][1] < d.val:
                                need[k] = (d.sem, d.val)
                        for k, (sem, val) in need.items():
                            if waited.get(k, 0) < val:
                                eng.wait_ge(sem, val)
                                waited[k] = val
                        if op.fn is None:
                            continue
                        ins = op.fn(eng)
                        if op.is_dma:
                            ins.then_inc(op.sem, 16)
                        elif op.observed:
                            ins.then_inc(op.sem, 1)

                getattr(block, handles[e])(body)


ALL_PHASES = ("p1", "p2", "p3", "m0a", "m0b", "p4a", "p4b", "m1a", "m1b")

SCRATCH = {
    "KT": ([4, 128, NKEY], BF16),
    "V4": ([4, 128, NKEY // 128, 128], BF16),
    "QT": ([4, 128, NLOC], BF16),
    "catT": ([D, NLOC], BF16),
    "catB": ([D // 2, NLOC], BF16),
    "hA": ([D, NLOC], F32),
    "hB": ([D, NLOC], F32),
    "hn": ([D, NLOC], BF16),
    "gT": ([D, NLOC], BF16),
}
PHASE_IO = {
    "p1": ((), ("KT", "V4", "QT", "catT")),
    "p2": (("KT", "V4", "QT"), ("catB",)),
    "p3": (("catT", "catB"), ("hA", "hn")),
    "m0a": (("hA", "hn"), ("hB",)),
    "m0b": (("hB", "hn"), ("hA",)),
    "p4a": (("hA",), ("gT",)),
    "p4b": (("hA", "gT"), ("hB", "hn")),
    "m1a": (("hB", "hn"), ("hA",)),
    "m1b": (("hA", "hn"), ()),
}


def build_program(phases=ALL_PHASES, dump=(), dbg=None):
    dbg = dbg or {}
    nc = bass.Bass("TRN2", target_bir_lowering=False)
    dr = {}

    def din(name, shape, dt=F32):
        dr[name] = nc.dram_tensor(name, list(shape), dt, kind="ExternalInput").ap()

    din("xT_loc", [D, NLOC])
    din("xT_pre", [D, NPRE])
    din("ropeC", [128, NKEY])
    din("ropeS", [128, NKEY])
    din("kbias", [128, NKEY // 128])
    din("kvalid", [128, (NPREB + 1) * 32])
    din("flags", [128, 4])
    din("pfix", [128, 4 * 16])
    din("vec", [128, VC_N])
    din("lamv", [128, 4 * 64])
    din("cmat", [128, 5 * 128])
    din("selm", [64, 256])
    if "p1" in phases:
        din("w_in", [D, 2048])
        din("pool_w", [4, 128, 128])
    if "p3" in phases:
        din("w_out", [D, D])
    if any(p in phases for p in ("m0a", "m0b", "m1a", "m1b")):
        din("w_up", [2, D, DFF])
        din("w_down", [2, DFF, D])
    if "p4a" in phases:
        din("pw1", [D, 2048])
    if "p4b" in phases:
        din("pw2", [D, D])
        din("wst", [128, 32 * 8])
        din("i4", [128, 32])

    produced = set()
    consumed_ext = set()
    for ph in ALL_PHASES:
        if ph not in phases:
            continue
        ins_, outs_ = PHASE_IO[ph]
        for t in ins_:
            if t not in produced:
                consumed_ext.add(t)
        produced.update(outs_)
    for name, (shape, dt) in SCRATCH.items():
        if name in consumed_ext:
            kind = "ExternalInput"
        elif name in dump and name in produced:
            kind = "ExternalOutput"
        else:
            kind = "Internal"
        dr[name] = nc.dram_tensor(name, list(shape), dt, kind=kind).ap()
    if "m1b" in phases:
        dr["yT"] = nc.dram_tensor("yT", [D, HALF], F32, kind="ExternalOutput").ap()

    with ExitStack() as stack:
        P = Prog(nc, stack)

        uniq = [0]

        def sb(st, name, shape, dt):
            uniq[0] += 1
            return st.enter_context(nc.sbuf_tensor("%s_u%d" % (name, uniq[0]), list(shape), dt))

        psum = stack.enter_context(nc.psum_tensor("psum", [128, 8, 512], F32))
        wg = [None, None]

        def wl_rows(name, b, width, row_sel):
            def f():
                t = wg[b]
                for c in range(8):
                    P.add("pool", lambda g, c=c, t=t: g.dma_start(out=t[:, c * width:(c + 1) * width], in_=row_sel(c)),
                          writes=["wg%d" % b], dma="wg%d" % b)
            return f

        wplan = []
        if "p1" in phases:
            wplan.append(("p1", 2048, lambda c: dr["w_in"][c * 128:(c + 1) * 128, :]))
        if "p3" in phases:
            wplan.append(("p3", 1024, lambda c: dr["w_out"][c * 128:(c + 1) * 128, :]))
        for nm, l, hf in (("m0a", 0, 0), ("m0b", 0, 1)):
            if nm in phases:
                wplan.append((nm, 2048, lambda c, l=l, hf=hf: dr["w_up"][l, c * 128:(c + 1) * 128, hf * 2048:(hf + 1) * 2048]))
        if "p4a" in phases:
            wplan.append(("p4a", 2048, lambda c: dr["pw1"][c * 128:(c + 1) * 128, :]))
        for nm, l, hf in (("m1a", 1, 0), ("m1b", 1, 1)):
            if nm in phases:
                wplan.append((nm, 2048, lambda c, l=l, hf=hf: dr["w_up"][l, c * 128:(c + 1) * 128, hf * 2048:(hf + 1) * 2048]))
        wslot = {nm: i % 2 for i, (nm, _, _) in enumerate(wplan)}
        wloaders = {nm: wl_rows(nm, i % 2, width, sel) for i, (nm, width, sel) in enumerate(wplan)}
        GROUP2 = ("m1a", "m1b")
        wnext = {wplan[i][0]: wplan[i + 1][0] for i in range(len(wplan) - 1)
                 if (wplan[i][0] in GROUP2) == (wplan[i + 1][0] in GROUP2)}
        wfirst = set()
        for grp in (False, True):
            names = [w[0] for w in wplan if (w[0] in GROUP2) == grp]
            if names:
                wfirst.add(names[0])

        def wstart(nm):
            if nm in wfirst:
                wloaders[nm]()

        def wprefetch(nm):
            if nm in wnext:
                wloaders[wnext[nm]]()
        cst = sb(stack, "cst", [128, 5, 128], BF16)
        selm = sb(stack, "selm_sb", [64, 2, 128], F32)
        cavg = sb(stack, "cavg", [128, 2, 128], BF16)
        vec = sb(stack, "vecs", [128, VC_N], F32)
        flags = sb(stack, "flags_sb", [128, 4], F32)
        kbias = sb(stack, "kbias_sb", [128, NKEY // 128], F32)
        kvalid = sb(stack, "kvalid_sb", [128, NPREB + 1, 32], BF16)
        lam_sb = sb(stack, "lam_sb", [128, 8], F32)
        lamv = sb(stack, "lamv_sb", [128, 4, 64], F32)
        lamt = sb(stack, "lamt_sb", [128, 2, 64], F32)
        subl = sb(stack, "subl_sb", [128, 1], F32)
        epsb = sb(stack, "epsb", [128, 4], F32)

        P.add("pool", lambda g: g.dma_start(out=cst[:], in_=dr["cmat"].rearrange("p (a b) -> p a b", a=5)),
              writes=["cst"], dma="cst")
        P.add("sp", lambda e: e.dma_start(out=vec[:], in_=dr["vec"]), writes=["vec"], dma="vec")
        P.add("sp", lambda e: e.dma_start(out=selm[:], in_=dr["selm"].rearrange("p (a b) -> p a b", a=2)), writes=["selm"], dma="selm")
        P.add("sp", lambda e: e.dma_start(out=flags[:], in_=dr["flags"]), writes=["flags"], dma="flags")
        P.add("sp", lambda e: e.dma_start(out=kbias[:], in_=dr["kbias"]), writes=["kbias"], dma="kbias")
        P.add("pool", lambda g: g.dma_start(out=kvalid[:], in_=dr["kvalid"].rearrange("p (a b) -> p a b", b=32)), writes=["kvalid"], dma="kvalid")
        P.add("sp", lambda e: e.dma_start(out=lamv[:], in_=dr["lamv"].rearrange("p (a b) -> p a b", a=4)),
              writes=["lamv"], dma="lamv")
        P.add("dve", lambda e: e.memset(cavg[:, 0, :], 1.0 / D), writes=["cavg"])
        P.add("dve", lambda e: e.memset(cavg[:, 1, :], 1.0 / 128), writes=["cavg"])
        P.add("dve", lambda e: e.memset(epsb[:, 0:1], RMS_EPS), writes=["epsb"])
        P.add("dve", lambda e: e.memset(epsb[:, 1:2], LN_EPS), writes=["epsb"])
        P.add("dve", lambda e: e.memset(epsb[:, 2:3], SUBLN_EPS), writes=["epsb"])
        lambda_init = 0.8 - 0.6 * math.exp(-0.3 * 0)
        P.add("dve", lambda e: e.tensor_tensor(out=lamt[:, 0, :], in0=lamv[:, 0, :], in1=lamv[:, 1, :], op=ALU.mult),
              reads=["lamv"], writes=["lamt"])
        P.add("dve", lambda e: e.tensor_tensor(out=lamt[:, 1, :], in0=lamv[:, 2, :], in1=lamv[:, 3, :], op=ALU.mult),
              reads=["lamv"], writes=["lamt"])
        P.add("dve", lambda e: e.reduce_sum(out=lam_sb[:, 2:3], in_=lamt[:, 0, :], axis=mybir.AxisListType.X),
              reads=["lamt"], writes=["lam_a"])
        P.add("dve", lambda e: e.reduce_sum(out=lam_sb[:, 3:4], in_=lamt[:, 1, :], axis=mybir.AxisListType.X),
              reads=["lamt"], writes=["lam_b"])
        P.add("act", lambda e: e.activation(out=lam_sb[:, 4:6], in_=lam_sb[:, 2:4], func=AF.Exp),
              reads=["lam_a", "lam_b"], writes=["lam_c"])
        P.add("dve", lambda e: e.scalar_tensor_tensor(out=lam_sb[:, 0:1], in0=lam_sb[:, 5:6], scalar=-lambda_init,
                                                      in1=lam_sb[:, 4:5], op0=ALU.add, op1=ALU.subtract),
              reads=["lam_c"], writes=["lam"])
        P.add("dve", lambda e: e.tensor_scalar(out=subl[:], in0=vec[:, VC_SUBLN:VC_SUBLN + 1], scalar1=1.0 - lambda_init,
                                               scalar2=None, op0=ALU.mult),
              reads=["vec"], writes=["subl"])
        IDENT, PERM, TRI, ONES = 0, 1, 2, 4

        def gcol(norm_idx, c):
            j = VC_GAIN + norm_idx * 8 + c
            return vec[:, j:j + 1]

        def vcol(base, c):
            return vec[:, base + c:base + c + 1]

        def emit_rstd(tag, sq_ap_fn, nch, n, ps_bank, avg_idx, eps_col, rstd_ap, tmp_ap, sq_res, out_res):
            if sq_res is None:
                sq_res = ["sq%d" % c for c in range(nch)]

            def mm(e):
                ins = None
                for c in range(nch):
                    ins = e.matmul(psum[:, ps_bank, :n], cavg[:, avg_idx, :], sq_ap_fn(c), start=(c == 0), stop=(c == nch - 1))
                return ins
            P.add("pe", mm, reads=list(sq_res) + ["cavg"], writes=["ps%d" % ps_bank])
            P.add("act", lambda e: e.activation(out=tmp_ap, in_=psum[:, ps_bank, :n], func=AF.Ln,
                                                bias=epsb[:, eps_col:eps_col + 1], scale=1.0),
                  reads=["ps%d" % ps_bank, "epsb"], writes=[out_res + "_t"])
            P.add("act", lambda e: e.activation(out=rstd_ap, in_=tmp_ap, func=AF.Exp, scale=-0.5),
                  reads=[out_res + "_t"], writes=[out_res])

        def phase1():
            with ExitStack() as st:
                win = WV(wg[wslot["p1"]], 2048)
                wres = "wg%d" % wslot["p1"]
                poolw = sb(st, "poolw", [128, 4, 128], BF16)
                pfix = sb(st, "pfix_sb", [128, 4, 16], F32)
                xt = [sb(st, "xt%d" % i, [128, 8, 512], F32) for i in range(2)]
                cs = [sb(st, "cs%d" % i, [128, 2, 512], F32) for i in range(2)]
                sq = sb(st, "sq", [128, 8, 512], BF16)
                rstd = sb(st, "rstd", [128, 512], F32)
                rtmp = sb(st, "rtmp", [128, 512], F32)
                hnd = [sb(st, "hn_sb%d" % i, [128, 8, 512], BF16) for i in range(2)]
                ubuf = [sb(st, "ubuf%d" % i, [128, 4, 16 + 512], F32) for i in range(2)]
                wk = [sb(st, "wk%d" % i, [128, 16 + 512], F32) for i in range(2)]
                pooled = sb(st, "pooled", [128, 4, 512], BF16)
                ptmp = sb(st, "ptmp", [128, 2, 16], F32)
                qkb = [sb(st, "qkb%d" % i, [128, 512], BF16) for i in range(2)]
                t1 = [sb(st, "t1_%d" % i, [128, 512], F32) for i in range(2)]
                t2 = [sb(st, "t2_%d" % i, [128, 512], F32) for i in range(2)]
                qrot = [sb(st, "qrot%d" % i, [128, 4, 512], BF16) for i in range(2)]
                krot = [sb(st, "krot%d" % i, [128, 4, 512], BF16) for i in range(2)]
                vbuf = sb(st, "vbuf", [128, 4, 512], BF16)
                cata = sb(st, "cata", [128, 4, 512], BF16)

                wstart("p1")
                P.add("pool", lambda g: g.dma_start(out=poolw[:], in_=dr["pool_w"].rearrange("g c d -> c g d")),
                      writes=["poolw"], dma="poolw")
                wprefetch("p1")
                P.add("sp", lambda e: e.dma_start(out=pfix[:], in_=dr["pfix"].rearrange("p (a b) -> p a b", a=4)),
                      writes=["pfix"], dma="pfix")
                P.add("dve", lambda e: e.memset(ubuf[0][:, :, 0:16], 0.0), writes=["ubuf0"])

                tiles = [("pre", o, n) for (o, n) in PRE_TILES] + [("loc", o, n) for (o, n) in LOC_TILES]
                if "p1_tiles" in dbg:
                    tiles = [tiles[j] for j in dbg["p1_tiles"]]
                loc_index = {}
                for i, (kind, off, n) in enumerate(tiles):
                    if kind == "loc":
                        loc_index[i] = len(loc_index)

                def load_x(i):
                    kind, off, n = tiles[i]
                    s = i % 2
                    src = dr["xT_pre"] if kind == "pre" else dr["xT_loc"]
                    P.add("sp", lambda e: e.dma_start(out=xt[s][:, :, :n], in_=src.rearrange("(c p) t -> p c t", p=128)[:, :, off:off + n]),
                          writes=["xt%d" % s], dma="xt%d" % s)

                def load_cs(i):
                    kind, off, n = tiles[i]
                    s = i % 2
                    koff = off if kind == "pre" else NPRE + off
                    P.add("sp", lambda e: e.dma_start(out=cs[s][:, 0, :n], in_=dr["ropeC"][:, koff:koff + n]),
                          writes=["cs%d" % s], dma="cs%d" % s)
                    P.add("sp", lambda e: e.dma_start(out=cs[s][:, 1, :n], in_=dr["ropeS"][:, koff:koff + n]),
                          writes=["cs%d" % s], dma="cs%d" % s)

                rr = [0]

                def stage_a(i):
                    kind, off, n = tiles[i]
                    s = i % 2
                    xs = xt[s]
                    xres = "xt%d" % s
                    hn = hnd[s]
                    for c in range(8):
                        P.add("act", lambda e, c=c: e.activation(out=sq[:, c, :n], in_=xs[:, c, :n], func=AF.Square),
                              reads=[xres], writes=["sq%d" % c])
                    emit_rstd("n", lambda c: sq[:, c, :n], 8, n, 0, 0, 0, rstd[:, :n], rtmp[:, :n], None, "rstd")
                    for c in range(8):
                        P.add("dve", lambda e, c=c: e.scalar_tensor_tensor(out=hn[:, c, :n], in0=xs[:, c, :n], scalar=gcol(0, c),
                                                                           in1=rstd[:, :n], op0=ALU.mult, op1=ALU.mult),
                              reads=[xres, "rstd", "vec"], writes=["hn%d_%d" % (s, c)])

                def stage_b(i):
                    kind, off, n = tiles[i]
                    s = i % 2
                    hn = hnd[s]
                    koff = off if kind == "pre" else NPRE + off
                    hn_res = ["hn%d_%d" % (s, c) for c in range(8)]
                    nb = n // 128
                    is_loc = kind == "loc"

                    def proj_fm(ocol, bank):
                        def mm(e):
                            ins = None
                            for c in range(8):
                                ins = e.matmul(psum[:, bank, :n], win[:, c, ocol:ocol + 128], hn[:, c, :n], start=(c == 0), stop=(c == 7))
                            return ins
                        P.add("pe", mm, reads=hn_res + [wres], writes=["ps%d" % bank])

                    def rope_chunk(ocol, dst_ap, dst_res, between=None):
                        k = rr[0]
                        rr[0] += 1
                        bank = 1 + (k % 2)
                        b2 = 3 + (k % 2)
                        u = k % 2
                        proj_fm(ocol, bank)
                        P.add("act", lambda e: e.activation(out=qkb[u][:, :n], in_=psum[:, bank, :n], func=AF.Identity),
                              reads=["ps%d" % bank], writes=["qkb%d" % u])
                        if between is not None:
                            between()
                        P.add("pe", lambda e: e.matmul(psum[:, b2, :n], cst[:, PERM, :], qkb[u][:, :n], start=True, stop=True),
                              reads=["qkb%d" % u, "cst"], writes=["ps%d" % b2])
                        P.add("dve", lambda e: e.tensor_tensor(out=t1[u][:, :n], in0=psum[:, bank, :n], in1=cs[s][:, 0, :n], op=ALU.mult),
                              reads=["ps%d" % bank, "cs%d" % s], writes=["t1_%d" % u])
                        P.add("dve", lambda e: e.tensor_tensor(out=t2[u][:, :n], in0=psum[:, b2, :n], in1=cs[s][:, 1, :n], op=ALU.mult),
                              reads=["ps%d" % b2, "cs%d" % s], writes=["t2_%d" % u])
                        P.add("pool", lambda e: e.tensor_tensor(out=dst_ap, in0=t1[u][:, :n], in1=t2[u][:, :n], op=ALU.add),
                              reads=["t1_%d" % u, "t2_%d" % u], writes=[dst_res])

                    def v_block(tb):
                        bank = 5 + (tb % 2)

                        def mmv(e):
                            ins = None
                            for c in range(8):
                                ins = e.matmul(psum[:, bank, :], hn[:, c, tb * 128:(tb + 1) * 128], win[:, c, 1536:2048], start=(c == 0), stop=(c == 7))
                            return ins
                        P.add("pe", mmv, reads=hn_res + [wres], writes=["ps%d" % bank])
                        P.add("act", lambda e: e.activation(out=vbuf[:, tb, :], in_=psum[:, bank, :], func=AF.Identity),
                              reads=["ps%d" % bank], writes=["vbuf"])

                    if is_loc:
                        lt = loc_index[i]
                        us = lt % 2
                        ub = ubuf[us]
                        ures = "ubuf%d" % us

                    def u_chunk(g):
                        bank = 7
                        proj_fm(g * 128, bank)
                        P.add("act", lambda e: e.activation(out=ub[:, g, 16:16 + n], in_=psum[:, bank, :n], func=AF.Identity),
                              reads=["ps%d" % bank], writes=[ures + "_%d" % g])

                    for h in range(4):
                        rope_chunk(1024 + h * 128, krot[s][:, h, :n], "krot%d" % s,
                                   between=(lambda h=h: v_block(h)) if h < nb else None)
                        if is_loc:
                            rope_chunk(512 + h * 128, qrot[s][:, h, :n], "qrot%d" % s, between=lambda h=h: u_chunk(h))
                    for h in range(4):
                        P.add("sp", lambda e, h=h: e.dma_start(out=dr["KT"][h, :, koff:koff + n], in_=krot[s][:, h, :n]),
                              reads=["krot%d" % s], dma="krot%d" % s)
                    for h in range(4):
                        P.add("sp", lambda e, h=h: e.dma_start(out=dr["V4"][h, :, koff // 128:koff // 128 + nb, :],
                                                              in_=vbuf[:, 0:nb, h * 128:(h + 1) * 128]),
                              reads=["vbuf"], dma="vbuf")
                    if not is_loc:
                        return
                    for h in range(4):
                        P.add("sp", lambda e, h=h: e.dma_start(out=dr["QT"][h, :, off:off + n], in_=qrot[s][:, h, :n]),
                              reads=["qrot%d" % s], dma="qrot%d" % s)
                    for g, w in enumerate(POOL_WINDOWS):
                        ug = ures + "_%d" % g
                        lvl = 1
                        k = 0
                        src = None
                        srcres = None
                        while lvl < w:
                            lo = -(w - 2 * lvl)
                            dst = wk[k % 2]
                            dres = "wk%d" % (k % 2)
                            if src is None:
                                a0 = ub[:, g, 16 + lo:16 + n]
                                a1 = ub[:, g, 16 + lo - lvl:16 + n - lvl]
                                sres = [ug, ures]
                            else:
                                a0 = src[:, 16 + lo:16 + n]
                                a1 = src[:, 16 + lo - lvl:16 + n - lvl]
                                sres = [srcres]
                            P.add("pool", lambda e, a0=a0, a1=a1, dst=dst, lo=lo: e.tensor_tensor(out=dst[:, 16 + lo:16 + n], in0=a0, in1=a1, op=ALU.add),
                                  reads=sres, writes=[dres])
                            src, srcres = dst, dres
                            lvl *= 2
                            k += 1
                        P.add("dve", lambda e, g=g, w=w, src=src: e.scalar_tensor_tensor(out=pooled[:, g, :n], in0=src[:, 16:16 + n], scalar=1.0 / w,
                                                                                        in1=ub[:, g, 16:16 + n], op0=ALU.mult, op1=ALU.subtract),
                              reads=[srcres, ug], writes=["pooled%d" % g])
                        if lt == 1:
                            P.add("dve", lambda e, g=g, src=src: e.tensor_tensor(out=ptmp[:, 0, :], in0=src[:, 16:32], in1=pfix[:, g, :], op=ALU.mult),
                                  reads=[srcres, "pfix"], writes=["ptmp0"])
                            P.add("dve", lambda e, g=g, w=w, src=src: e.scalar_tensor_tensor(out=ptmp[:, 1, :], in0=src[:, 16:32], scalar=1.0 / w,
                                                                                            in1=ub[:, g, 16:32], op0=ALU.mult, op1=ALU.subtract),
                                  reads=[srcres, ug], writes=["ptmp1"])
                            P.add("dve", lambda e, g=g: e.tensor_tensor(out=pooled[:, g, 0:16], in0=ptmp[:, 0, :], in1=ptmp[:, 1, :], op=ALU.add),
                                  reads=["ptmp0", "ptmp1"], writes=["pooled%d" % g])
                        P.add("pe", lambda e, g=g: e.matmul(psum[:, 0, :n], poolw[:, g, :], pooled[:, g, :n], start=True, stop=True),
                              reads=["pooled%d" % g, "poolw"], writes=["ps0"])
                        P.add("act", lambda e, g=g: e.activation(out=cata[:, g, :n], in_=psum[:, 0, :n], func=AF.Identity, scale=vcol(VC_PSCALE, g)),
                              reads=["ps0", "vec"], writes=["cata"])
                    nub = ubuf[1 - us]
                    P.add("dve", lambda e: e.tensor_copy(out=nub[:, :, 0:16], in_=ub[:, :, n:n + 16]),
                          reads=[ures] + [ures + "_%d" % g for g in range(4)], writes=["ubuf%d" % (1 - us)])
                    P.add("sp", lambda e: e.dma_start(out=dr["catT"].rearrange("(g p) t -> p g t", p=128)[:, 0:4, off:off + n], in_=cata[:, :, :n]),
                          reads=["cata"], dma="cata")

                nt = len(tiles)
                load_x(0)
                load_cs(0)
                if nt > 1:
                    load_x(1)
                stage_a(0)
                for i in range(nt):
                    if i + 1 < nt:
                        stage_a(i + 1)
                        load_cs(i + 1)
                    if i + 2 < nt:
                        load_x(i + 2)
                    stage_b(i)
                P.barrier()

        def phase2():
            with ExitStack() as st:
                kt = [sb(st, "kt%d" % i, [128, NKEY], BF16) for i in range(2)]
                v4 = [sb(st, "v4_%d" % i, [128, NKEY // 128, 128], BF16) for i in range(2)]
                qt = [sb(st, "qt%d" % i, [128, 512], BF16) for i in range(2)]
                pT = [sb(st, "pT%d" % i, [128, 2, 512], BF16) for i in range(2)]
                lsb = sb(st, "lsb", [64, 512], F32)
                oo = [sb(st, "oo%d" % m, [128, 512], F32) for m in range(2)]
                osq = sb(st, "osq", [128, 512], BF16)
                orstd = sb(st, "orstd", [128, 512], F32)
                otmp = sb(st, "otmp", [128, 512], F32)
                on = [sb(st, "on%d" % i, [128, 512], BF16) for i in range(2)]
                nlam = sb(st, "nlam", [64, 1], F32)
                scale = 64 ** -0.5
                P.add("dve", lambda e: e.memset(nlam[0:32, :], 1.0), writes=["nlam"])
                P.add("dve", lambda e: e.tensor_copy(out=nlam[32:64, :], in_=lam_sb[32:64, 0:1]), reads=["lam"], writes=["nlam"])

                def load_head(h):
                    hs = h % 2
                    P.add("sp", lambda e: e.dma_start(out=kt[hs][:], in_=dr["KT"][h]), writes=["kt%d" % hs], dma="kt%d" % hs)
                    P.add("sp", lambda e: e.dma_start(out=v4[hs][:], in_=dr["V4"][h]), writes=["v4_%d" % hs], dma="v4_%d" % hs)

                jobs = [(h, t) for h in range(dbg.get("p2_heads", 4)) for t in range(len(LOC_TILES))]
                if "p2_jobs" in dbg:
                    jobs = [tuple(j) for j in dbg["p2_jobs"]]

                def load_q(j):
                    h, t = jobs[j]
                    off, n = LOC_TILES[t]
                    qs = j % 2
                    P.add("sp", lambda e: e.dma_start(out=qt[qs][:, :n], in_=dr["QT"][h, :, off:off + n]),
                          writes=["qt%d" % qs], dma="qt%d" % qs)

                def job(j, pending):
                    h, t = jobs[j]
                    off, n = LOC_TILES[t]
                    hs = h % 2
                    qs = j % 2
                    qb0 = off // 128
                    nb = n // 128
                    nkb = NPREB + qb0 + nb
                    ktr, v4r, qtr = "kt%d" % hs, "v4_%d" % hs, "qt%d" % qs

                    def c0_of(kb):
                        jj = kb - (NPREB + qb0)
                        return 0 if jj < 0 else jj * 128

                    def qk(kb):
                        c0 = c0_of(kb)
                        for m in range(2):
                            bank = 2 * (kb % 2) + m
                            P.add("pe", lambda e, m=m, bank=bank: e.matmul(psum[:, bank, c0:n], kt[hs][m * 64:(m + 1) * 64, kb * 128:(kb + 1) * 128],
                                                                           qt[qs][m * 64:(m + 1) * 64, c0:n], start=True, stop=True),
                                  reads=[ktr, qtr], writes=["ps%d" % bank])

                    def ex(kb):
                        c0 = c0_of(kb)
                        diag = kb >= NPREB + qb0
                        b0 = 2 * (kb % 2)
                        pt = pT[kb % 2]
                        pr = "pT%d" % (kb % 2)
                        P.add("act", lambda e: e.activation(out=pt[:, :, c0:n], in_=psum[:, b0:b0 + 2, c0:n], func=AF.Exp, scale=scale),
                              reads=["ps%d" % b0, "ps%d" % (b0 + 1)], writes=[pr])
                        if diag:
                            P.add("dve", lambda e: e.tensor_tensor(out=pt[:, :, c0:c0 + 128], in0=pt[:, :, c0:c0 + 128], in1=cst[:, TRI:TRI + 2, :], op=ALU.mult),
                                  reads=["cst"], writes=[pr])

                    def pv(kb):
                        c0 = c0_of(kb)
                        first = kb == 0
                        last = kb == nkb - 1
                        pt = pT[kb % 2]
                        pr = "pT%d" % (kb % 2)
                        sumw = kvalid[:, kb, :] if kb <= NPREB else cst[:, ONES, 0:32]

                        def mm(e):
                            for m in range(2):
                                e.matmul(psum[:, 4 + m, c0:n], v4[hs][:, kb, :], pt[:, m, c0:n], start=first, stop=last, skip_group_check=True)
                            ins = None
                            for m in range(2):
                                ins = e.matmul(psum[32 * m:32 * m + 32, 6, c0:n], sumw, pt[:, m, c0:n], start=first, stop=last,
                                               skip_group_check=True, tile_position=(0, 32 * m))
                            return ins
                        P.add("pe", mm, reads=[pr, v4r, "cst", "kvalid"], writes=["ps4", "ps5", "ps6"])

                    qk(0)
                    qk(1)
                    for kb in range(nkb):
                        ex(kb)
                        if kb + 2 < nkb:
                            qk(kb + 2)
                        pv(kb)
                        if pending and kb >= 3 and kb % 3 == 0:
                            pending.pop(0)()
                    while pending:
                        pending.pop(0)()
                    P.add("dve", lambda e: e.tensor_scalar(out=lsb[:, :n], in0=psum[0:64, 6, :n], scalar1=1e-30, scalar2=None, op0=ALU.max),
                          reads=["ps6"], writes=["lsb"])
                    for m in range(2):
                        P.add("dve", lambda e, m=m: e.tensor_copy(out=oo[m][:, :n], in_=psum[:, 4 + m, :n]),
                              reads=["ps%d" % (4 + m)], writes=["oo%d" % m])
                    os_ = j % 2

                    def st0():
                        P.add("dve", lambda e: e.reciprocal(out=lsb[:, :n], in_=lsb[:, :n]), writes=["lsb"])
                        P.add("dve", lambda e: e.tensor_scalar(out=lsb[:, :n], in0=lsb[:, :n], scalar1=nlam[:, 0:1], scalar2=None, op0=ALU.mult),
                              reads=["nlam"], writes=["lsb"])
                        P.add("pe", lambda e: e.matmul(psum[:, 7, :n], selm[:, 0, :], lsb[:, :n], start=True, stop=True),
                              reads=["lsb", "selm"], writes=["ps7"])

                    def st1():
                        P.add("dve", lambda e: e.tensor_tensor(out=oo[0][:, :n], in0=oo[0][:, :n], in1=psum[:, 7, :n], op=ALU.mult),
                              reads=["ps7"], writes=["oo0"])
                        P.add("pe", lambda e: e.matmul(psum[:, 7, :n], selm[:, 1, :], lsb[:, :n], start=True, stop=True),
                              reads=["lsb", "selm"], writes=["ps7"])

                    def st2():
                        P.add("dve", lambda e: e.tensor_tensor(out=oo[1][:, :n], in0=oo[1][:, :n], in1=psum[:, 7, :n], op=ALU.mult),
                              reads=["ps7"], writes=["oo1"])
                        P.add("dve", lambda e: e.tensor_tensor(out=oo[0][:, :n], in0=oo[0][:, :n], in1=oo[1][:, :n], op=ALU.add),
                              reads=["oo1"], writes=["oo0"])
                        P.add("dve", lambda e: e.tensor_tensor(out=osq[:, :n], in0=oo[0][:, :n], in1=oo[0][:, :n], op=ALU.mult), reads=["oo0"], writes=["osq"])

                        def mm(e):
                            return e.matmul(psum[:, 7, :n], cavg[:, 1, :], osq[:, :n], start=True, stop=True)
                        P.add("pe", mm, reads=["osq", "cavg"], writes=["ps7"])

                    def st3():
                        P.add("act", lambda e: e.activation(out=otmp[:, :n], in_=psum[:, 7, :n], func=AF.Ln, bias=epsb[:, 2:3], scale=1.0),
                              reads=["ps7", "epsb"], writes=["otmp"])
                        P.add("act", lambda e: e.activation(out=orstd[:, :n], in_=otmp[:, :n], func=AF.Exp, scale=-0.5),
                              reads=["otmp"], writes=["orstd"])

                    def st4():
                        P.add("dve", lambda e: e.scalar_tensor_tensor(out=on[os_][:, :n], in0=oo[0][:, :n], scalar=subl[:, 0:1], in1=orstd[:, :n],
                                                                      op0=ALU.mult, op1=ALU.mult),
                              reads=["oo0", "orstd", "subl"], writes=["on%d" % os_])
                        P.add("sp", lambda e: e.dma_start(out=dr["catB"][h * 128:(h + 1) * 128, off:off + n], in_=on[os_][:, :n]),
                              reads=["on%d" % os_], dma="on%d" % os_)
                    return [st0, st1, st2, st3, st4]

                loaded = set()
                pend = []
                load_q(0)
                for j in range(len(jobs)):
                    h = jobs[j][0]
                    if h not in loaded:
                        load_head(h)
                        loaded.add(h)
                    if j + 1 < len(jobs):
                        load_q(j + 1)
                    if j == 0 or jobs[j - 1][0] != h:
                        nxt = [jj[0] for jj in jobs[j:] if jj[0] != h]
                        if nxt and nxt[0] not in loaded:
                            load_head(nxt[0])
                            loaded.add(nxt[0])
                    pend = job(j, pend)
                while pend:
                    pend.pop(0)()
                P.barrier()

        def phase3():
            with ExitStack() as st:
                wout = WV(wg[wslot["p3"]], 1024)
                NS = 3
                cat = [sb(st, "cat%d" % i, [128, 8, 512], BF16) for i in range(NS)]
                xr = [sb(st, "xr%d" % i, [128, 8, 512], F32) for i in range(NS)]
                sqd = [sb(st, "sq3_%d" % i, [128, 8, 512], BF16) for i in range(2)]
                rstd = sb(st, "rstd3", [128, 512], F32)
                rtmp = sb(st, "rtmp3", [128, 512], F32)
                hno = [sb(st, "hno%d" % i, [128, 8, 512], BF16) for i in range(2)]
                wstart("p3")
                wprefetch("p3")
                tiles = LOC_TILES

                def load(i):
                    off, n = tiles[i]
                    s_ = i % NS
                    P.add("sp", lambda e: e.dma_start(out=cat[s_][:, 0:4, :n], in_=dr["catT"].rearrange("(c p) t -> p c t", p=128)[:, 0:4, off:off + n]),
                          writes=["cat%d" % s_], dma="cat%d" % s_)
                    P.add("sp", lambda e: e.dma_start(out=cat[s_][:, 4:8, :n], in_=dr["catB"].rearrange("(c p) t -> p c t", p=128)[:, :, off:off + n]),
                          writes=["cat%d" % s_], dma="cat%d" % s_)
                    P.add("sp", lambda e: e.dma_start(out=xr[s_][:, :, :n], in_=dr["xT_loc"].rearrange("(c p) t -> p c t", p=128)[:, :, off:off + n]),
                          writes=["xr%d" % s_] + ["xr%d_%d" % (s_, c) for c in range(8)], dma="xr%d" % s_)

                def s1(i):
                    off, n = tiles[i]
                    s_ = i % NS
                    sq = sqd[i % 2]
                    for oc in range(8):
                        bank = oc % 2

                        def mm(e, oc=oc, bank=bank):
                            ins = None
                            for c in range(8):
                                ins = e.matmul(psum[:, bank, :n], wout[:, c, oc * 128:(oc + 1) * 128], cat[s_][:, c, :n], start=(c == 0), stop=(c == 7))
                            return ins
                        P.add("pe", mm, reads=["cat%d" % s_, "wg%d" % wslot["p3"]], writes=["ps%d" % bank])
                        P.add("dve", lambda e, oc=oc, bank=bank: e.tensor_tensor(out=xr[s_][:, oc, :n], in0=psum[:, bank, :n], in1=xr[s_][:, oc, :n], op=ALU.add),
                              reads=["ps%d" % bank], writes=["xr%d_%d" % (s_, oc), "xr%d" % s_])
                        P.add("act", lambda e, oc=oc: e.activation(out=sq[:, oc, :n], in_=xr[s_][:, oc, :n], func=AF.Square),
                              reads=["xr%d_%d" % (s_, oc)], writes=["sq3_%d_%d" % (i % 2, oc)])
                    P.add("sp", lambda e: e.dma_start(out=dr["hA"].rearrange("(c p) t -> p c t", p=128)[:, :, off:off + n], in_=xr[s_][:, :, :n]),
                          reads=["xr%d" % s_] + ["xr%d_%d" % (s_, c) for c in range(8)], dma="xr%d" % s_)

                def s2(i):
                    off, n = tiles[i]
                    s_ = i % NS
                    h_ = i % 2
                    sq = sqd[h_]
                    emit_rstd("3", lambda c: sq[:, c, :n], 8, n, 2, 0, 0, rstd[:, :n], rtmp[:, :n], ["sq3_%d_%d" % (h_, c) for c in range(8)], "rstd3")
                    for c in range(8):
                        P.add("dve", lambda e, c=c: e.scalar_tensor_tensor(out=hno[h_][:, c, :n], in0=xr[s_][:, c, :n], scalar=gcol(1, c), in1=rstd[:, :n],
                                                                           op0=ALU.mult, op1=ALU.mult),
                              reads=["xr%d_%d" % (s_, c), "rstd3", "vec"], writes=["hno%d" % h_])
                    P.add("sp", lambda e: e.dma_start(out=dr["hn"].rearrange("(c p) t -> p c t", p=128)[:, :, off:off + n], in_=hno[h_][:, :, :n]),
                          reads=["hno%d" % h_], dma="hno%d" % h_)

                nt = len(tiles)
                for i0 in range(min(NS, nt)):
                    load(i0)
                s1(0)
                for i in range(nt):
                    if i + 1 < nt:
                        s1(i + 1)
                    s2(i)
                    if i + NS < nt:
                        load(i + NS)
                P.barrier()

        def mlp_half(nm, l, hf, src, dst, final):
            with ExitStack() as st:
                wup = WV(wg[wslot[nm]], 2048)
                wdn = sb(st, "wdn", [128, 16, 1024], BF16)
                hnb = [sb(st, "hnb%d" % i, [128, 8, 512], BF16) for i in range(2)]
                hr = [sb(st, "hr%d" % i, [128, 8, 512], F32) for i in range(2)]
                act = [sb(st, "act%d" % i, [128, 16, 512], BF16) for i in range(2)]
                rl = [sb(st, "rl%d" % i, [128, 512], F32) for i in range(2)]
                if final:
                    sq = sb(st, "sqf", [128, 8, 512], BF16)
                    rstd = sb(st, "rstdf", [128, 512], F32)
                    rtmp = sb(st, "rtmpf", [128, 512], F32)
                wstart(nm)
                for j4 in range(4):
                    P.add("pool", lambda g, j4=j4: g.dma_start(
                        out=wdn[:, j4 * 4:(j4 + 1) * 4, :],
                        in_=dr["w_down"][l, hf * 2048 + j4 * 512:hf * 2048 + (j4 + 1) * 512, :].rearrange("(j p) o -> p j o", p=128)),
                        writes=["wdn"], dma="wdn")
                wprefetch(nm)
                tiles = (LOC_TILES[1:] + LOC_TILES[:1]) if l == 0 else LOC_TILES[1:]
                if "mlp_tiles" in dbg:
                    tiles = [tiles[j] for j in dbg["mlp_tiles"]]

                def load(i):
                    off, n = tiles[i]
                    s_ = i % 2
                    P.add("sp", lambda e: e.dma_start(out=hnb[s_][:, :, :n], in_=dr["hn"].rearrange("(c p) t -> p c t", p=128)[:, :, off:off + n]),
                          writes=["hnb%d" % s_], dma="hnb%d" % s_)
                    P.add("sp", lambda e: e.dma_start(out=hr[s_][:, :, :n], in_=dr[src].rearrange("(c p) t -> p c t", p=128)[:, :, off:off + n]),
                          writes=["hr%d" % s_] + ["hr%d_%d" % (s_, c) for c in range(8)], dma="hr%d" % s_)

                def body(i):
                    off, n = tiles[i]
                    s_ = i % 2
                    a_ = act[s_]
                    for j in range(16):
                        bank = j % 4
                        u = j % 2

                        def mm(e, j=j, bank=bank):
                            ins = None
                            for c in range(8):
                                ins = e.matmul(psum[:, bank, :n], wup[:, c, j * 128:(j + 1) * 128], hnb[s_][:, c, :n], start=(c == 0), stop=(c == 7))
                            return ins
                        P.add("pe", mm, reads=["hnb%d" % s_, "wg%d" % wslot[nm]], writes=["ps%d" % bank])
                        P.add("act", lambda e, bank=bank, u=u: e.activation(out=rl[u][:, :n], in_=psum[:, bank, :n], func=AF.Relu),
                              reads=["ps%d" % bank], writes=["rl%d" % u])
                        P.add("dve", lambda e, j=j, u=u: e.tensor_tensor(out=a_[:, j, :n], in0=rl[u][:, :n], in1=rl[u][:, :n], op=ALU.mult),
                              reads=["rl%d" % u], writes=["act%d_%d" % (s_, j)])
                    ares = ["act%d_%d" % (s_, j) for j in range(16)]
                    for oc in range(8):
                        bank = 4 + (oc % 2)

                        def mm2(e, oc=oc, bank=bank):
                            ins = None
                            for j in range(16):
                                ins = e.matmul(psum[:, bank, :n], wdn[:, j, oc * 128:(oc + 1) * 128], a_[:, j, :n], start=(j == 0), stop=(j == 15))
                            return ins
                        P.add("pe", mm2, reads=ares + ["wdn"], writes=["ps%d" % bank])
                        P.add("dve", lambda e, oc=oc, bank=bank: e.tensor_tensor(out=hr[s_][:, oc, :n], in0=psum[:, bank, :n], in1=hr[s_][:, oc, :n], op=ALU.add),
                              reads=["ps%d" % bank], writes=["hr%d_%d" % (s_, oc), "hr%d" % s_])
                        if final:
                            P.add("act", lambda e, oc=oc: e.activation(out=sq[:, oc, :n], in_=hr[s_][:, oc, :n], func=AF.Square),
                                  reads=["hr%d_%d" % (s_, oc)], writes=["sqf_%d" % oc])
                    hres = ["hr%d" % s_] + ["hr%d_%d" % (s_, c) for c in range(8)]
                    if not final:
                        P.add("sp", lambda e: e.dma_start(out=dr[dst].rearrange("(c p) t -> p c t", p=128)[:, :, off:off + n], in_=hr[s_][:, :, :n]),
                              reads=hres, dma="hr%d" % s_)
                    else:
                        emit_rstd("f", lambda c: sq[:, c, :n], 8, n, 6, 0, 0, rstd[:, :n], rtmp[:, :n], ["sqf_%d" % c for c in range(8)], "rstdf")
                        for c in range(8):
                            P.add("dve", lambda e, c=c: e.scalar_tensor_tensor(out=hr[s_][:, c, :n], in0=hr[s_][:, c, :n], scalar=gcol(4, c), in1=rstd[:, :n],
                                                                               op0=ALU.mult, op1=ALU.mult),
                                  reads=["rstdf", "vec"], writes=["hr%d_%d" % (s_, c), "hr%d" % s_])
                        P.add("sp", lambda e: e.dma_start(out=dr["yT"].rearrange("(c p) t -> p c t", p=128)[:, :, off - HALO:off - HALO + n], in_=hr[s_][:, :, :n]),
                              reads=hres, dma="hr%d" % s_)

                load(0)
                for i in range(len(tiles)):
                    if i + 1 < len(tiles):
                        load(i + 1)
                    body(i)
                P.barrier()

        def phase4a():
            with ExitStack() as st:
                pw1 = WV(wg[wslot["p4a"]], 2048)
                xr = [sb(st, "x4_%d" % i, [128, 8, 512], F32) for i in range(2)]
                sq = sb(st, "sq4", [128, 8, 512], BF16)
                rstd = sb(st, "rstd4", [128, 512], F32)
                rtmp = sb(st, "rtmp4", [128, 512], F32)
                hnd = [sb(st, "hn4_%d" % i, [128, 8, 512], BF16) for i in range(2)]
                sg = [sb(st, "sg%d" % i, [128, 512], F32) for i in range(2)]
                gb = [sb(st, "gb%d" % i, [128, 8, 512], BF16) for i in range(2)]
                wstart("p4a")
                wprefetch("p4a")
                tiles = LOC_TILES

                def load(i):
                    off, n = tiles[i]
                    s_ = i % 2
                    P.add("sp", lambda e: e.dma_start(out=xr[s_][:, :, :n], in_=dr["hA"].rearrange("(c p) t -> p c t", p=128)[:, :, off:off + n]),
                          writes=["x4_%d" % s_], dma="x4_%d" % s_)

                def sa(i):
                    off, n = tiles[i]
                    s_ = i % 2
                    hnb = hnd[s_]
                    for c in range(8):
                        P.add("act", lambda e, c=c: e.activation(out=sq[:, c, :n], in_=xr[s_][:, c, :n], func=AF.Square),
                              reads=["x4_%d" % s_], writes=["sq4_%d" % c])
                    emit_rstd("4", lambda c: sq[:, c, :n], 8, n, 6, 0, 0, rstd[:, :n], rtmp[:, :n], ["sq4_%d" % c for c in range(8)], "rstd4")
                    for c in range(8):
                        P.add("dve", lambda e, c=c: e.scalar_tensor_tensor(out=hnb[:, c, :n], in0=xr[s_][:, c, :n], scalar=gcol(2, c), in1=rstd[:, :n],
                                                                           op0=ALU.mult, op1=ALU.mult),
                              reads=["x4_%d" % s_, "rstd4", "vec"], writes=["hn4_%d_%d" % (s_, c)])

                def sb_(i):
                    off, n = tiles[i]
                    s_ = i % 2
                    hnb = hnd[s_]
                    hres = ["hn4_%d_%d" % (s_, c) for c in range(8)]
                    for c in range(8):
                        ba = 2 * (c % 2)
                        bb = ba + 1
                        u = c % 2

                        def mm(e, col, bank):
                            ins = None
                            for k in range(8):
                                ins = e.matmul(psum[:, bank, :n], pw1[:, k, col:col + 128], hnb[:, k, :n], start=(k == 0), stop=(k == 7))
                            return ins
                        P.add("pe", lambda e, c=c, ba=ba: mm(e, c * 128, ba), reads=hres + ["wg%d" % wslot["p4a"]], writes=["ps%d" % ba])
                        P.add("pe", lambda e, c=c, bb=bb: mm(e, 1024 + c * 128, bb), reads=hres + ["wg%d" % wslot["p4a"]], writes=["ps%d" % bb])
                        P.add("act", lambda e, c=c, bb=bb, u=u: e.activation(out=sg[u][:, :n], in_=psum[:, bb, :n], func=AF.Sigmoid,
                                                                             bias=vcol(VC_PW1B, 8 + c), scale=1.0),
                              reads=["ps%d" % bb, "vec"], writes=["sg%d" % u])
                        P.add("dve", lambda e, c=c, ba=ba, u=u: e.scalar_tensor_tensor(out=gb[s_][:, c, :n], in0=psum[:, ba, :n], scalar=vcol(VC_PW1B, c),
                                                                                      in1=sg[u][:, :n], op0=ALU.add, op1=ALU.mult),
                              reads=["ps%d" % ba, "sg%d" % u, "vec"], writes=["gb%d" % s_])
                    if i == 0:
                        P.add("dve", lambda e: e.tensor_scalar(out=gb[s_][:, :, :n], in0=gb[s_][:, :, :n], scalar1=flags[:, 0:1], scalar2=None, op0=ALU.mult),
                              reads=["flags"], writes=["gb%d" % s_])
                    P.add("sp", lambda e: e.dma_start(out=dr["gT"].rearrange("(c p) t -> p c t", p=128)[:, :, off:off + n], in_=gb[s_][:, :, :n]),
                          reads=["gb%d" % s_], dma="gb%d" % s_)

                nt = len(tiles)
                load(0)
                if nt > 1:
                    load(1)
                sa(0)
                for i in range(nt):
                    if i + 1 < nt:
                        sa(i + 1)
                    if i + 2 < nt:
                        load(i + 2)
                    sb_(i)
                P.barrier()

        def phase4b():
            with ExitStack() as st:
                pw2t = sb(st, "pw2l", [128, 8192], BF16)
                pw2 = WV(pw2t, 1024)
                dgs = sb(st, "dgs", [128, 32, 8, 32], BF16)
                wst = sb(st, "wst_sb", [128, 32, 8], F32)
                i4 = sb(st, "i4_sb", [128, 32], F32)
                SW = 32 + 512
                gst = [sb(st, "gst%d" % i, [128, 32, SW], BF16) for i in range(2)]
                hr = [sb(st, "h4_%d" % i, [128, 8, 512], F32) for i in range(2)]
                yf = sb(st, "yf", [128, 8, 512], F32)
                b8 = sb(st, "b8", [128, 8, 512], BF16)
                sq = sb(st, "sq5", [128, 8, 512], BF16)
                mean = sb(st, "mean", [128, 512], F32)
                m2 = sb(st, "m2", [128, 512], F32)
                var = sb(st, "var", [128, 512], F32)
                lrstd = sb(st, "lrstd", [128, 512], F32)
                ltmp = sb(st, "ltmp", [128, 512], F32)
                tt = [sb(st, "tt%d" % i, [128, 512], F32) for i in range(2)]
                rstd = sb(st, "rstd5", [128, 512], F32)
                rtmp = sb(st, "rtmp5", [128, 512], F32)
                hno1 = sb(st, "hno5", [128, 8, 512], BF16)
                hno = [hno1, hno1]
                for c in range(8):
                    P.add("pool", lambda g, c=c: g.dma_start(out=pw2.chunk(c), in_=dr["pw2"][c * 128:(c + 1) * 128, :]),
                          writes=["pw2l"], dma="pw2l")
                for i_ in range(2):
                    P.add("dve", lambda e, i_=i_: e.memset(gst[i_][96:128, :, 28 + 511:28 + 512], 0.0), writes=["gbuf%d" % i_])
                P.add("sp", lambda e: e.dma_start(out=wst[:], in_=dr["wst"].rearrange("p (q t) -> p q t", q=32)), writes=["wst"], dma="wst")
                P.add("sp", lambda e: e.dma_start(out=i4[:], in_=dr["i4"]), writes=["i4"], dma="i4")
                for q in range(32):
                    for tg in range(8):
                        P.add("dve", lambda e, q=q, tg=tg: e.tensor_scalar(out=dgs[:, q, tg, :], in0=i4[:], scalar1=wst[:, q, tg:tg + 1],
                                                                           scalar2=None, op0=ALU.mult),
                              reads=["i4", "wst"], writes=["dgs_%d" % q])
                tiles = LOC_TILES[1:]
                if "p4b_tiles" in dbg:
                    tiles = [tiles[j] for j in dbg["p4b_tiles"]]

                def load(i):
                    off, n = tiles[i]
                    s_ = i % 2
                    for jj in range(4):
                        w_ = 28 + n - (1 if jj == 3 else 0)
                        P.add("sp", lambda e, jj=jj, w_=w_: e.dma_start(
                            out=gst[s_][32 * jj:32 * jj + 32, :, 0:w_],
                            in_=dr["gT"].rearrange("(q ch) t -> ch q t", ch=32)[:, :, off - 30 + jj:off - 30 + jj + w_]),
                            writes=["gbuf%d" % s_], dma="gbuf%d" % s_)
                    P.add("sp", lambda e: e.dma_start(out=hr[s_][:, :, :n], in_=dr["hA"].rearrange("(c p) t -> p c t", p=128)[:, :, off:off + n]),
                          writes=["h4_%d" % s_] + ["h4_%d_%d" % (s_, c) for c in range(8)], dma="h4_%d" % s_)

                def body(i):
                    off, n = tiles[i]
                    s_ = i % 2
                    gres = "gbuf%d" % s_
                    for c in range(8):
                        bank = c % 2

                        def mm(e, c=c, bank=bank):
                            ins = None
                            for tg in range(8):
                                for qq in range(4):
                                    q = 4 * c + qq
                                    ins = e.matmul(psum[32 * qq:32 * qq + 32, bank, :n], dgs[:, q, tg, :], gst[s_][:, q, 4 * tg:4 * tg + n],
                                                   start=(tg == 0), stop=(tg == 7), skip_group_check=True, tile_position=(0, 32 * qq))
                            return ins
                        P.add("pe", mm, reads=[gres] + ["dgs_%d" % (4 * c + qq) for qq in range(4)], writes=["ps%d" % bank])
                        P.add("act", lambda e, c=c, bank=bank: e.activation(out=yf[:, c, :n], in_=psum[:, bank, :n], func=AF.Identity,
                                                                            bias=vcol(VC_DWB, c), scale=1.0),
                              reads=["ps%d" % bank, "vec"], writes=["yf%d" % c])
                        P.add("act", lambda e, c=c: e.activation(out=sq[:, c, :n], in_=yf[:, c, :n], func=AF.Square),
                              reads=["yf%d" % c], writes=["sq5_%d" % c])
                        P.add("dve", lambda e, c=c: e.tensor_copy(out=b8[:, c, :n], in_=yf[:, c, :n]),
                              reads=["yf%d" % c], writes=["b8_%d" % c])

                    def mm_mean(e):
                        ins = None
                        for c in range(8):
                            ins = e.matmul(psum[:, 2, :n], cavg[:, 0, :], b8[:, c, :n], start=(c == 0), stop=(c == 7))
                        return ins
                    P.add("pe", mm_mean, reads=["b8_%d" % c for c in range(8)] + ["cavg"], writes=["ps2"])

                    def mm_msq(e):
                        ins = None
                        for c in range(8):
                            ins = e.matmul(psum[:, 3, :n], cavg[:, 0, :], sq[:, c, :n], start=(c == 0), stop=(c == 7))
                        return ins
                    P.add("pe", mm_msq, reads=["sq5_%d" % c for c in range(8)] + ["cavg"], writes=["ps3"])
                    P.add("act", lambda e: e.activation(out=mean[:, :n], in_=psum[:, 2, :n], func=AF.Identity), reads=["ps2"], writes=["mean"])
                    P.add("dve", lambda e: e.tensor_tensor(out=m2[:, :n], in0=mean[:, :n], in1=mean[:, :n], op=ALU.mult), reads=["mean"], writes=["m2"])
                    P.add("dve", lambda e: e.tensor_tensor(out=var[:, :n], in0=psum[:, 3, :n], in1=m2[:, :n], op=ALU.subtract),
                          reads=["ps3", "m2"], writes=["var"])
                    P.add("dve", lambda e: e.tensor_scalar(out=var[:, :n], in0=var[:, :n], scalar1=0.0, scalar2=None, op0=ALU.max), writes=["var"])
                    P.add("act", lambda e: e.activation(out=ltmp[:, :n], in_=var[:, :n], func=AF.Ln, bias=epsb[:, 1:2], scale=1.0),
                          reads=["var", "epsb"], writes=["ltmp"])
                    P.add("act", lambda e: e.activation(out=lrstd[:, :n], in_=ltmp[:, :n], func=AF.Exp, scale=-0.5), reads=["ltmp"], writes=["lrstd"])
                    for c in range(8):
                        u = c % 2
                        P.add("dve", lambda e, c=c, u=u: e.tensor_tensor(out=tt[u][:, :n], in0=yf[:, c, :n], in1=mean[:, :n], op=ALU.subtract),
                              reads=["yf%d" % c, "mean"], writes=["tt%d" % u])
                        P.add("dve", lambda e, u=u: e.tensor_tensor(out=tt[u][:, :n], in0=tt[u][:, :n], in1=lrstd[:, :n], op=ALU.mult),
                              reads=["lrstd"], writes=["tt%d" % u])
                        P.add("act", lambda e, c=c, u=u: e.activation(out=b8[:, c, :n], in_=tt[u][:, :n], func=AF.Silu,
                                                                      bias=vcol(VC_LNB, c), scale=vcol(VC_LNG, c)),
                              reads=["tt%d" % u, "vec"], writes=["b8_%d" % c])
                    for oc in range(8):
                        bank = 4 + (oc % 2)

                        def mm2(e, oc=oc, bank=bank):
                            ins = None
                            for c in range(8):
                                ins = e.matmul(psum[:, bank, :n], pw2[:, c, oc * 128:(oc + 1) * 128], b8[:, c, :n], start=(c == 0), stop=(c == 7))
                            return ins
                        P.add("pe", mm2, reads=["b8_%d" % c for c in range(8)] + ["pw2l"], writes=["ps%d" % bank])
                        P.add("dve", lambda e, oc=oc, bank=bank: e.scalar_tensor_tensor(out=hr[s_][:, oc, :n], in0=psum[:, bank, :n], scalar=vcol(VC_PW2B, oc),
                                                                                       in1=hr[s_][:, oc, :n], op0=ALU.add, op1=ALU.add),
                              reads=["ps%d" % bank, "vec"], writes=["h4_%d_%d" % (s_, oc), "h4_%d" % s_])
                        P.add("act", lambda e, oc=oc: e.activation(out=sq[:, oc, :n], in_=hr[s_][:, oc, :n], func=AF.Square),
                              reads=["h4_%d_%d" % (s_, oc)], writes=["sq5_%d" % oc])
                    emit_rstd("5", lambda c: sq[:, c, :n], 8, n, 6, 0, 0, rstd[:, :n], rtmp[:, :n], ["sq5_%d" % c for c in range(8)], "rstd5")
                    for c in range(8):
                        P.add("dve", lambda e, c=c: e.scalar_tensor_tensor(out=hno[s_][:, c, :n], in0=hr[s_][:, c, :n], scalar=gcol(3, c), in1=rstd[:, :n],
                                                                           op0=ALU.mult, op1=ALU.mult),
                              reads=["h4_%d_%d" % (s_, c), "rstd5", "vec"], writes=["hno5"])
                    hres = ["h4_%d" % s_] + ["h4_%d_%d" % (s_, c) for c in range(8)]
                    P.add("sp", lambda e: e.dma_start(out=dr["hB"].rearrange("(c p) t -> p c t", p=128)[:, :, off:off + n], in_=hr[s_][:, :, :n]),
                          reads=hres, dma="h4_%d" % s_)
                    P.add("sp", lambda e: e.dma_start(out=dr["hn"].rearrange("(c p) t -> p c t", p=128)[:, :, off:off + n], in_=hno[s_][:, :, :n]),
                          reads=["hno5"], dma="hno5")

                load(0)
                for i in range(len(tiles)):
                    if i + 1 < len(tiles):
                        load(i + 1)
                    body(i)
                P.barrier()

        g1 = [p for p in ("p1", "p2", "p3", "m0a", "m0b", "p4a") if p in phases]
        if g1:
            with ExitStack() as wst:
                wg[0] = sb(wst, "wgA0", [128, 16384], BF16)
                wg[1] = sb(wst, "wgA1", [128, 16384], BF16)
                if "p1" in phases:
                    phase1()
                if "p2" in phases:
                    phase2()
                if "p3" in phases:
                    phase3()
                if "m0a" in phases:
                    mlp_half("m0a", 0, 0, "hA", "hB", False)
                if "m0b" in phases:
                    mlp_half("m0b", 0, 1, "hB", "hA", False)
                if "p4a" in phases:
                    phase4a()
        if "p4b" in phases:
            phase4b()
        if "m1a" in phases or "m1b" in phases:
            with ExitStack() as wst:
                wg[0] = sb(wst, "wgB0", [128, 16384], BF16)
                wg[1] = sb(wst, "wgB1", [128, 16384], BF16)
                if "m1a" in phases:
                    mlp_half("m1a", 1, 0, "hB", "hA", False)
                if "m1b" in phases:
                    mlp_half("m1b", 1, 1, "hA", None, True)

        P.barrier()
        P.emit()
    return nc


def _rope_tables(pos):
    inv_freq = (np.float32(500000.0) ** (-np.arange(0, 16, 2, dtype=np.float32) / np.float32(16))).astype(np.float32)
    ang = pos[:, None].astype(np.float32) * inv_freq[None, :]
    cos = np.cos(ang).astype(np.float32).T
    sin = np.sin(ang).astype(np.float32).T
    C = np.ones((128, pos.shape[0]), np.float32)
    S = np.zeros((128, pos.shape[0]), np.float32)
    for m in range(2):
        b = m * 64
        C[b:b + 8] = cos
        C[b + 8:b + 16] = cos
        S[b:b + 8] = -sin
        S[b + 8:b + 16] = sin
    return C, S


def _const_mats():
    ident = np.eye(128, dtype=np.float32)
    perm = np.eye(128, dtype=np.float32)
    for m in range(2):
        b = m * 64
        for d in range(8):
            perm[b + d, b + d] = 0
            perm[b + d + 8, b + d + 8] = 0
            perm[b + d, b + d + 8] = 1
            perm[b + d + 8, b + d] = 1
    tri = (np.arange(128)[None, :] >= np.arange(128)[:, None]).astype(np.float32)
    ones = np.ones((128, 128), np.float32)
    return np.concatenate([ident, perm, tri, tri, ones], axis=1)


def prepare_inputs(inp):
    x = np.asarray(inp["x"], np.float32)
    f = lambda k: np.asarray(inp[k], np.float32)

    def chunked(v):
        return np.ascontiguousarray(v.reshape(-1, 128).T)

    vec = np.zeros((128, VC_N), np.float32)
    gains = [f("mix_norm")[0], f("mlp_norm")[0], f("mix_norm")[1], f("mlp_norm")[1], f("final_norm")]
    for i, g in enumerate(gains):
        vec[:, VC_GAIN + 8 * i:VC_GAIN + 8 * i + 8] = chunked(g)
    vec[:, VC_PSCALE:VC_PSCALE + 4] = chunked(f("pool_scale")[0])
    vec[:, VC_PW1B:VC_PW1B + 16] = chunked(f("conv_pw1_b")[0])
    vec[:, VC_DWB:VC_DWB + 8] = chunked(f("conv_dw_b")[0])
    vec[:, VC_LNG:VC_LNG + 8] = chunked(f("conv_ln_g")[0])
    vec[:, VC_LNB:VC_LNB + 8] = chunked(f("conv_ln_b")[0])
    vec[:, VC_PW2B:VC_PW2B + 8] = chunked(f("conv_pw2_b")[0])
    vec[:, VC_SUBLN] = f("subln")[0]
    dww = f("conv_dw_w")[0]
    for t in range(CONV_K):
        vec[:, VC_DWW + t * 8:VC_DWW + t * 8 + 8] = chunked(dww[t])
    lamv = np.concatenate([f("lam_q1")[0], f("lam_k1")[0], f("lam_q2")[0], f("lam_k2")[0]])[None, :]
    lamv = np.ascontiguousarray(np.broadcast_to(lamv, (128, 256)))
    cmat = _const_mats()
    wst = np.zeros((128, 32, 8), np.float32)
    for jj in range(4):
        for tg in range(8):
            k = 4 * tg + jj
            if k < CONV_K:
                wst[32 * jj:32 * jj + 32, :, tg] = dww[k].reshape(32, 32).T
    i4 = np.concatenate([np.eye(32, dtype=np.float32)] * 4, axis=0)
    selm = np.zeros((64, 2, 128), np.float32)
    selm[0:32, 0, :] = 1.0 / 32
    selm[32:64, 1, :] = 1.0 / 32
    shared = {
        "vec": vec, "lamv": lamv, "cmat": cmat, "selm": np.ascontiguousarray(selm.reshape(64, 256)),
        "wst": np.ascontiguousarray(wst.reshape(128, 256)), "i4": i4,
        "w_in": np.ascontiguousarray(f("w_in")[0]), "pool_w": np.ascontiguousarray(f("pool_w")[0]),
        "w_out": np.ascontiguousarray(f("w_out")[0]), "w_up": f("w_up"), "w_down": f("w_down"),
        "pw1": np.ascontiguousarray(f("conv_pw1_w")[0]), "pw2": np.ascontiguousarray(f("conv_pw2_w")[0]),
    }
    in_maps = []
    for c in range(NCORES):
        b, half = divmod(c, 2)
        xb = x[b]
        if half == 0:
            xl = np.zeros((NLOC, D), np.float32)
            xl[HALO:] = xb[:HALF]
            xp = np.zeros((NPRE, D), np.float32)
            pos = np.arange(NKEY, dtype=np.float32) - np.float32(HALF)
            kb = np.zeros((128, NKEY // 128), np.float32)
            kb[:, :NPREB + 1] = NEG
            hv = 0.0
            pfix = np.zeros((128, 4, 16), np.float32)
            for g, w in enumerate(POOL_WINDOWS):
                t = np.arange(16)
                pfix[:, g, :] = (1.0 / np.minimum(t + 1, w) - 1.0 / w).astype(np.float32)[None, :]
        else:
            xl = xb[HALF - HALO:]
            xp = xb[:NPRE]
            pos = np.arange(NKEY, dtype=np.float32)
            kb = np.zeros((128, NKEY // 128), np.float32)
            hv = 1.0
            pfix = np.zeros((128, 4, 16), np.float32)
        C, S = _rope_tables(pos)
        kvalid = np.full((128, (NPREB + 1) * 32), hv, np.float32)
        flags = np.zeros((128, 4), np.float32)
        flags[:, 0] = hv
        m = dict(shared)
        m.update({
            "xT_loc": np.ascontiguousarray(xl.T), "xT_pre": np.ascontiguousarray(xp.T),
            "ropeC": C, "ropeS": S, "kbias": kb, "flags": flags, "kvalid": kvalid,
            "pfix": np.ascontiguousarray(pfix.reshape(128, 64)),
        })
        in_maps.append(m)
    return in_maps


def kernel(**inputs):
    in_maps = prepare_inputs(inputs)
    nc = build_program()
    res = run_bass_kernel_spmd(nc, in_maps, core_ids=list(range(NCORES)))
    out = np.empty((4, SEQ, D), np.float32)
    for c in range(NCORES):
        b, half = divmod(c, 2)
        out[b, half * HALF:(half + 1) * HALF, :] = res.results[c]["yT"].T
    return out
```
